# Optimizing a Trainium2 kernel written in Bass

```python
import math
import jax
import jax.numpy as jnp
from jax import lax
import numpy as np

D_MODEL = 1024
BATCH = 1
SEQ = 16384
DEPTH = 4
DEC_BATCH = 32
DEC_SEQ = 32
PAST_LEN = 2048

CHUNK = 64
N_META = 16
N_EVEN = (DEPTH + 1) // 2
N_ODD = DEPTH // 2
EPS = 1e-6
MIX_HALF = D_MODEL // 2

A_HEADS = 4
A_DH = MIX_HALF // (2 * A_HEADS)
A_DV = 2 * A_DH
A_WIDTH = A_HEADS * A_DV
A_QK = A_HEADS * 2 * A_DH
N_BUCKETS = 32
MAX_DIST = 128
Q_BLOCK = 128
B_HEADS = 4
B_DK = MIX_HALF // B_HEADS
B_DV = B_DK
B_WIDTH = B_HEADS * B_DV
B_CONV = 4
C_GROUP = 16
C_WIDTH = MIX_HALF
C_GROUPS = C_WIDTH // C_GROUP
C_STATE = 64
D_HEAD = 64
D_WIDTH = MIX_HALF
D_HEADS = D_WIDTH // D_HEAD
D_LORA_W = 64
D_LORA_A = 64
D_LORA_G = 128
D_COLS = 3 * D_WIDTH + D_LORA_W + D_LORA_A + D_LORA_G
D_GN_EPS = 64e-5
FFN_HIDDEN = -(-8 * D_MODEL // (3 * 256)) * 256

EV_SIZES = (A_QK, A_QK, A_WIDTH, 2 * B_WIDTH, B_WIDTH, B_WIDTH, B_HEADS, B_HEADS)
EV_COLS = sum(EV_SIZES)
EV_SPLITS = tuple(int(s) for s in np.cumsum(EV_SIZES)[:-1])
D_SIZES = (D_WIDTH, D_WIDTH, D_WIDTH, D_LORA_W, D_LORA_A, D_LORA_G)
D_SPLITS = tuple(int(s) for s in np.cumsum(D_SIZES)[:-1])
OD_COLS = C_WIDTH + D_COLS

kernel_name = 'hybrid_chunk_streaming_encoder_step'


def rmsnorm(x, g):
    xf = x.astype(jnp.float32)
    y = xf * lax.rsqrt(jnp.mean(xf * xf, axis=-1, keepdims=True) + EPS)
    return (y * g.astype(jnp.float32)).astype(x.dtype)


def swiglu(h, w1, w3, w2):
    return (jax.nn.silu(h @ w1) * (h @ w3)) @ w2


def t5_bucket(rel):
    nb = N_BUCKETS // 2
    max_exact = nb // 2
    ret = jnp.where(rel > 0, nb, 0)
    n = jnp.abs(rel)
    nf = jnp.maximum(n, 1).astype(jnp.float32)
    large = max_exact + (jnp.log(nf / max_exact) / math.log(MAX_DIST / max_exact) * (nb - max_exact)).astype(jnp.int32)
    large = jnp.minimum(large, nb - 1)
    return ret + jnp.where(n < max_exact, n, large)


def diff_attention(q, k, v, q_pos, q_cid, k_pos, k_cid, rel_bias, lam):
    f = jnp.float32
    bsz, tq = q.shape[0], q.shape[1]
    qb = min(Q_BLOCK, tq)
    nblk = -(-tq // qb)
    padq = nblk * qb - tq
    q = jnp.pad(q, ((0, 0), (0, padq), (0, 0), (0, 0), (0, 0)))
    q_pos = jnp.pad(q_pos, (0, padq))
    q_cid = jnp.pad(q_cid, (0, padq))
    q_blocks = jnp.moveaxis(q.reshape(bsz, nblk, qb, A_HEADS, 2, A_DH), 1, 0)
    kf = k.astype(f) * (A_DH ** -0.5)
    vf = v.astype(f)

    def one_block(args):
        qblk, pblk, cblk = args
        s = jnp.einsum('bqhmd,bkhmd->bhmqk', qblk.astype(f), kf)
        bias = jnp.transpose(rel_bias.astype(f)[t5_bucket(k_pos[None, :] - pblk[:, None])], (2, 0, 1))
        visible = k_cid[None, :] <= cblk[:, None]
        s = jnp.where(visible, s + bias[None, :, None], -jnp.inf)
        prob = jax.nn.softmax(s, axis=-1)
        w = prob[:, :, 0] - lam * prob[:, :, 1]
        return jnp.einsum('bhqk,bkhe->bqhe', w, vf)

    o = lax.map(one_block, (q_blocks, q_pos.reshape(nblk, qb), q_cid.reshape(nblk, qb)))
    o = jnp.moveaxis(o, 0, 1).reshape(bsz, nblk * qb, A_HEADS, A_DV)
    return o[:, :tq]


def mlstm_chunkwise(q, k, v, logi, logf, c0, n0, m0):
    f = jnp.float32
    bsz, t, nh, _ = q.shape
    ln = min(CHUNK, t)
    pad = (-t) % ln
    pt = ((0, 0), (pad, 0), (0, 0), (0, 0))
    q = jnp.pad(q.astype(f), pt)
    k = jnp.pad(k.astype(f), pt)
    v = jnp.pad(v.astype(f), pt)
    logi = jnp.pad(logi.astype(f), pt[:3], constant_values=-jnp.inf)
    logf = jnp.pad(logf.astype(f), pt[:3])
    nc = (t + pad) // ln

    def to_chunks(a):
        a = a.reshape((bsz, nc, ln) + a.shape[2:])
        return jnp.swapaxes(jnp.moveaxis(a, 1, 0), 2, 3)

    tril = jnp.tril(jnp.ones((ln, ln), dtype=bool))

    def step(carry, inp):
        c, n, m = carry
        qc, kc, vc, li, lf = inp
        b = jnp.cumsum(lf, axis=-1)
        inter = b + m[..., None]
        dmat = jnp.where(tril, b[..., :, None] - b[..., None, :] + li[..., None, :], -jnp.inf)
        mt = jnp.maximum(inter, jnp.max(dmat, axis=-1))
        w_inter = jnp.exp(inter - mt)
        s = jnp.einsum('bhtd,bhsd->bhts', qc, kc) * jnp.exp(dmat - mt[..., None])
        num = w_inter[..., None] * jnp.einsum('bhtd,bhde->bhte', qc, c) + jnp.einsum('bhts,bhse->bhte', s, vc)
        den = w_inter * jnp.einsum('bhtd,bhd->bht', qc, n) + jnp.sum(s, axis=-1)
        h = num / jnp.maximum(jnp.abs(den), jnp.exp(-mt))[..., None]
        b_last = b[..., -1]
        g = b_last[..., None] - b + li
        m_new = jnp.maximum(b_last + m, jnp.max(g, axis=-1))
        decay = jnp.exp(b_last + m - m_new)
        wk = jnp.exp(g - m_new[..., None])
        c_new = decay[..., None, None] * c + jnp.einsum('bhs,bhsd,bhse->bhde', wk, kc, vc)
        n_new = decay[..., None] * n + jnp.einsum('bhs,bhsd->bhd', wk, kc)
        return (c_new, n_new, m_new), h

    (c1, n1, m1), h = lax.scan(step, (c0.astype(f), n0.astype(f), m0.astype(f)),
                               (to_chunks(q), to_chunks(k), to_chunks(v), to_chunks(logi), to_chunks(logf)))
    h = jnp.moveaxis(jnp.swapaxes(h, 2, 3), 0, 1).reshape(bsz, t + pad, nh, v.shape[-1])
    return h[:, pad:], c1, n1, m1


def _complex_affine_combine(e1, e2):
    a1r, a1i, b1r, b1i = e1
    a2r, a2i, b2r, b2i = e2
    return (a2r * a1r - a2i * a1i, a2r * a1i + a2i * a1r,
            a2r * b1r - a2i * b1i + b2r, a2r * b1i + a2i * b1r + b2i)


def s5_ssm(u, lam_re, lam_im, log_dt, b_re, b_im, c_re, c_im, d_skip, x0_re, x0_im):
    f = jnp.float32
    bsz, t, _ = u.shape
    uf = u.astype(f)
    ug = uf.reshape(bsz, t, C_GROUPS, C_GROUP)
    lr, li = lam_re.astype(f), lam_im.astype(f)
    dt = jnp.exp(log_dt.astype(f))[:, None]
    mag = jnp.exp(lr * dt)
    ar, ai = mag * jnp.cos(li * dt), mag * jnp.sin(li * dt)
    den = lr * lr + li * li
    cr = ((ar - 1.0) * lr + ai * li) / den
    ci = (ai * lr - (ar - 1.0) * li) / den
    br, bi = b_re.astype(f), b_im.astype(f)
    bbr = cr[..., None] * br - ci[..., None] * bi
    bbi = cr[..., None] * bi + ci[..., None] * br
    bur = jnp.einsum('btgc,gpc->btgp', ug, bbr)
    bui = jnp.einsum('btgc,gpc->btgp', ug, bbi)
    x0r, x0i = x0_re.astype(f), x0_im.astype(f)
    bur = bur.at[:, 0].add(ar * x0r - ai * x0i)
    bui = bui.at[:, 0].add(ar * x0i + ai * x0r)
    a_r = jnp.broadcast_to(ar, bur.shape)
    a_i = jnp.broadcast_to(ai, bui.shape)
    _, _, xr, xi = lax.associative_scan(_complex_affine_combine, (a_r, a_i, bur, bui), axis=1)
    y = jnp.einsum('btgp,gcp->btgc', xr, c_re.astype(f)) - jnp.einsum('btgp,gcp->btgc', xi, c_im.astype(f))
    y = y.reshape(bsz, t, C_WIDTH) + d_skip.astype(f) * uf
    return y, xr[:, -1], xi[:, -1]


def rwkv7_mix(dcols, shift0, s0, mu, w0, w_w2, a0, w_a2, w_g2, k_k, k_a, r_k, ln_g, ln_b):
    f = jnp.float32
    bsz, t, _ = dcols.shape
    prev = jnp.concatenate([shift0.astype(dcols.dtype), dcols[:, :-1]], axis=1)
    xm = dcols + mu * (prev - dcols)
    r, k, v, wlo, alo, glo = jnp.split(xm, D_SPLITS, axis=-1)
    w_raw = (w0 + jnp.tanh(wlo) @ w_w2).astype(f)
    decay = jnp.exp(-jnp.exp(-jax.nn.softplus(-w_raw) - 0.5))
    a = jax.nn.sigmoid((a0 + alo @ w_a2).astype(f))
    g = (jax.nn.sigmoid(glo) @ w_g2).astype(f)

    def heads(z):
        return z.astype(f).reshape(bsz, t, D_HEADS, D_HEAD)

    r, k, v, a, decay = heads(r), heads(k), heads(v), heads(a), heads(decay)
    kk = k * k_k.astype(f).reshape(D_HEADS, D_HEAD)
    kk = kk / jnp.maximum(jnp.sqrt(jnp.sum(kk * kk, axis=-1, keepdims=True)), 1e-12)
    k = k * (1.0 + (a - 1.0) * k_a.astype(f).reshape(D_HEADS, D_HEAD))

    def step(s, inp):
        rt, wt, kt, vt, kkt, at = inp
        sa = jnp.einsum('bhvk,bhk->bhv', s, -kkt)
        s = s * wt[:, :, None, :] + sa[..., None] * (kkt * at)[:, :, None, :] + vt[..., None] * kt[:, :, None, :]
        return s, jnp.einsum('bhvk,bhk->bhv', s, rt)

    def tm(z):
        return jnp.moveaxis(z, 1, 0)

    s1, y = lax.scan(step, s0.astype(f), (tm(r), tm(decay), tm(k), tm(v), tm(kk), tm(a)))
    y = jnp.moveaxis(y, 0, 1)
    mean = jnp.mean(y, axis=-1, keepdims=True)
    var = jnp.mean(jnp.square(y - mean), axis=-1, keepdims=True)
    y = (y - mean) * lax.rsqrt(var + D_GN_EPS) * ln_g.astype(f).reshape(D_HEADS, D_HEAD) + ln_b.astype(f).reshape(D_HEADS, D_HEAD)
    y = y + jnp.sum(r * k * r_k.astype(f).reshape(D_HEADS, D_HEAD), axis=-1, keepdims=True) * v
    y = y.reshape(bsz, t, D_WIDTH) * g
    return y.astype(dcols.dtype), s1, dcols[:, -1:]


def even_layer(h, e, layer, q_pos, q_cid, k_pos, k_cid, past_k, past_v, bc0, bn0, bm0, bconv0, p):
    f = jnp.float32
    bsz, t, _ = h.shape
    proj = h @ p['ev_w_in'][e]
    aq, ak, av, bqk, bv, bo, bi, bfg = jnp.split(proj, EV_SPLITS, axis=-1)
    new_k = ak.reshape(bsz, t, A_HEADS, 2 * A_DH)
    new_v = av.reshape(bsz, t, A_HEADS, A_DV)
    if past_k is None:
        keys, vals = new_k, new_v
    else:
        keys = jnp.concatenate([past_k.astype(h.dtype), new_k], axis=1)
        vals = jnp.concatenate([past_v.astype(h.dtype), new_v], axis=1)
    lam_init = 0.8 - 0.6 * math.exp(-0.3 * layer)
    lam = (jnp.exp(jnp.sum(p['a_lq1'][e].astype(f) * p['a_lk1'][e].astype(f)))
           - jnp.exp(jnp.sum(p['a_lq2'][e].astype(f) * p['a_lk2'][e].astype(f))) + lam_init)
    oa = diff_attention(aq.reshape(bsz, t, A_HEADS, 2, A_DH), keys.reshape(bsz, -1, A_HEADS, 2, A_DH), vals,
                        q_pos, q_cid, k_pos, k_cid, p['rel_bias'], lam)
    oa = (rmsnorm(oa, p['a_subln'][e]) * (1.0 - lam_init)).reshape(bsz, t, A_WIDTH).astype(h.dtype)
    xc = jnp.concatenate([bconv0.astype(h.dtype), bqk], axis=1)
    cw = p['b_conv_w'][e]
    conv = p['b_conv_b'][e] + cw[0] * xc[:, 0:t]
    for j in range(1, B_CONV):
        conv = conv + cw[j] * xc[:, j:j + t]
    conv = jax.nn.silu(conv)
    bq = conv[..., :B_WIDTH].reshape(bsz, t, B_HEADS, B_DK) * (B_DK ** -0.5)
    bk = conv[..., B_WIDTH:].reshape(bsz, t, B_HEADS, B_DK)
    logi = (bi + p['b_ig_bias'][e]).astype(f)
    logf = jax.nn.log_sigmoid((bfg + p['b_fg_bias'][e]).astype(f))
    hb, bc, bn, bm = mlstm_chunkwise(bq, bk, bv.reshape(bsz, t, B_HEADS, B_DV), logi, logf, bc0, bn0, bm0)
    hb = rmsnorm(hb, p['b_norm'][e].reshape(B_HEADS, B_DV)) * jax.nn.sigmoid(bo.astype(f)).reshape(bsz, t, B_HEADS, B_DV)
    hb = hb.reshape(bsz, t, B_WIDTH).astype(h.dtype)
    out = jnp.concatenate([oa, hb], axis=-1) @ p['ev_w_out'][e]
    return out, new_k, new_v, bc, bn, bm, xc[:, -(B_CONV - 1):]


def odd_layer(h, o, cre0, cim0, ds0, dsh0, p):
    proj = h @ p['od_w_in'][o]
    cu, dcols = proj[..., :C_WIDTH], proj[..., C_WIDTH:]
    y, cre, cim = s5_ssm(cu, p['c_lam_re'][o], p['c_lam_im'][o], p['c_log_dt'][o], p['c_b_re'][o], p['c_b_im'][o],
                         p['c_c_re'][o], p['c_c_im'][o], p['c_d'][o], cre0, cim0)
    yg = jax.nn.gelu(y)
    oc = (yg * jax.nn.sigmoid(yg @ p['c_w_glu'][o].astype(jnp.float32))).astype(h.dtype)
    od, ds, dsh = rwkv7_mix(dcols, dsh0, ds0, p['d_mu'][o], p['d_w0'][o], p['d_w_w2'][o], p['d_a0'][o], p['d_w_a2'][o],
                            p['d_w_g2'][o], p['d_k_k'][o], p['d_k_a'][o], p['d_r_k'][o], p['d_ln_g'][o], p['d_ln_b'][o])
    out = jnp.concatenate([oc, od], axis=-1) @ p['od_w_out'][o]
    return out, cre, cim, ds, dsh


def trunk(x, q_pos, q_cid, k_pos, k_cid, past_k, past_v, st, p):
    dt = x.dtype
    names_even = ('a_k', 'a_v', 'b_c', 'b_n', 'b_m', 'b_conv')
    names_odd = ('c_re', 'c_im', 'd_s', 'd_shift')
    new = {name: [] for name in names_even + names_odd}
    for layer in range(DEPTH):
        h = rmsnorm(x, p['norm_mix'][layer])
        if layer % 2 == 0:
            e = layer // 2
            pk = None if past_k is None else past_k[e]
            pv = None if past_v is None else past_v[e]
            mix, *vals = even_layer(h, e, layer, q_pos, q_cid, k_pos, k_cid, pk, pv,
                                    st['b_c'][e], st['b_n'][e], st['b_m'][e], st['b_conv'][e], p)
            for name, val in zip(names_even, vals):
                new[name].append(val.astype(dt))
        else:
            o = layer // 2
            mix, *vals = odd_layer(h, o, st['c_re'][o], st['c_im'][o], st['d_s'][o], st['d_shift'][o], p)
            for name, val in zip(names_odd, vals):
                new[name].append(val.astype(dt))
        x = x + mix
        x = x + swiglu(rmsnorm(x, p['norm_ffn'][layer]), p['ffn_w1'][layer], p['ffn_w3'][layer], p['ffn_w2'][layer])
    y = rmsnorm(x, p['norm_final'])
    s = {name: jnp.stack(vals, axis=0) for name, vals in new.items()}
    return (y, s['a_k'], s['a_v'], s['b_c'], s['b_n'], s['b_m'], s['b_conv'], s['c_re'], s['c_im'], s['d_s'], s['d_shift'])


def setup_inputs(seed: int = 0) -> dict:
    key = jax.random.key(seed)
    keys = jax.random.split(key, 64)
    ctr = [0]
    f = jnp.float32

    def nk():
        ctr[0] += 1
        return keys[ctr[0] - 1]

    def nrm(shape, scale=1.0):
        return scale * jax.random.normal(nk(), shape, f)

    def gain(shape):
        return 1.0 + nrm(shape, 0.02)

    d = {}
    d['x_prompt'] = nrm((BATCH, SEQ, D_MODEL))
    d['x_sample'] = nrm((DEC_BATCH, DEC_SEQ, D_MODEL))
    d['cache_a_k'] = nrm((N_EVEN, DEC_BATCH, N_META + PAST_LEN, A_HEADS, 2 * A_DH))
    d['cache_a_v'] = nrm((N_EVEN, DEC_BATCH, N_META + PAST_LEN, A_HEADS, A_DV))
    d['state_b_c'] = nrm((N_EVEN, DEC_BATCH, B_HEADS, B_DK, B_DV), 0.1)
    d['state_b_n'] = nrm((N_EVEN, DEC_BATCH, B_HEADS, B_DK), 0.1)
    d['state_b_m'] = nrm((N_EVEN, DEC_BATCH, B_HEADS))
    d['state_b_conv'] = nrm((N_EVEN, DEC_BATCH, B_CONV - 1, 2 * B_WIDTH))
    d['state_c_re'] = nrm((N_ODD, DEC_BATCH, C_GROUPS, C_STATE), 0.1)
    d['state_c_im'] = nrm((N_ODD, DEC_BATCH, C_GROUPS, C_STATE), 0.1)
    d['state_d_s'] = nrm((N_ODD, DEC_BATCH, D_HEADS, D_HEAD, D_HEAD), 0.1)
    d['state_d_shift'] = nrm((N_ODD, DEC_BATCH, 1, D_COLS))
    d['meta_tokens'] = nrm((N_META, D_MODEL))
    d['rel_bias'] = nrm((N_BUCKETS, A_HEADS), 0.5)
    d['norm_mix'] = gain((DEPTH, D_MODEL))
    d['norm_ffn'] = gain((DEPTH, D_MODEL))
    d['norm_final'] = gain((D_MODEL,))
    d['ev_w_in'] = nrm((N_EVEN, D_MODEL, EV_COLS), D_MODEL ** -0.5)
    d['ev_w_out'] = nrm((N_EVEN, A_WIDTH + B_WIDTH, D_MODEL), (A_WIDTH + B_WIDTH) ** -0.5)
    d['a_lq1'] = nrm((N_EVEN, A_DH), 0.1)
    d['a_lk1'] = nrm((N_EVEN, A_DH), 0.1)
    d['a_lq2'] = nrm((N_EVEN, A_DH), 0.1)
    d['a_lk2'] = nrm((N_EVEN, A_DH), 0.1)
    d['a_subln'] = gain((N_EVEN, A_DV))
    d['b_conv_w'] = nrm((N_EVEN, B_CONV, 2 * B_WIDTH), B_CONV ** -0.5)
    d['b_conv_b'] = nrm((N_EVEN, 2 * B_WIDTH), 0.01)
    d['b_ig_bias'] = nrm((N_EVEN, B_HEADS), 0.1)
    d['b_fg_bias'] = jnp.linspace(3.0, 6.0, B_HEADS, dtype=f)[None] + nrm((N_EVEN, B_HEADS), 0.1)
    d['b_norm'] = gain((N_EVEN, B_WIDTH))
    d['od_w_in'] = nrm((N_ODD, D_MODEL, OD_COLS), D_MODEL ** -0.5)
    d['od_w_out'] = nrm((N_ODD, C_WIDTH + D_WIDTH, D_MODEL), (C_WIDTH + D_WIDTH) ** -0.5)
    d['c_lam_re'] = -0.5 + nrm((N_ODD, C_GROUPS, C_STATE), 0.01)
    d['c_lam_im'] = math.pi * jnp.arange(C_STATE, dtype=f)[None, None] + nrm((N_ODD, C_GROUPS, C_STATE), 0.01)
    d['c_log_dt'] = jax.random.uniform(nk(), (N_ODD, C_GROUPS), f, math.log(1e-3), math.log(1e-1))
    d['c_b_re'] = nrm((N_ODD, C_GROUPS, C_STATE, C_GROUP), (2 * C_GROUP) ** -0.5)
    d['c_b_im'] = nrm((N_ODD, C_GROUPS, C_STATE, C_GROUP), (2 * C_GROUP) ** -0.5)
    d['c_c_re'] = nrm((N_ODD, C_GROUPS, C_GROUP, C_STATE), (2 * C_STATE) ** -0.5)
    d['c_c_im'] = nrm((N_ODD, C_GROUPS, C_GROUP, C_STATE), (2 * C_STATE) ** -0.5)
    d['c_d'] = nrm((N_ODD, C_WIDTH))
    d['c_w_glu'] = nrm((N_ODD, C_WIDTH, C_WIDTH), C_WIDTH ** -0.5)
    d['d_mu'] = jax.random.uniform(nk(), (N_ODD, D_COLS), f, 0.0, 1.0)
    d['d_w0'] = jnp.tile(jnp.linspace(-6.0, -1.0, D_HEAD, dtype=f), D_HEADS)[None] + nrm((N_ODD, D_WIDTH), 0.1)
    d['d_w_w2'] = nrm((N_ODD, D_LORA_W, D_WIDTH), 0.1)
    d['d_a0'] = nrm((N_ODD, D_WIDTH), 0.1)
    d['d_w_a2'] = nrm((N_ODD, D_LORA_A, D_WIDTH), 0.1)
    d['d_w_g2'] = nrm((N_ODD, D_LORA_G, D_WIDTH), D_LORA_G ** -0.5)
    d['d_k_k'] = 0.85 + nrm((N_ODD, D_WIDTH), 0.05)
    d['d_k_a'] = 1.0 + nrm((N_ODD, D_WIDTH), 0.05)
    d['d_r_k'] = nrm((N_ODD, D_WIDTH), 0.1)
    d['d_ln_g'] = gain((N_ODD, D_WIDTH))
    d['d_ln_b'] = nrm((N_ODD, D_WIDTH), 0.01)
    d['ffn_w1'] = nrm((DEPTH, D_MODEL, FFN_HIDDEN), D_MODEL ** -0.5)
    d['ffn_w3'] = nrm((DEPTH, D_MODEL, FFN_HIDDEN), D_MODEL ** -0.5)
    d['ffn_w2'] = nrm((DEPTH, FFN_HIDDEN, D_MODEL), FFN_HIDDEN ** -0.5)
    return d


def reference(x_prompt, x_sample, cache_a_k, cache_a_v, state_b_c, state_b_n, state_b_m, state_b_conv,
              state_c_re, state_c_im, state_d_s, state_d_shift, meta_tokens, rel_bias, norm_mix, norm_ffn,
              norm_final, ev_w_in, ev_w_out, a_lq1, a_lk1, a_lq2, a_lk2, a_subln, b_conv_w, b_conv_b,
              b_ig_bias, b_fg_bias, b_norm, od_w_in, od_w_out, c_lam_re, c_lam_im, c_log_dt, c_b_re, c_b_im,
              c_c_re, c_c_im, c_d, c_w_glu, d_mu, d_w0, d_w_w2, d_a0, d_w_a2, d_w_g2, d_k_k, d_k_a, d_r_k,
              d_ln_g, d_ln_b, ffn_w1, ffn_w3, ffn_w2):
    p = dict(rel_bias=rel_bias, norm_mix=norm_mix, norm_ffn=norm_ffn, norm_final=norm_final,
             ev_w_in=ev_w_in, ev_w_out=ev_w_out, a_lq1=a_lq1, a_lk1=a_lk1, a_lq2=a_lq2, a_lk2=a_lk2,
             a_subln=a_subln, b_conv_w=b_conv_w, b_conv_b=b_conv_b, b_ig_bias=b_ig_bias, b_fg_bias=b_fg_bias,
             b_norm=b_norm, od_w_in=od_w_in, od_w_out=od_w_out, c_lam_re=c_lam_re, c_lam_im=c_lam_im,
             c_log_dt=c_log_dt, c_b_re=c_b_re, c_b_im=c_b_im, c_c_re=c_c_re, c_c_im=c_c_im, c_d=c_d,
             c_w_glu=c_w_glu, d_mu=d_mu, d_w0=d_w0, d_w_w2=d_w_w2, d_a0=d_a0, d_w_a2=d_w_a2, d_w_g2=d_w_g2,
             d_k_k=d_k_k, d_k_a=d_k_a, d_r_k=d_r_k, d_ln_g=d_ln_g, d_ln_b=d_ln_b,
             ffn_w1=ffn_w1, ffn_w3=ffn_w3, ffn_w2=ffn_w2)
    f = jnp.float32
    i32 = jnp.int32
    bp, sp = x_prompt.shape[0], x_prompt.shape[1]
    meta = jnp.broadcast_to(meta_tokens.astype(x_prompt.dtype)[None], (bp, N_META, D_MODEL))
    xp = jnp.concatenate([meta, x_prompt], axis=1)
    pos_p = jnp.arange(N_META + sp, dtype=i32)
    cid_p = jnp.concatenate([jnp.zeros((N_META,), i32), 1 + jnp.arange(sp, dtype=i32) // CHUNK])
    st_p = dict(b_c=jnp.zeros((N_EVEN, bp, B_HEADS, B_DK, B_DV), f), b_n=jnp.zeros((N_EVEN, bp, B_HEADS, B_DK), f),
                b_m=jnp.zeros((N_EVEN, bp, B_HEADS), f), b_conv=jnp.zeros((N_EVEN, bp, B_CONV - 1, 2 * B_WIDTH), xp.dtype),
                c_re=jnp.zeros((N_ODD, bp, C_GROUPS, C_STATE), f), c_im=jnp.zeros((N_ODD, bp, C_GROUPS, C_STATE), f),
                d_s=jnp.zeros((N_ODD, bp, D_HEADS, D_HEAD, D_HEAD), f), d_shift=jnp.zeros((N_ODD, bp, 1, D_COLS), xp.dtype))
    y_p, ak_p, av_p, bc_p, bn_p, bm_p, bconv_p, cre_p, cim_p, ds_p, dsh_p = trunk(
        xp, pos_p, cid_p, pos_p, cid_p, None, None, st_p, p)
    ts = x_sample.shape[1]
    past_len = cache_a_k.shape[2] - N_META
    new_idx = past_len + jnp.arange(ts, dtype=i32)
    q_pos_s = N_META + new_idx
    q_cid_s = 1 + new_idx // CHUNK
    past_pos = jnp.arange(N_META + past_len, dtype=i32)
    past_cid = jnp.concatenate([jnp.zeros((N_META,), i32), 1 + jnp.arange(past_len, dtype=i32) // CHUNK])
    k_pos_s = jnp.concatenate([past_pos, q_pos_s])
    k_cid_s = jnp.concatenate([past_cid, q_cid_s])
    st_s = dict(b_c=state_b_c, b_n=state_b_n, b_m=state_b_m, b_conv=state_b_conv, c_re=state_c_re,
                c_im=state_c_im, d_s=state_d_s, d_shift=state_d_shift)
    y_s, ak_s, av_s, bc_s, bn_s, bm_s, bconv_s, cre_s, cim_s, ds_s, dsh_s = trunk(
        x_sample, q_pos_s, q_cid_s, k_pos_s, k_cid_s, cache_a_k, cache_a_v, st_s, p)
    return (y_p[:, N_META:], y_s,
            ak_p, av_p, bc_p, bn_p, bm_p, bconv_p, cre_p, cim_p, ds_p, dsh_p,
            ak_s, av_s, bc_s, bn_s, bm_s, bconv_s, cre_s, cim_s, ds_s, dsh_s)
```

```python
import contextlib
import math
import numpy as np
import concourse.bass as bass
import concourse.mybir as mybir
from concourse.bass_utils import run_bass_kernel_spmd

F32 = mybir.dt.float32
BF16 = mybir.dt.bfloat16
ALU = mybir.AluOpType
AF = mybir.ActivationFunctionType
AX = mybir.AxisListType

D = 1024
DEPTH = 4
NTOK = 128
EV_COLS = 3592
OD_COLS = 2304
FFN = 2816
EPS = 1e-6
EPOCH = 1000000
NCORES = 8
import os
NTILE = int(os.environ.get('K_NTILE', '129'))
NGRP = (NTILE + 3) // 4
TP = NGRP * 512
NVALID = 128 * (NTILE - 1) + 16
PROMPT_LAYERS = int(os.environ.get('K_PROMPT_LAYERS', '4'))
SKIP_SAMPLE = int(os.environ.get('K_SKIP_SAMPLE', '0'))
P_STAGES = os.environ.get('K_P_STAGES', 'dmaso')
ATT_HEADS = int(os.environ.get('K_ATT_HEADS', '4'))
DBG = int(os.environ.get('K_DBG', '0'))
ATT_MODE = int(os.environ.get('K_ATT_MODE', '4'))
PREP = os.environ.get('K_PREP', 'kv')


class FW:
    def __init__(self, nc, n_dma_sems=6):
        self.nc = nc
        self.es = contextlib.ExitStack()
        self.cur = self.es
        self.eng = {'pe': nc.tensor, 'act': nc.scalar, 'dve': nc.vector, 'pool': nc.gpsimd, 'sp': nc.sync}
        self.cnt = {e: 0 for e in ('pe', 'act', 'dve', 'pool')}
        self.sems = {e: [] for e in self.cnt}
        self.seen = {e: {} for e in self.eng}
        self.res = {}
        self.dma_ring = {}
        self.dma_i = {}
        self.n_dma_sems = n_dma_sems
        self.semobj = {}
        self.n_instr = 0
        for q in ('sp', 'pool'):
            self.dma_ring[q] = [self._newsem(f'dma_{q}_{i}') for i in range(n_dma_sems)]
            self.dma_i[q] = 0

    def _newsem(self, name):
        s = self.es.enter_context(self.nc.semaphore(name))
        self.semobj[name] = s
        return name

    def sb(self, name, shape, dt=F32):
        return self.cur.enter_context(self.nc.sbuf_tensor("s_" + name, list(shape), dt))

    @contextlib.contextmanager
    def scope(self):
        old = self.cur
        self.cur = contextlib.ExitStack()
        try:
            yield
        finally:
            self.barrier()
            self.cur.close()
            self.cur = old

    def barrier(self):
        toks = []
        for e, c in self.cnt.items():
            if c > 0:
                ep = (c - 1) // EPOCH
                toks.append((self.sems[e][ep], c - ep * EPOCH))
        for q, ring in self.dma_ring.items():
            n = self.n_dma_sems
            for j, sname in enumerate(ring):
                issued = (self.dma_i[q] - j + n - 1) // n if self.dma_i[q] > j else 0
                if issued > 0:
                    toks.append((sname, 16 * issued))
        for e in self.eng:
            for t in toks:
                self._wait(e, t)
        self.res = {}

    def ps(self, name, shape, dt=F32):
        return self.es.enter_context(self.nc.psum_tensor("p_" + name, list(shape), dt))

    def _wait(self, e, tok):
        sname, val = tok
        if self.seen[e].get(sname, 0) >= val:
            return
        self.eng[e].wait_ge(self.semobj[sname], val)
        self.seen[e][sname] = val
        self.n_instr += 1

    def _deps(self, e, reads, writes):
        toks = []
        for k in reads:
            r = self.res.get(k)
            if r and r['w'] is not None:
                toks.append(r['w'])
        for k in writes:
            r = self.res.get(k)
            if r:
                if r['w'] is not None:
                    toks.append(r['w'])
                toks.extend(r['r'].items())
        for t in toks:
            self._wait(e, t)

    def _commit(self, tok, reads, writes):
        for k in reads:
            r = self.res.setdefault(k, {'w': None, 'r': {}})
            if r['r'].get(tok[0], 0) < tok[1]:
                r['r'][tok[0]] = tok[1]
        for k in writes:
            self.res[k] = {'w': tok, 'r': {}}

    def op(self, e, fn, reads=(), writes=()):
        self._deps(e, reads, writes)
        ep = self.cnt[e] // EPOCH
        while len(self.sems[e]) <= ep:
            self.sems[e].append(self._newsem(f'c_{e}_{len(self.sems[e])}'))
        sname = self.sems[e][ep]
        ins = fn(self.eng[e])
        ins.then_inc(self.semobj[sname], 1)
        self.cnt[e] += 1
        tok = (sname, self.cnt[e] - ep * EPOCH)
        self._commit(tok, reads, writes)
        self.n_instr += 1
        return tok

    def dma(self, q, out, in_, reads=(), writes=(), **kw):
        self._deps(q, reads, writes)
        i = self.dma_i[q]
        n = self.n_dma_sems
        sname = self.dma_ring[q][i % n]
        rnd = i // n
        if rnd > 0:
            self._wait(q, (sname, 16 * rnd))
        self.eng[q].dma_start(out=out, in_=in_, **kw).then_inc(self.semobj[sname], 16)
        self.dma_i[q] = i + 1
        tok = (sname, 16 * (rnd + 1))
        self._commit(tok, reads, writes)
        self.n_instr += 1
        return tok

    def finish(self, keys, e='sp'):
        for k in keys:
            r = self.res.get(k)
            if r and r['w'] is not None:
                self._wait(e, r['w'])

    def close(self):
        self.es.close()


def build_program():
    nc = bass.Bass("TRN2", target_bir_lowering=False)
    fw = FW(nc)

    def din(name, shape):
        return nc.dram_tensor(name, list(shape), F32, kind="ExternalInput").ap()

    def dout(name, shape):
        return nc.dram_tensor(name, list(shape), F32, kind="ExternalOutput").ap()

    xs = din("xs", [NTOK, D])
    norm_mix = din("norm_mix", [DEPTH, D])
    norm_ffn = din("norm_ffn", [DEPTH, D])
    norm_final = din("norm_final", [1, D])
    ev_w_in = din("ev_w_in", [2, D, EV_COLS])
    ev_w_out = din("ev_w_out", [2, D, D])
    od_w_in = din("od_w_in", [2, D, OD_COLS])
    od_w_out = din("od_w_out", [2, D, D])
    ffn_w1 = din("ffn_w1", [DEPTH, D, FFN])
    ffn_w3 = din("ffn_w3", [DEPTH, D, FFN])
    ffn_w2 = din("ffn_w2", [DEPTH, FFN, D])
    ident_in = din("ident", [128, 128])
    ys = dout("ys", [NTOK, D])
    cak = din("cak", [2, 4, 2064, 512])
    cav = din("cav", [2, 4, 2064, 512])
    sbc = din("sbc", [2, 4, 4, 128, 128])
    sbn = din("sbn", [2, 16, 128])
    sbm = din("sbm", [2, 16])
    sbconv = din("sbconv", [2, 12, 1024])
    relb = din("relb", [1, 128])
    lqk = din("lqk", [2, 4, 64])
    a_subln = din("a_subln", [2, 128])
    convwb = din("convwb", [2, 5, 1024])
    igfg = din("igfg", [2, 8])
    b_norm = din("b_norm", [2, 512])
    bk_in = din("bk", [128, 96])
    selr_in = din("selr", [8, 8 * 128])
    negbd_in = din("negbd", [128, 128])
    bmask_in = din("bmask", [128, 4])
    scanmask_in = din("scanmask", [128, 128])
    blk_in = din("blk", [128, 128])
    id2_in = din("id2", [128, 64])
    s5p = din("s5p", [2, 48, 128])
    cd16 = din("cd16", [2, 16, 32])
    sc0 = din("sc0", [2, 128, 128])
    c_b_re = din("c_b_re", [2, 32, 64, 16]); c_b_im = din("c_b_im", [2, 32, 64, 16])
    c_c_re = din("c_c_re", [2, 32, 16, 64]); c_c_im = din("c_c_im", [2, 32, 16, 64])
    c_w_glu = din("c_w_glu", [2, 512, 512])
    sds = din("sds", [2, 4, 8, 64, 64])
    sdsh = din("sdsh", [2, 4, 1792])
    dvecs = din("dvecs", [2, 42, 128])
    d_w_w2 = din("d_w_w2", [2, 64, 512]); d_w_a2 = din("d_w_a2", [2, 64, 512]); d_w_g2 = din("d_w_g2", [2, 128, 512])
    ncs = dout("ncs", [2, 128, 128])
    nds = dout("nds", [2, 4, 8, 64, 64])
    ndsh = dout("ndsh", [2, 4, 1792])
    nak = dout("nak", [2, NTOK, 512])
    nav = dout("nav", [2, NTOK, 512])
    nbc = dout("nbc", [2, 4, 4, 128, 128])
    nbn = dout("nbn", [2, 16, 128])
    nbm = dout("nbm", [2, 16])
    nbconv = dout("nbconv", [2, 12, 1024])
    dbg = dout("dbg", [128, 8 * NTOK])
    xp0 = din("xp0", [8, 128, TP])
    negc_in = din("negc", [128, 128])
    bkp_in = din("bkp", [128, 384])
    mskp_in = din("mskp", [128, 384])
    yp = dout("yp", [NVALID - 16, D])
    pak = dout("pak", [2, NVALID, 512]); pav = dout("pav", [2, NVALID, 512])
    pbc = dout("pbc", [2, 4, 128, 128]); pbn = dout("pbn", [2, 4, 128]); pbm = dout("pbm", [2, 4]); pbconv = dout("pbconv", [2, 3, 1024])
    pcs = dout("pcs", [2, 32, 128]); pds = dout("pds", [2, 8, 64, 64]); pdsh = dout("pdsh", [2, 1792])
    xD = nc.dram_tensor("xD", [8, 128, TP], F32, kind="Internal").ap()
    projD = nc.dram_tensor("projD", [29, 128, TP], F32, kind="Internal").ap()
    mixD = nc.dram_tensor("mixD", [8, 128, TP], F32, kind="Internal").ap()

    ident = fw.sb("ident", [128, 128])
    ones = fw.sb("ones", [128, 128])
    epsc = fw.sb("epsc", [128, 1])
    xtok = xF = sq = rstd = hF = projF = mixF = gF = None

    def alloc_dense(nt, sfx, sample):
        t = [fw.sb("xtok" + sfx, [128, D]) if sample else None,
             fw.sb("xF" + sfx, [128, 8, nt]), fw.sb("sq" + sfx, [128, 8, nt]), fw.sb("rstd" + sfx, [128, nt]),
             fw.sb("hF" + sfx, [128, 8, nt], BF16),
             fw.sb("projF" + sfx, [128, 29, nt]) if sample else None,
             fw.sb("mixF" + sfx, [128, 8, nt], BF16), fw.sb("gF" + sfx, [128, 22, nt], BF16)]
        return t

    gvec = fw.sb("gvec", [128, 9, 8])
    SLAB = 4096
    slab32 = [fw.sb(f"slab32_{i}", [128, SLAB]) for i in range(2)]
    slab16 = [fw.sb(f"slab16_{i}", [128, SLAB], BF16) for i in range(2)]
    banks = [fw.ps(f"bank{i}", [128, 512]) for i in range(8)]
    BT = fw.sb("BT", [128, 4, 96])
    bk = fw.sb("bk", [128, 96]); rb_bc = fw.sb("rb_bc", [128, 128])
    selr = fw.sb("selr", [8, 8 * 128]); negbd = fw.sb("negbd", [128, 128])
    bmask = fw.sb("bmask", [128, 4]); scanmask = fw.sb("scanmask", [128, 128])
    onec = fw.sb("onec", [128, 1]); bttmp = fw.sb("bttmp", [128, 96])
    blk = fw.sb("blk", [128, 128]); id2 = fw.sb("id2", [128, 64]); halfpi = fw.sb("halfpi", [128, 1])
    gneps = fw.sb("gneps", [128, 1])
    kc = kT = vst = Vaug = PT = ktok = vtok = bvtok = botok = lq_bc = lamc = qb16 = LI = LF = BR = MT = nig = None
    subs_bc = xcF = qkF = cacc = cwb = stg = tmp12 = Caug = m0bc = igfg_bc = bnorm_bc = mrow = WK = CL = None

    def alloc_even(sfx):
        A = lambda n, shp, dt=F32: fw.sb(n + sfx, shp, dt)
        t = dict(kc=A("kc", [128, 17, 128]), kT=A("kT", [128, 2096], BF16), vst=A("vst", [128, 18, 128]),
                 Vaug=A("Vaug", [128, 18, 129], BF16), PT=A("PT", [128, 2, 576], BF16), ktok=A("ktok", [128, 512]),
                 vtok=A("vtok", [128, 512]), bvtok=A("bvtok", [128, 4, 129]), botok=A("botok", [128, 512]),
                 lq_bc=A("lq_bc", [128, 256]), lamc=A("lamc", [128, 4]), qb16=A("qb16", [128, 4, NTOK], BF16),
                 LI=A("LI", [128, 4, NTOK]), LF=A("LF", [128, 4, NTOK]), BR=A("BR", [128, 4, NTOK]), MT=A("MT", [128, 4, NTOK]),
                 nig=A("nig", [128, 8]), subs_bc=A("subs_bc", [128, 128]), xcF=A("xcF", [128, 8, 4, 35]),
                 qkF=A("qkF", [128, 8, NTOK]), cacc=A("cacc", [128, 8, NTOK]), cwb=A("cwb", [128, 8, 5]),
                 stg=A("stg", [16, 1024]), tmp12=A("tmp12", [128, 8, 12]), Caug=A("Caug", [128, 16, 129]),
                 m0bc=A("m0bc", [128, 16]), igfg_bc=A("igfg_bc", [128, 8]), bnorm_bc=A("bnorm_bc", [128, 512]),
                 mrow=A("mrow", [1, 16]), WK=[A(f"wk{i}", [128, 129]) for i in range(14)], CL=A("cl", [128, 24]))
        return t

    state = {'slab': 0, 'bank': 0, 'nt': NTOK}

    fw.dma('sp', ident[:], ident_in, writes=['ident'])
    fw.op('dve', lambda e: e.memset(ones[:], 1.0), writes=['ones'])
    fw.op('dve', lambda e: e.memset(epsc[:], EPS), writes=['epsc'])
    fw.op('dve', lambda e: e.memset(onec[:], 1.0), writes=['onec'])
    fw.dma('sp', bk[:], bk_in, writes=['bk'])
    fw.dma('sp', blk[:], blk_in, writes=['blk'])
    fw.dma('sp', id2[:], id2_in, writes=['id2'])
    fw.op('dve', lambda e: e.memset(halfpi[:], math.pi / 2), writes=['halfpi'])
    fw.op('dve', lambda e: e.memset(gneps[:], 64e-5), writes=['gneps'])
    fw.dma('sp', selr[:], selr_in, writes=['selr'])
    fw.dma('sp', negbd[:], negbd_in, writes=['negbd'])
    fw.dma('sp', bmask[:], bmask_in, writes=['bmask'])
    fw.dma('sp', scanmask[:], scanmask_in, writes=['scanmask'])
    fw.dma('sp', rb_bc[:], relb[0].partition_broadcast(128), writes=['rb_bc'])
    fw.op('dve', lambda e: e.memset(BT[:], 0.0), writes=['BT'])
    for h in range(4):
        for bkt in range(32):
            fw.op('dve', lambda e, h=h, bkt=bkt: e.tensor_scalar(
                out=bttmp[:], in0=bk[:], scalar1=float(bkt), scalar2=rb_bc[:, bkt * 4 + h:bkt * 4 + h + 1],
                op0=ALU.is_equal, op1=ALU.mult), reads=['bk', 'rb_bc'], writes=['bttmp'])
            fw.op('dve', lambda e, h=h: e.tensor_tensor(out=BT[:, h, :], in0=BT[:, h, :], in1=bttmp[:], op=ALU.add),
                  reads=['bttmp', 'BT'], writes=['BT'])
    for i in range(DEPTH):
        fw.dma('sp', gvec[:, i, :], norm_mix[i].rearrange("(k p) -> p k", p=128), writes=['gvec'],
               allow_slow_non_contiguous=True)
        fw.dma('sp', gvec[:, 4 + i, :], norm_ffn[i].rearrange("(k p) -> p k", p=128), writes=['gvec'],
               allow_slow_non_contiguous=True)
    fw.dma('sp', gvec[:, 8, :], norm_final[0].rearrange("(k p) -> p k", p=128), writes=['gvec'],
           allow_slow_non_contiguous=True)

    sc_sample = fw.scope()
    sc_sample.__enter__()
    xtok, xF, sq, rstd, hF, projF, mixF, gF = alloc_dense(NTOK, "_s", True)
    fw.op('dve', lambda e: e.memset(mixF[:], 0.0), writes=['mixF'])
    fw.dma('sp', xtok[:], xs, writes=['xtok'])
    for k in range(8):
        b = banks[k % 2]
        fw.op('pe', lambda e, k=k, b=b: e.transpose(b[:, 0:128], xtok[:, k * 128:(k + 1) * 128], ident[:]),
              reads=['xtok', 'ident'], writes=[f'bank{k % 2}'])
        fw.op('act', lambda e, k=k, b=b: e.copy(out=xF[:, k, :], in_=b[:, 0:128]),
              reads=[f'bank{k % 2}'], writes=['xF'])

    def rmsnorm(which, out_tile, out_key):
        nt = state['nt']
        fw.op('act', lambda e: e.activation(out=sq[:], in_=xF[:], func=AF.Square), reads=['xF'], writes=['sq'])
        b = banks[7]
        for k in range(8):
            fw.op('pe', lambda e, k=k: e.matmul(b[:, 0:nt], lhsT=ones[:], rhs=sq[:, k, :],
                                                start=(k == 0), stop=(k == 7)),
                  reads=['sq', 'ones'], writes=['bank7'])
        fw.op('act', lambda e: e.activation(out=rstd[:], in_=b[:, 0:nt], func=AF.Sqrt, scale=1.0 / D,
                                            bias=epsc[:]), reads=['bank7', 'epsc'], writes=['rstd'])
        fw.op('dve', lambda e: e.reciprocal(out=rstd[:], in_=rstd[:]), reads=['rstd'], writes=['rstd'])
        for k in range(8):
            fw.op('dve', lambda e, k=k: e.scalar_tensor_tensor(
                out=out_tile[:, k, :], in0=xF[:, k, :], scalar=gvec[:, which, k:k + 1], in1=rstd[:],
                op0=ALU.mult, op1=ALU.mult), reads=['xF', 'gvec', 'rstd'], writes=[out_key])

    def std(src, n):
        return [(k * 128, 128, 0, src[:, k, :]) for k in range(n)]

    def linear(chunks, src_key, W, M, consume, tok_groups=None, mtile=128):
        nk = len(chunks)
        nt = state['nt']
        skeys = [src_key] if isinstance(src_key, str) else list(src_key)
        gw = 512 if nk * 512 <= SLAB else (256 if nk * 256 <= SLAB else 128)
        for g0 in range(0, M, gw):
            w = min(gw, M - g0)
            si = state['slab']
            state['slab'] ^= 1
            s32, s16 = slab32[si], slab16[si]
            for ci, (row0, kp, p0, src) in enumerate(chunks):
                fw.dma('sp', s32[p0:p0 + kp, ci * w:(ci + 1) * w], W[row0:row0 + kp, g0:g0 + w],
                       writes=[f's32_{si}_{ci}'])
            fw.op('pool', lambda e, s32=s32, s16=s16, n=nk * w: e.tensor_copy(out=s16[:, 0:n], in_=s32[:, 0:n]),
                  reads=[f's32_{si}_{ci}' for ci in range(nk)], writes=[f's16_{si}'])
            if tok_groups and g0 in tok_groups:
                b = banks[6]
                for ci, (row0, kp, p0, src) in enumerate(chunks):
                    fw.op('pe', lambda e, ci=ci, kp=kp, p0=p0, src=src, b=b, s16=s16, w=w: e.matmul(
                        b[:, 0:w], lhsT=src, rhs=s16[p0:p0 + kp, ci * w:(ci + 1) * w],
                        start=(ci == 0), stop=(ci == nk - 1)), reads=[f's16_{si}'] + skeys, writes=['bank6'])
                tok_groups[g0](b)
            for m0 in range(0, w, mtile):
                msz = min(mtile, w - m0)
                bi = state['bank']
                state['bank'] = (bi + 1) % 4
                b = banks[bi]
                for ci, (row0, kp, p0, src) in enumerate(chunks):
                    fw.op('pe', lambda e, ci=ci, kp=kp, p0=p0, src=src, m0=m0, msz=msz, b=b, s16=s16, w=w: e.matmul(
                        b[0:msz, 0:nt], lhsT=s16[p0:p0 + kp, ci * w + m0:ci * w + m0 + msz], rhs=src,
                        start=(ci == 0), stop=(ci == nk - 1)),
                        reads=[f's16_{si}'] + skeys, writes=[f'bank{bi}'])
                consume((g0 + m0) // mtile, msz, b[0:msz, 0:nt], f'bank{bi}')

    def to_tile(dst, dst_key, eng='act'):
        def c(mt, msz, p, pkey):
            if eng == 'act':
                fw.op('act', lambda e: e.copy(out=dst[0:msz, mt, :], in_=p), reads=[pkey], writes=[dst_key])
            else:
                fw.op('dve', lambda e: e.tensor_copy(out=dst[0:msz, mt, :], in_=p), reads=[pkey], writes=[dst_key])
        return c

    def add_resid(mt, msz, p, pkey):
        fw.op('dve', lambda e: e.tensor_tensor(out=xF[0:msz, mt, :], in0=xF[0:msz, mt, :], in1=p, op=ALU.add),
              reads=[pkey, 'xF'], writes=['xF'])


    LAM_INIT = [0.8 - 0.6 * math.exp(-0.3 * l) for l in range(DEPTH)]

    def Vop(fn, r, w):
        return fw.op('dve', fn, reads=r, writes=w)

    def Aop(fn, r, w):
        return fw.op('act', fn, reads=r, writes=w)

    def Pop(fn, r, w):
        return fw.op('pe', fn, reads=r, writes=w)

    def diag(src_ap, src_key, col_ap, scratch=11):
        Vop(lambda e: e.scalar_tensor_tensor(out=WK[scratch][:, 0:128], in0=src_ap, scalar=1.0, in1=ident[:],
                                             op0=ALU.mult, op1=ALU.mult, accum_out=col_ap),
            [src_key, 'ident'], [f'wk{scratch}', 'cl'])

    def even_tok_groups(e_):
        def kcopy(b):
            Aop(lambda e: e.copy(out=ktok[:], in_=b[:, 0:512]), ['bank6'], ['ktok'])
            fw.dma('sp', nak[e_], ktok[:], reads=['ktok'], writes=['nak'])

        def vcopy(b):
            Aop(lambda e: e.copy(out=vtok[:], in_=b[:, 0:512]), ['bank6'], ['vtok'])
            fw.dma('sp', nav[e_], vtok[:], reads=['vtok'], writes=['nav'])

        def bvcopy(b):
            Aop(lambda e: e.copy(out=bvtok[:, :, 0:128], in_=b[:, 0:512].rearrange("p (h v) -> p h v", v=128)),
                ['bank6'], ['bvtok'])

        def bocopy(b):
            Aop(lambda e: e.copy(out=botok[:], in_=b[:, 0:512]), ['bank6'], ['botok'])
        return {512: kcopy, 1024: vcopy, 2560: bvcopy, 3072: bocopy}

    def attention(e_, layer):
        lam_init = LAM_INIT[layer]
        fw.dma('sp', lq_bc[:], lqk[e_].rearrange("a d -> (a d)").partition_broadcast(128), writes=['lq_bc'])
        fw.dma('sp', subs_bc[:], a_subln[e_].partition_broadcast(128), writes=['subs_bc'])
        Vop(lambda e: e.scalar_tensor_tensor(out=WK[0][:, 0:64], in0=lq_bc[:, 0:64], scalar=1.0, in1=lq_bc[:, 64:128],
                                             op0=ALU.mult, op1=ALU.mult, accum_out=lamc[:, 0:1]), ['lq_bc'], ['wk0', 'lamc'])
        Vop(lambda e: e.scalar_tensor_tensor(out=WK[0][:, 0:64], in0=lq_bc[:, 128:192], scalar=1.0, in1=lq_bc[:, 192:256],
                                             op0=ALU.mult, op1=ALU.mult, accum_out=lamc[:, 1:2]), ['lq_bc', 'wk0'], ['wk0', 'lamc'])
        Aop(lambda e: e.activation(out=lamc[:, 0:2], in_=lamc[:, 0:2], func=AF.Exp), ['lamc'], ['lamc'])
        Vop(lambda e: e.tensor_tensor(out=lamc[:, 2:3], in0=lamc[:, 0:1], in1=lamc[:, 1:2], op=ALU.subtract), ['lamc'], ['lamc'])
        Vop(lambda e: e.tensor_scalar(out=lamc[:, 3:4], in0=lamc[:, 2:3], scalar1=-1.0, scalar2=-lam_init,
                                      op0=ALU.mult, op1=ALU.add), ['lamc'], ['lamc'])
        Vop(lambda e: e.tensor_scalar(out=subs_bc[:], in0=subs_bc[:], scalar1=1.0 - lam_init, scalar2=None, op0=ALU.mult),
            ['subs_bc'], ['subs_bc'])
        Vop(lambda e: e.tensor_copy(out=qb16[:], in_=projF[:, 0:4, :]), ['projF'], ['qb16'])
        for b in range(4):
            seg = slice(b * 32, (b + 1) * 32)
            for h in range(4):
                hs = slice(h * 128, (h + 1) * 128)
                fw.dma('sp', kc[:, 0:16, :], cak[e_, b, 0:2048, hs].rearrange("(blk p) d -> p blk d", p=128),
                       writes=['kc'])
                fw.dma('sp', kc[0:16, 16, :], cak[e_, b, 2048:2064, hs], writes=['kc'])
                fw.dma('sp', vst[:, 0:16, :], cav[e_, b, 0:2048, hs].rearrange("(blk p) d -> p blk d", p=128),
                       writes=['vst'])
                fw.dma('sp', vst[0:16, 16, :], cav[e_, b, 2048:2064, hs], writes=['vst'])
                fw.dma('sp', vst[0:32, 17, :], vtok[seg, hs], reads=['vtok'], writes=['vst'])
                for grp in range(5):
                    bi = 4 + grp % 2
                    bb = banks[bi]
                    for blk in range(grp * 4, min(grp * 4 + 4, 17)):
                        npart = 128 if blk < 16 else 16
                        c0 = (blk % 4) * 128
                        Pop(lambda e, bb=bb, blk=blk, npart=npart, c0=c0: e.transpose(
                            bb[:, c0:c0 + npart], kc[0:npart, blk, :], ident[0:npart, 0:npart]),
                            ['kc', 'ident'], [f'bank{bi}'])
                    if grp < 4:
                        Aop(lambda e, bb=bb, grp=grp: e.copy(out=kT[:, grp * 512:(grp + 1) * 512], in_=bb[:, 0:512]),
                            [f'bank{bi}'], ['kT'])
                    else:
                        Aop(lambda e, bb=bb: e.copy(out=kT[:, 2048:2064], in_=bb[:, 0:16]), [f'bank{bi}'], ['kT'])
                Aop(lambda e, h=h, seg=seg: e.copy(out=kT[:, 2064:2096], in_=projF[:, 4 + h, seg]), ['projF'], ['kT'])
                fw.op('pool', lambda e: e.tensor_copy(out=Vaug[:, :, 0:128], in_=vst[:]), reads=['vst'], writes=['Vaug'])
                for m in range(2):
                    ms = slice(64 * m, 64 * m + 64)
                    for blk in range(15):
                        Pop(lambda e, m=m, ms=ms, blk=blk, h=h, seg=seg: e.matmul(
                            banks[m][:, blk * 32:(blk + 1) * 32], lhsT=kT[ms, blk * 128:(blk + 1) * 128],
                            rhs=qb16[ms, h, seg], start=True, stop=True), ['kT', 'qb16'], [f'bank{m}'])
                    for j, (c0, nk) in enumerate(((1920, 128), (2048, 16), (2064, 32))):
                        Pop(lambda e, m=m, ms=ms, j=j, c0=c0, nk=nk, h=h, seg=seg: e.matmul(
                            banks[2 + m][0:nk, j * 32:(j + 1) * 32], lhsT=kT[ms, c0:c0 + nk],
                            rhs=qb16[ms, h, seg], start=True, stop=True), ['kT', 'qb16'], [f'bank{2 + m}'])
                    Aop(lambda e, m=m, h=h: e.activation(out=PT[:, m, 0:480], in_=banks[m][:, 0:480], func=AF.Exp,
                                                         bias=rb_bc[:, 60 + h:61 + h], scale=0.125),
                        [f'bank{m}', 'rb_bc'], ['PT'])
                    Vop(lambda e, m=m, h=h: e.scalar_tensor_tensor(out=WK[1][:, 0:96], in0=banks[2 + m][:, 0:96], scalar=0.125,
                                                                   in1=BT[:, h, :], op0=ALU.mult, op1=ALU.add),
                        [f'bank{2 + m}', 'BT'], ['wk1'])
                    Aop(lambda e, m=m: e.activation(out=PT[:, m, 480:576], in_=WK[1][:, 0:96], func=AF.Exp),
                        ['wk1'], ['PT'])
                for m in range(2):
                    for blk in range(18):
                        nk = 128 if blk < 16 else (16 if blk == 16 else 32)
                        off = blk * 32 if blk < 15 else 480 + (blk - 15) * 32
                        Pop(lambda e, m=m, blk=blk, nk=nk, off=off: e.matmul(
                            banks[7][0:32, m * 129:(m + 1) * 129], lhsT=PT[0:nk, m, off:off + 32], rhs=Vaug[0:nk, blk, :],
                            start=(blk == 0), stop=(blk == 17)), ['PT', 'Vaug'], ['bank7'])
                o7 = banks[7]
                Vop(lambda e: e.reciprocal(out=CL[0:32, 0:1], in_=o7[0:32, 128:129]), ['bank7'], ['cl'])
                Vop(lambda e: e.reciprocal(out=CL[0:32, 1:2], in_=o7[0:32, 257:258]), ['bank7'], ['cl'])
                Vop(lambda e: e.tensor_tensor(out=CL[0:32, 2:3], in0=CL[0:32, 1:2], in1=lamc[0:32, 3:4], op=ALU.mult),
                    ['cl', 'lamc'], ['cl'])
                Vop(lambda e: e.tensor_scalar(out=WK[2][0:32, 0:128], in0=o7[0:32, 0:128], scalar1=CL[0:32, 0:1], scalar2=None,
                                              op0=ALU.mult), ['bank7', 'cl'], ['wk2'])
                Vop(lambda e: e.scalar_tensor_tensor(out=WK[3][0:32, 0:128], in0=o7[0:32, 129:257], scalar=CL[0:32, 2:3],
                                                     in1=WK[2][0:32, 0:128], op0=ALU.mult, op1=ALU.add),
                    ['bank7', 'cl', 'wk2'], ['wk3'])
                Vop(lambda e: e.scalar_tensor_tensor(out=WK[4][0:32, 0:128], in0=WK[3][0:32, 0:128], scalar=1.0,
                                                     in1=WK[3][0:32, 0:128], op0=ALU.mult, op1=ALU.mult,
                                                     accum_out=CL[0:32, 3:4]), ['wk3'], ['wk4', 'cl'])
                Aop(lambda e: e.activation(out=CL[0:32, 4:5], in_=CL[0:32, 3:4], func=AF.Sqrt, scale=1.0 / 128,
                                           bias=epsc[0:32, :]), ['cl', 'epsc'], ['cl'])
                Vop(lambda e: e.reciprocal(out=CL[0:32, 5:6], in_=CL[0:32, 4:5]), ['cl'], ['cl'])
                Vop(lambda e: e.scalar_tensor_tensor(out=WK[2][0:32, 0:128], in0=WK[3][0:32, 0:128], scalar=CL[0:32, 5:6],
                                                     in1=subs_bc[0:32, :], op0=ALU.mult, op1=ALU.mult),
                    ['wk3', 'cl', 'subs_bc'], ['wk2'])
                Pop(lambda e: e.transpose(banks[5][:, 0:32], WK[2][0:32, 0:128], ident[0:32, 0:32]),
                    ['wk2', 'ident'], ['bank5'])
                Aop(lambda e, h=h, seg=seg: e.copy(out=mixF[:, h, seg], in_=banks[5][:, 0:32]), ['bank5'], ['mixF'])

    def mlstm(e_):
        fw.dma('sp', stg[0:5, :], convwb[e_], writes=['stg'])
        for k in range(8):
            Pop(lambda e, k=k: e.transpose(banks[4][:, k * 8:k * 8 + 5], stg[0:5, k * 128:(k + 1) * 128], ident[0:5, 0:5]),
                ['stg', 'ident'], ['bank4'])
        for k in range(8):
            Aop(lambda e, k=k: e.copy(out=cwb[:, k, :], in_=banks[4][:, k * 8:k * 8 + 5]), ['bank4'], ['cwb'])
        fw.dma('sp', stg[0:12, :], sbconv[e_], reads=[], writes=['stg'])
        for k in range(8):
            Pop(lambda e, k=k: e.transpose(banks[5][:, k * 16:k * 16 + 12], stg[0:12, k * 128:(k + 1) * 128], ident[0:12, 0:12]),
                ['stg', 'ident'], ['bank5'])
        for k in range(8):
            Aop(lambda e, k=k: e.copy(out=xcF[:, k, :, 0:3], in_=banks[5][:, k * 16:k * 16 + 12].rearrange("p (b j) -> p b j", j=3)),
                ['bank5'], ['xcF'])
            Vop(lambda e, k=k: e.tensor_copy(out=xcF[:, k, :, 3:35], in_=projF[:, 12 + k, :].rearrange("p (b s) -> p b s", s=32)),
                ['projF'], ['xcF'])
        for k in range(8):
            cv = cacc[:, k, :].rearrange("p (b s) -> p b s", s=32)
            Vop(lambda e, k=k, cv=cv: e.tensor_scalar(out=cv, in0=xcF[:, k, :, 0:32], scalar1=cwb[:, k, 0:1], scalar2=cwb[:, k, 4:5],
                                                      op0=ALU.mult, op1=ALU.add), ['xcF', 'cwb'], ['cacc'])
            for j in range(1, 4):
                Vop(lambda e, k=k, j=j, cv=cv: e.scalar_tensor_tensor(out=cv, in0=xcF[:, k, :, j:j + 32], scalar=cwb[:, k, j:j + 1],
                                                                      in1=cv, op0=ALU.mult, op1=ALU.add),
                    ['xcF', 'cwb', 'cacc'], ['cacc'])
        Aop(lambda e: e.activation(out=qkF[:], in_=cacc[:], func=AF.Silu), ['cacc'], ['qkF'])
        Aop(lambda e: e.mul(out=qkF[:, 0:4, :], in_=qkF[:, 0:4, :], mul=128 ** -0.5), ['qkF'], ['qkF'])
        Vop(lambda e: e.tensor_copy(out=tmp12[:].rearrange("p k (b j) -> p k b j", j=3), in_=xcF[:, :, :, 32:35]),
            ['xcF'], ['tmp12'])
        for k in range(8):
            bi = 4 + k // 4
            Pop(lambda e, k=k, bi=bi: e.transpose(banks[bi][0:12, (k % 4) * 128:(k % 4 + 1) * 128], tmp12[:, k, :], ident[:]),
                ['tmp12', 'ident'], [f'bank{bi}'])
        Aop(lambda e: e.copy(out=stg[0:12, 0:512], in_=banks[4][0:12, 0:512]), ['bank4'], ['stg'])
        Aop(lambda e: e.copy(out=stg[0:12, 512:1024], in_=banks[5][0:12, 0:512]), ['bank5'], ['stg'])
        fw.dma('sp', nbconv[e_], stg[0:12, :], reads=['stg'], writes=['nbconv'])
        fw.dma('sp', Caug[:, :, 0:128], sbc[e_].rearrange("b h k v -> k (b h) v"), writes=['Caug'])
        fw.dma('sp', stg[0:16, 0:128], sbn[e_], writes=['stg'])
        Pop(lambda e: e.transpose(banks[4][:, 0:16], stg[0:16, 0:128], ident[0:16, 0:16]), ['stg', 'ident'], ['bank4'])
        Aop(lambda e: e.copy(out=Caug[:, :, 128], in_=banks[4][:, 0:16]), ['bank4'], ['Caug'])
        fw.dma('sp', m0bc[:], sbm[e_].partition_broadcast(128), writes=['m0bc'])
        fw.dma('sp', igfg_bc[:], igfg[e_].partition_broadcast(128), writes=['igfg_bc'])
        fw.dma('sp', bnorm_bc[:], b_norm[e_].partition_broadcast(128), writes=['bnorm_bc'])
        Vop(lambda e: e.tensor_scalar(out=nig[:], in0=igfg_bc[:], scalar1=-1.0, scalar2=None, op0=ALU.mult), ['igfg_bc'], ['nig'])
        for j in range(8):
            bi = 4 + j // 4
            Pop(lambda e, j=j, bi=bi: e.matmul(banks[bi][:, (j % 4) * 128:(j % 4 + 1) * 128], lhsT=selr[0:8, j * 128:(j + 1) * 128],
                                               rhs=projF[0:8, 28, :], start=True, stop=True), ['selr', 'projF'], [f'bank{bi}'])
        for h in range(4):
            Vop(lambda e, h=h: e.tensor_scalar(out=LI[:, h, :], in0=banks[4][:, h * 128:(h + 1) * 128], scalar1=igfg_bc[:, h:h + 1],
                                               scalar2=None, op0=ALU.add), ['bank4', 'igfg_bc'], ['LI'])
            Aop(lambda e, h=h: e.activation(out=LF[:, h, :], in_=banks[5][:, h * 128:(h + 1) * 128], func=AF.Exp,
                                            bias=nig[:, 4 + h:5 + h], scale=-1.0), ['bank5', 'nig'], ['LF'])
        Aop(lambda e: e.activation(out=LF[:], in_=LF[:], func=AF.Ln, bias=onec[:], scale=1.0), ['LF', 'onec'], ['LF'])
        Vop(lambda e: e.tensor_scalar(out=LF[:], in0=LF[:], scalar1=-1.0, scalar2=None, op0=ALU.mult), ['LF'], ['LF'])
        for h in range(4):
            Vop(lambda e, h=h: e.tensor_tensor_scan(out=BR[:, h, :], data0=scanmask[:], data1=LF[:, h, :], initial=0.0,
                                                    op0=ALU.mult, op1=ALU.add), ['scanmask', 'LF'], ['BR'])
            for b in range(4):
                seg = slice(b * 32, (b + 1) * 32)
                Vop(lambda e, h=h, b=b, seg=seg: e.tensor_tensor_scan(
                    out=MT[:, h, seg], data0=LF[:, h, seg], data1=LI[:, h, seg], initial=m0bc[:, b * 4 + h:b * 4 + h + 1],
                    op0=ALU.add, op1=ALU.max), ['LF', 'LI', 'm0bc'], ['MT'])
        for h in range(4):
            hs = slice(h * 128, (h + 1) * 128)
            Pop(lambda e, h=h: e.matmul(banks[4][:, 0:128], lhsT=qkF[:, 4 + h, :], rhs=qkF[:, h, :], start=True, stop=True),
                ['qkF'], ['bank4'])
            Vop(lambda e, h=h: e.tensor_tensor(out=WK[5][:, 0:128], in0=BR[:, h, :], in1=MT[:, h, :], op=ALU.subtract),
                ['BR', 'MT'], ['wk5'])
            Vop(lambda e: e.tensor_tensor(out=WK[5][:, 0:128], in0=WK[5][:, 0:128], in1=negbd[:], op=ALU.add),
                ['wk5', 'negbd'], ['wk5'])
            Vop(lambda e, h=h: e.tensor_tensor(out=WK[6][:, 0:128], in0=LI[:, h, :], in1=BR[:, h, :], op=ALU.subtract),
                ['LI', 'BR'], ['wk6'])
            diag(WK[6][:, 0:128], 'wk6', CL[:, 8:9])
            Aop(lambda e: e.activation(out=WK[5][:, 0:128], in_=WK[5][:, 0:128], func=AF.Exp, bias=CL[:, 8:9], scale=1.0),
                ['wk5', 'cl'], ['wk5'])
            Vop(lambda e: e.tensor_tensor(out=WK[6][:, 0:128], in0=banks[4][:, 0:128], in1=WK[5][:, 0:128], op=ALU.mult),
                ['bank4', 'wk5', 'wk6'], ['wk6'])
            Pop(lambda e, h=h: e.matmul(banks[5][:, 0:129], lhsT=WK[6][:, 0:128], rhs=bvtok[:, h, :], start=True, stop=True),
                ['wk6', 'bvtok'], ['bank5'])
            for b in range(4):
                dst = banks[6][:, b * 129:(b + 1) * 129] if b < 3 else banks[4][:, 256:385]
                dk = 'bank6' if b < 3 else 'bank4'
                Pop(lambda e, h=h, b=b, dst=dst: e.matmul(dst, lhsT=qkF[:, h, :], rhs=Caug[:, b * 4 + h, :], start=True, stop=True),
                    ['qkF', 'Caug'], [dk])
            Vop(lambda e: e.tensor_scalar(out=WK[7][:, 0:129], in0=banks[6][:, 0:129], scalar1=bmask[:, 0:1], scalar2=None,
                                          op0=ALU.mult), ['bank6', 'bmask'], ['wk7'])
            for b in range(1, 4):
                src = banks[6][:, b * 129:(b + 1) * 129] if b < 3 else banks[4][:, 256:385]
                sk = 'bank6' if b < 3 else 'bank4'
                Vop(lambda e, b=b, src=src: e.scalar_tensor_tensor(out=WK[7][:, 0:129], in0=src, scalar=bmask[:, b:b + 1],
                                                                   in1=WK[7][:, 0:129], op0=ALU.mult, op1=ALU.add),
                    [sk, 'bmask', 'wk7'], ['wk7'])
            for b in range(4):
                seg = slice(b * 32, (b + 1) * 32)
                Vop(lambda e, h=h, b=b, seg=seg: e.scalar_tensor_tensor(
                    out=WK[8][:, seg], in0=BR[:, h, seg], scalar=m0bc[:, b * 4 + h:b * 4 + h + 1], in1=MT[:, h, seg],
                    op0=ALU.add, op1=ALU.subtract), ['BR', 'MT', 'm0bc'], ['wk8'])
            Aop(lambda e: e.activation(out=WK[8][:, 0:128], in_=WK[8][:, 0:128], func=AF.Exp), ['wk8'], ['wk8'])
            diag(WK[8][:, 0:128], 'wk8', CL[:, 9:10])
            Vop(lambda e: e.scalar_tensor_tensor(out=WK[9][:, 0:129], in0=WK[7][:, 0:129], scalar=CL[:, 9:10],
                                                 in1=banks[5][:, 0:129], op0=ALU.mult, op1=ALU.add),
                ['wk7', 'cl', 'bank5'], ['wk9'])
            Vop(lambda e: e.tensor_scalar(out=CL[:, 20:21], in0=WK[9][:, 128:129], scalar1=-1.0, scalar2=None, op0=ALU.mult),
                ['wk9'], ['cl'])
            Vop(lambda e: e.tensor_tensor(out=CL[:, 10:11], in0=CL[:, 20:21], in1=WK[9][:, 128:129], op=ALU.max),
                ['wk9', 'cl'], ['cl'])
            diag(MT[:, h, :], 'MT', CL[:, 11:12])
            Aop(lambda e: e.activation(out=CL[:, 12:13], in_=CL[:, 11:12], func=AF.Exp, scale=-1.0), ['cl'], ['cl'])
            Vop(lambda e: e.tensor_tensor(out=CL[:, 13:14], in0=CL[:, 10:11], in1=CL[:, 12:13], op=ALU.max), ['cl'], ['cl'])
            Vop(lambda e: e.reciprocal(out=CL[:, 14:15], in_=CL[:, 13:14]), ['cl'], ['cl'])
            Vop(lambda e: e.tensor_scalar(out=WK[10][:, 0:128], in0=WK[9][:, 0:128], scalar1=CL[:, 14:15], scalar2=None,
                                          op0=ALU.mult), ['wk9', 'cl'], ['wk10'])
            Vop(lambda e: e.scalar_tensor_tensor(out=WK[11][:, 0:128], in0=WK[10][:, 0:128], scalar=1.0, in1=WK[10][:, 0:128],
                                                 op0=ALU.mult, op1=ALU.mult, accum_out=CL[:, 15:16]), ['wk10'], ['wk11', 'cl'])
            Aop(lambda e: e.activation(out=CL[:, 16:17], in_=CL[:, 15:16], func=AF.Sqrt, scale=1.0 / 128, bias=epsc[:]),
                ['cl', 'epsc'], ['cl'])
            Vop(lambda e: e.reciprocal(out=CL[:, 17:18], in_=CL[:, 16:17]), ['cl'], ['cl'])
            Aop(lambda e, hs=hs: e.activation(out=WK[11][:, 0:128], in_=botok[:, hs], func=AF.Sigmoid), ['botok', 'wk11'], ['wk11'])
            Vop(lambda e, hs=hs: e.scalar_tensor_tensor(out=WK[10][:, 0:128], in0=WK[10][:, 0:128], scalar=CL[:, 17:18],
                                                        in1=bnorm_bc[:, hs], op0=ALU.mult, op1=ALU.mult),
                ['wk10', 'cl', 'bnorm_bc'], ['wk10'])
            Vop(lambda e: e.tensor_tensor(out=WK[10][:, 0:128], in0=WK[10][:, 0:128], in1=WK[11][:, 0:128], op=ALU.mult),
                ['wk10', 'wk11'], ['wk10'])
            Pop(lambda e: e.transpose(banks[5][:, 256:384], WK[10][:, 0:128], ident[:]), ['wk10', 'ident'], ['bank5'])
            Aop(lambda e, h=h: e.copy(out=mixF[:, 4 + h, :], in_=banks[5][:, 256:384]), ['bank5'], ['mixF'])
            Pop(lambda e, h=h: e.transpose(banks[4][:, 128:256], qkF[:, 4 + h, :], ident[:]), ['qkF', 'ident'], ['bank4'])
            Aop(lambda e: e.copy(out=WK[12][:, 0:128], in_=banks[4][:, 128:256]), ['bank4'], ['wk12'])
            Vop(lambda e, h=h: e.tensor_tensor(out=WK[13][:, 0:128], in0=LI[:, h, :], in1=BR[:, h, :], op=ALU.subtract),
                ['LI', 'BR'], ['wk13'])
            for b in range(4):
                seg = slice(b * 32, (b + 1) * 32)
                last = b * 32 + 31
                Vop(lambda e, h=h, seg=seg, last=last: e.tensor_scalar(
                    out=WK[13][:, seg], in0=WK[13][:, seg], scalar1=BR[:, h, last:last + 1], scalar2=MT[:, h, last:last + 1],
                    op0=ALU.add, op1=ALU.subtract), ['wk13', 'BR', 'MT'], ['wk13'])
            Aop(lambda e: e.activation(out=WK[13][:, 0:128], in_=WK[13][:, 0:128], func=AF.Exp), ['wk13'], ['wk13'])
            diag(WK[13][:, 0:128], 'wk13', CL[:, 18:19])
            Vop(lambda e: e.tensor_scalar(out=WK[12][:, 0:128], in0=WK[12][:, 0:128], scalar1=CL[:, 18:19], scalar2=None,
                                          op0=ALU.mult), ['wk12', 'cl'], ['wk12'])
            for b in range(4):
                last = b * 32 + 31
                idx = b * 4 + h
                Vop(lambda e, b=b: e.tensor_scalar(out=WK[11][:, 0:128], in0=WK[12][:, 0:128], scalar1=bmask[:, b:b + 1], scalar2=None,
                                                   op0=ALU.mult), ['wk12', 'bmask', 'wk11'], ['wk11'])
                Pop(lambda e, h=h: e.matmul(banks[5][:, 0:129], lhsT=WK[11][:, 0:128], rhs=bvtok[:, h, :], start=True, stop=True),
                    ['wk11', 'bvtok'], ['bank5'])
                Vop(lambda e, h=h, last=last, idx=idx: e.scalar_tensor_tensor(
                    out=CL[:, 19:20], in0=BR[:, h, last:last + 1], scalar=m0bc[:, idx:idx + 1], in1=MT[:, h, last:last + 1],
                    op0=ALU.add, op1=ALU.subtract), ['BR', 'MT', 'm0bc'], ['cl'])
                Aop(lambda e: e.activation(out=CL[:, 19:20], in_=CL[:, 19:20], func=AF.Exp), ['cl'], ['cl'])
                Vop(lambda e, idx=idx: e.scalar_tensor_tensor(out=WK[9][:, 0:129], in0=Caug[:, idx, :], scalar=CL[:, 19:20],
                                                              in1=banks[5][:, 0:129], op0=ALU.mult, op1=ALU.add),
                    ['Caug', 'cl', 'bank5', 'wk9'], ['wk9'])
                fw.dma('sp', nbc[e_, b, h], WK[9][:, 0:128], reads=['wk9'], writes=['nbc'])
                fw.dma('sp', nbn[e_, idx].rearrange("(k o) -> k o", o=1), WK[9][:, 128:129], reads=['wk9'], writes=['nbn'])
                Aop(lambda e, h=h, last=last, idx=idx: e.copy(out=mrow[0:1, idx:idx + 1], in_=MT[0:1, h, last:last + 1]),
                    ['MT'], ['mrow'])
        fw.dma('sp', nbm[e_].rearrange("(o k) -> o k", o=1), mrow[0:1, :], reads=['mrow'], writes=['nbm'])

    def vts(out, in0, s1, op0, r, w, s2=None, op1=None):
        if op1 is None:
            Vop(lambda e: e.tensor_scalar(out=out, in0=in0, scalar1=s1, scalar2=None, op0=op0), r, w)
        else:
            Vop(lambda e: e.tensor_scalar(out=out, in0=in0, scalar1=s1, scalar2=s2, op0=op0, op1=op1), r, w)

    def vtt(out, in0, in1, op, r, w):
        Vop(lambda e: e.tensor_tensor(out=out, in0=in0, in1=in1, op=op), r, w)

    def vstt(out, in0, sc, in1, op0, op1, r, w):
        Vop(lambda e: e.scalar_tensor_tensor(out=out, in0=in0, scalar=sc, in1=in1, op0=op0, op1=op1), r, w)

    def act(out, in_, func, r, w, **kw):
        Aop(lambda e: e.activation(out=out, in_=in_, func=func, **kw), r, w)

    def vcopy(out, in_, r, w):
        Vop(lambda e: e.tensor_copy(out=out, in_=in_), r, w)

    def acopy(out, in_, r, w):
        Aop(lambda e: e.copy(out=out, in_=in_), r, w)

    def tpose(out, in_, idn, r, w):
        Pop(lambda e: e.transpose(out, in_, idn), list(r) + ['ident'], w)

    MUL, ADD, SUB = ALU.mult, ALU.add, ALU.subtract

    def s5(o_, sfx, oc32):
        A = lambda n, shp, dt=F32: fw.sb(n + sfx, shp, dt)
        stg3 = A("stg3", [48, 128]); stgd = A("stgd", [16, 32]); d32 = A("d32", [32, 16]); stgx = A("stgx", [128, 128])
        X0T = A("X0T", [128, 128]); Z = A("Z", [128, 20, 16]); ZI = A("ZI", [128, 16], mybir.dt.int32)
        BRI = A("BRI", [128, 2, 16, 16]); T1 = A("T1", [128, 16, 16]); T2 = A("T2", [128, 16, 16])
        XB = A("XB", [128, 2, 16, 32]); BBT = A("BBT", [32, 2, 16, 128]); u32 = A("u32", [32, 16, NTOK])
        XR = A("XR", [128, 16, NTOK]); XI = A("XI", [128, 16, NTOK]); ARb = A("ARb", [128, 16, 4]); AIb = A("AIb", [128, 16, 4])
        TT = A("TT", [128, 4, 16, 4]); ZC = A("ZC", [128, 128]); CZ = A("CZ", [128, 2, 4, 128])
        y32 = A("y32", [32, 16, NTOK]); G1 = A("G1", [32, 16, NTOK]); yg16 = A("yg16", [32, 16, NTOK], BF16)
        tmp32 = A("tmp32", [32, NTOK]); XF = A("XF", [128, 128])
        z = lambda i: Z[:, i, :]
        zk = ['Z']
        fw.dma('sp', stg3[:], s5p[o_], writes=['stg3'])
        tpose(banks[4][:, 0:48], stg3[0:48, :], ident[0:48, 0:48], ['stg3'], ['bank4'])
        acopy(Z[:, 0:3, :], banks[4][:, 0:48].rearrange("p (a i) -> p a i", a=3), ['bank4'], zk)
        fw.dma('sp', stgd[:], cd16[o_], writes=['stgd'])
        tpose(banks[5][0:32, 0:16], stgd[0:16, :], ident[0:16, 0:16], ['stgd'], ['bank5'])
        acopy(d32[:], banks[5][0:32, 0:16], ['bank5'], ['d32'])
        fw.dma('sp', stgx[:], sc0[o_], writes=['stgx'])
        tpose(banks[4][:, 128:256], stgx[:], ident[:], ['stgx'], ['bank4'])
        acopy(X0T[:], banks[4][:, 128:256], ['bank4'], ['X0T'])
        act(z(3), z(2), AF.Exp, zk, zk)
        vtt(z(4), z(0), z(3), MUL, zk, zk)
        act(z(5), z(4), AF.Exp, zk, zk)
        vtt(z(6), z(1), z(3), MUL, zk, zk)
        vts(z(7), z(6), 1.0 / (2 * math.pi), MUL, zk, zk, 0.5, ADD)
        vcopy(ZI[:], z(7), zk, ['ZI'])
        vcopy(z(7), ZI[:], ['ZI'], zk)
        vstt(z(8), z(7), -2 * math.pi, z(6), MUL, ADD, zk, zk)
        act(z(9), z(8), AF.Sin, zk, zk, scale=0.25)
        act(z(10), z(8), AF.Sin, zk + ['halfpi'], zk, scale=0.25, bias=halfpi[:])
        vstt(z(11), z(9), 2.0, z(10), MUL, MUL, zk, zk)
        vtt(z(12), z(9), z(9), MUL, zk, zk)
        vts(z(12), z(12), -2.0, MUL, zk, zk, 1.0, ADD)
        vstt(z(13), z(11), 2.0, z(12), MUL, MUL, zk, zk)
        vtt(z(14), z(11), z(11), MUL, zk, zk)
        vts(z(14), z(14), -2.0, MUL, zk, zk, 1.0, ADD)
        vtt(z(15), z(5), z(14), MUL, zk, zk)
        vtt(z(16), z(5), z(13), MUL, zk, zk)
        vtt(z(17), z(0), z(0), MUL, zk, zk)
        vtt(z(18), z(1), z(1), MUL, zk, zk)
        vtt(z(17), z(17), z(18), ADD, zk, zk)
        Vop(lambda e: e.reciprocal(out=z(17), in_=z(17)), zk, zk)
        vts(z(18), z(15), -1.0, ADD, zk, zk)
        vtt(z(19), z(18), z(0), MUL, zk, zk)
        vtt(z(7), z(16), z(1), MUL, zk, zk)
        vtt(z(19), z(19), z(7), ADD, zk, zk)
        vtt(z(19), z(19), z(17), MUL, zk, zk)
        vtt(z(7), z(16), z(0), MUL, zk, zk)
        vtt(z(8), z(18), z(1), MUL, zk, zk)
        vtt(z(7), z(7), z(8), SUB, zk, zk)
        vtt(z(18), z(7), z(17), MUL, zk, zk)
        for ri, src in ((0, c_b_re), (1, c_b_im)):
            v = src[o_].rearrange("(i a) p c -> (a p) i c", a=2)
            for i0 in range(0, 16, 4):
                fw.dma('sp', BRI[:, ri, i0:i0 + 4, :], v[:, i0:i0 + 4, :], writes=[f'BRI{ri}{i0}'], allow_slow_non_contiguous=True)
        brk = [f'BRI{ri}{i0}' for ri in range(2) for i0 in range(0, 16, 4)]
        crb = z(19).unsqueeze(2).to_broadcast([128, 16, 16])
        cib = z(18).unsqueeze(2).to_broadcast([128, 16, 16])
        Vop(lambda e: e.memset(XB[:], 0.0), [], ['XB'])
        for ri in range(2):
            if ri == 0:
                vtt(T1[:], BRI[:, 0], crb, MUL, brk + zk, ['T1'])
                vtt(T2[:], BRI[:, 1], cib, MUL, brk + zk, ['T2'])
                vtt(T1[:], T1[:], T2[:], SUB, ['T1', 'T2'], ['T1'])
            else:
                vtt(T1[:], BRI[:, 1], crb, MUL, brk + zk + ['XB'], ['T1'])
                vtt(T2[:], BRI[:, 0], cib, MUL, brk + zk, ['T2'])
                vtt(T1[:], T1[:], T2[:], ADD, ['T1', 'T2'], ['T1'])
            vcopy(XB[0:64, ri, :, 0:16], T1[0:64], ['T1'], ['XB'])
            vcopy(XB[64:128, ri, :, 16:32], T1[64:128], ['T1'], ['XB'])
        for ri in range(2):
            for i0 in range(0, 16, 4):
                bi = 4 + (i0 // 4) % 2
                for i in range(i0, i0 + 4):
                    tpose(banks[bi][0:32, (i % 4) * 128:(i % 4 + 1) * 128], XB[:, ri, i, :], ident[:], ['XB'], [f'bank{bi}'])
                acopy(BBT[0:32, ri, i0:i0 + 4, :], banks[bi][0:32, 0:512].rearrange("p (i q) -> p i q", i=4), [f'bank{bi}'], ['BBT'])
        def cons_u(mt, msz, p, pkey):
            acopy(u32[0:32, mt, :], p, [pkey], ['u32'])
        linear(std(hF, 8), 'hF', od_w_in[o_][:, 0:512], 512, cons_u, mtile=32)
        for ri, X, xk in ((0, XR, 'XR'), (1, XI, 'XI')):
            for i0 in range(0, 16, 4):
                bi = 4 + (i0 // 4) % 2
                for i in range(i0, i0 + 4):
                    Pop(lambda e, ri=ri, i=i, bi=bi: e.matmul(banks[bi][:, (i % 4) * 128:(i % 4 + 1) * 128], lhsT=BBT[0:32, ri, i, :],
                                                              rhs=u32[0:32, i, :], start=True, stop=True), ['BBT', 'u32'], [f'bank{bi}'])
                acopy(X[:, i0:i0 + 4, :], banks[bi][:, 0:512].rearrange("p (i q) -> p i q", i=4), [f'bank{bi}'], [xk])
        vcopy(ARb[:], z(15).unsqueeze(2).to_broadcast([128, 16, 4]), zk, ['ARb'])
        vcopy(AIb[:], z(16).unsqueeze(2).to_broadcast([128, 16, 4]), zk, ['AIb'])
        for t in range(32):
            if t == 0:
                pr = X0T[:, 0:64].rearrange("p (b i) -> p i b", b=4)
                pi = X0T[:, 64:128].rearrange("p (b i) -> p i b", b=4)
                pk = ['X0T']
            else:
                pr = XR[:, :, t - 1:128:32]
                pi = XI[:, :, t - 1:128:32]
                pk = ['XR', 'XI']
            vtt(TT[:, 0], ARb[:], pr, MUL, ['ARb'] + pk, ['TT0'])
            vtt(TT[:, 1], AIb[:], pi, MUL, ['AIb'] + pk, ['TT1'])
            vtt(TT[:, 2], ARb[:], pi, MUL, ['ARb'] + pk, ['TT2'])
            vtt(TT[:, 3], AIb[:], pr, MUL, ['AIb'] + pk, ['TT3'])
            vtt(TT[:, 0], TT[:, 0], TT[:, 1], SUB, ['TT0', 'TT1'], ['TT0'])
            vtt(TT[:, 2], TT[:, 2], TT[:, 3], ADD, ['TT2', 'TT3'], ['TT2'])
            vtt(XR[:, :, t:128:32], XR[:, :, t:128:32], TT[:, 0], ADD, ['XR', 'TT0'], ['XR'])
            vtt(XI[:, :, t:128:32], XI[:, :, t:128:32], TT[:, 2], ADD, ['XI', 'TT2'], ['XI'])
        for ri, src in ((0, c_c_re), (1, c_c_im)):
            for ct in range(4):
                Vop(lambda e: e.memset(ZC[:], 0.0), [], ['ZC'])
                for gl in range(8):
                    a = gl % 2
                    fw.dma('sp', ZC[gl * 16:(gl + 1) * 16, 64 * a:64 * a + 64], src[o_, 8 * ct + gl], reads=['ZC'], writes=[f'ZCd{gl}'])
                tpose(banks[4][:, 0:128], ZC[:], ident[:], [f'ZCd{gl}' for gl in range(8)] + ['ZC'], ['bank4'])
                if ri == 0:
                    acopy(CZ[:, 0, ct, :], banks[4][:, 0:128], ['bank4'], ['CZ'])
                else:
                    Aop(lambda e, ct=ct: e.mul(out=CZ[:, 1, ct, :], in_=banks[4][:, 0:128], mul=-1.0), ['bank4'], ['CZ'])
        for i0 in range(0, 16, 4):
            bi = 4 + (i0 // 4) % 2
            for i in range(i0, i0 + 4):
                ct, j = i // 4, i % 4
                Pop(lambda e, i=i, ct=ct, j=j, bi=bi: e.matmul(banks[bi][0:32, j * 128:(j + 1) * 128], lhsT=CZ[:, 0, ct, 32 * j:32 * j + 32],
                                                               rhs=XR[:, i, :], start=True, stop=False), ['CZ', 'XR'], [f'bank{bi}'])
                Pop(lambda e, i=i, ct=ct, j=j, bi=bi: e.matmul(banks[bi][0:32, j * 128:(j + 1) * 128], lhsT=CZ[:, 1, ct, 32 * j:32 * j + 32],
                                                               rhs=XI[:, i, :], start=False, stop=True), ['CZ', 'XI'], [f'bank{bi}'])
            for i in range(i0, i0 + 4):
                j = i % 4
                vstt(y32[0:32, i, :], u32[0:32, i, :], d32[0:32, i:i + 1], banks[bi][0:32, j * 128:(j + 1) * 128], MUL, ADD,
                     ['u32', 'd32', f'bank{bi}'], ['y32'])
        vtt(G1[:], y32[:], y32[:], MUL, ['y32'], ['G1'])
        vts(G1[:], G1[:], 0.044715, MUL, ['G1'], ['G1'], 1.0, ADD)
        vtt(G1[:], G1[:], y32[:], MUL, ['G1', 'y32'], ['G1'])
        act(G1[:], G1[:], AF.Sigmoid, ['G1'], ['G1'], scale=2.0 * math.sqrt(2.0 / math.pi))
        vtt(y32[:], y32[:], G1[:], MUL, ['y32', 'G1'], ['y32'])
        vcopy(yg16[:], y32[:], ['y32'], ['yg16'])

        def cons_glu(mt, msz, p, pkey):
            act(tmp32[0:32, :], p, AF.Sigmoid, [pkey], ['tmp32'])
            vtt(oc32[0:32, mt, :], y32[0:32, mt, :], tmp32[0:32, :], MUL, ['y32', 'tmp32'], ['oc32'])
        linear([(i * 32, 32, 0, yg16[0:32, i, :]) for i in range(16)], 'yg16', c_w_glu[o_], 512, cons_glu, mtile=32)
        vcopy(XF[:, 0:64].rearrange("p (b i) -> p i b", b=4), XR[:, :, 31:128:32], ['XR'], ['XF'])
        vcopy(XF[:, 64:128].rearrange("p (b i) -> p i b", b=4), XI[:, :, 31:128:32], ['XI'], ['XF'])
        tpose(banks[5][:, 0:128], XF[:], ident[:], ['XF'], ['bank5'])
        acopy(stgx[:], banks[5][:, 0:128], ['bank5'], ['stgx'])
        fw.dma('sp', ncs[o_], stgx[:], reads=['stgx'], writes=['ncs'])

    def rwkv(o_, sfx):
        A = lambda n, shp, dt=F32: fw.sb(n + sfx, shp, dt)
        STG = A("STG", [48, 1792]); DV = A("DV", [128, 42]); SH0 = A("SH0", [128, 14, 4]); SHN = A("SHN", [128, 14, 4])
        dcF = A("dcF", [128, 14, NTOK]); XM = A("XM", [128, 14, NTOK])
        QN = A("QN", [128, 4, NTOK]); QW = A("QW", [128, 4, NTOK]); QB = A("QB", [128, 4, NTOK]); QK = A("QK", [128, 4, NTOK])
        AA = A("AA", [128, 4, NTOK]); GG = A("GG", [128, 4, NTOK]); W1 = A("W1", [128, 4, NTOK]); W2 = A("W2", [128, 4, NTOK])
        YF = A("YF", [128, 4, NTOK])
        TL = A("TL", [128, NTOK], BF16); A16 = A("A16", [128, NTOK], BF16); SG = A("SG", [128, NTOK], BF16)
        S = A("S", [128, 16, 64]); RQ = A("RQ", [128, 16, 64]); TMP = A("TMP", [128, 16, 64]); SA = A("SA", [128, 16])
        dv = lambda r: DV[:, r:r + 1]
        linear(std(hF, 8), 'hF', od_w_in[o_][:, 512:2304], 1792, to_tile(dcF, 'dcF'))
        fw.dma('sp', STG[0:42, 0:128], dvecs[o_], writes=['STG'])
        tpose(banks[4][:, 0:42], STG[0:42, 0:128], ident[0:42, 0:42], ['STG'], ['bank4'])
        acopy(DV[:], banks[4][:, 0:42], ['bank4'], ['DV'])
        fw.dma('sp', STG[0:4, :], sdsh[o_], writes=['STG'])
        for k in range(14):
            tpose(banks[5][:, k * 4:k * 4 + 4], STG[0:4, k * 128:(k + 1) * 128], ident[0:4, 0:4], ['STG'], ['bank5'])
        acopy(SH0[:], banks[5][:, 0:56].rearrange("p (k b) -> p k b", b=4), ['bank5'], ['SH0'])
        vcopy(XM[:, :, 1:128], dcF[:, :, 0:127], ['dcF'], ['XM'])
        vcopy(XM[:, :, 0:128:32], SH0[:], ['SH0', 'XM'], ['XM'])
        vtt(XM[:], XM[:], dcF[:], SUB, ['XM', 'dcF'], ['XM'])
        for k in range(14):
            vstt(XM[:, k, :], XM[:, k, :], dv(k), dcF[:, k, :], MUL, ADD, ['XM', 'DV', 'dcF'], ['XM'])
        vcopy(SHN[:], dcF[:, :, 31:128:32], ['dcF'], ['SHN'])
        for k0 in range(0, 14, 4):
            bi = 4 + (k0 // 4) % 2
            n = min(4, 14 - k0)
            for k in range(k0, k0 + n):
                tpose(banks[bi][0:4, (k % 4) * 128:(k % 4 + 1) * 128], SHN[:, k, :], ident[:], ['SHN'], [f'bank{bi}'])
            acopy(STG[0:4, k0 * 128:(k0 + n) * 128], banks[bi][0:4, 0:n * 128], [f'bank{bi}'], ['STG'])
        fw.dma('sp', ndsh[o_], STG[0:4, :], reads=['STG'], writes=['ndsh'])
        act(TL[0:64, :], XM[0:64, 12, :], AF.Tanh, ['XM'], ['TL'])
        acopy(A16[64:128, :], XM[64:128, 12, :], ['XM'], ['A16'])
        act(SG[:], XM[:, 13, :], AF.Sigmoid, ['XM'], ['SG'])

        def cons_w(mt, msz, p, pkey):
            act(QW[:, mt, :], p, AF.Sigmoid, [pkey, 'DV'], ['QW'], bias=dv(14 + mt), scale=1.0)
        linear([(0, 64, 0, TL[0:64, :])], 'TL', d_w_w2[o_], 512, cons_w)
        act(QW[:], QW[:], AF.Exp, ['QW'], ['QW'], scale=-math.exp(-0.5))

        def cons_a(mt, msz, p, pkey):
            act(AA[:, mt, :], p, AF.Sigmoid, [pkey, 'DV'], ['AA'], bias=dv(18 + mt), scale=1.0)
        linear([(0, 64, 64, A16[64:128, :])], 'A16', d_w_a2[o_], 512, cons_a)
        linear([(0, 128, 0, SG[:, :])], 'SG', d_w_g2[o_], 512, to_tile(GG, 'GG'))
        for k in range(4):
            vts(W1[:, k, :], XM[:, 4 + k, :], dv(22 + k), MUL, ['XM', 'DV'], ['W1'])
        vtt(W2[:], W1[:], W1[:], MUL, ['W1'], ['W2'])
        Pop(lambda e: e.matmul(banks[4][:, 0:512], lhsT=blk[:], rhs=W2[:].rearrange("p k n -> p (k n)"), start=True, stop=True),
            ['blk', 'W2'], ['bank4'])
        act(W2[:].rearrange("p k n -> p (k n)"), banks[4][:, 0:512], AF.Sqrt, ['bank4'], ['W2'])
        vts(W2[:], W2[:], 1e-12, ALU.max, ['W2'], ['W2'])
        Vop(lambda e: e.reciprocal(out=W2[:], in_=W2[:]), ['W2'], ['W2'])
        vtt(W1[:], W1[:], W2[:], MUL, ['W1', 'W2'], ['W1'])
        vts(QN[:], W1[:], -1.0, MUL, ['W1'], ['QN'])
        vtt(QB[:], W1[:], AA[:], MUL, ['W1', 'AA'], ['QB'])
        for k in range(4):
            vts(W2[:, k, :], AA[:, k, :], -1.0, ADD, ['AA', 'DV', 'W2'], ['W2'], dv(26 + k), MUL)
        vts(W2[:], W2[:], 1.0, ADD, ['W2'], ['W2'])
        vtt(QK[:], XM[:, 4:8, :], W2[:], MUL, ['XM', 'W2'], ['QK'])
        QR = XM[:, 0:4, :]
        VV = XM[:, 8:12, :]
        for hl in range(2):
            for hh in range(4):
                fw.dma('sp', S[64 * hl:64 * hl + 64, hh * 4:hh * 4 + 4, :], sds[o_, :, 2 * hh + hl].rearrange("b v k -> v b k"),
                       writes=[f'S{hl}{hh}'])
        id2b = id2[:].unsqueeze(1).unsqueeze(1).to_broadcast([128, 4, 4, 64])
        R4 = RQ[:].rearrange("p (a b) k -> p a b k", a=4)
        S4 = S[:].rearrange("p (a b) k -> p a b k", a=4)
        T4 = TMP[:].rearrange("p (a b) k -> p a b k", a=4)
        sk = [f'S{hl}{hh}' for hl in range(2) for hh in range(4)] + ['S']
        slot = [0]

        def bcast(Q, qkeys, t):
            b0 = 2 * (slot[0] % 3)
            slot[0] += 1
            vtt(R4, id2b, Q[:, :, t:128:32].unsqueeze(3).to_broadcast([128, 4, 4, 64]), MUL, ['id2'] + qkeys, ['RQ'])
            for hf in range(2):
                Pop(lambda e, hf=hf, b0=b0: e.matmul(banks[b0 + hf][:, 0:512], lhsT=blk[:],
                                                     rhs=RQ[:, 8 * hf:8 * hf + 8, :].rearrange("p a k -> p (a k)"),
                                                     start=True, stop=True), ['blk', 'RQ'], [f'bank{b0 + hf}'])
            return [(banks[b0 + hf][:, 0:512].rearrange("p (a k) -> p a k", k=64), f'bank{b0 + hf}') for hf in range(2)]

        for t in range(32):
            P = bcast(QN, ['QN'], t)
            for hf in range(2):
                vtt(TMP[:, 8 * hf:8 * hf + 8, :], S[:, 8 * hf:8 * hf + 8, :], P[hf][0], MUL, sk + [P[hf][1]], ['TMP'])
            Vop(lambda e: e.tensor_reduce(out=SA[:], in_=TMP[:], axis=AX.X, op=ADD), ['TMP'], ['SA'])
            P = bcast(QW, ['QW'], t)
            for hf in range(2):
                vtt(S[:, 8 * hf:8 * hf + 8, :], S[:, 8 * hf:8 * hf + 8, :], P[hf][0], MUL, sk + [P[hf][1]], ['S'])
            P = bcast(QB, ['QB'], t)
            for hf in range(2):
                vtt(TMP[:, 8 * hf:8 * hf + 8, :], P[hf][0], SA[:, 8 * hf:8 * hf + 8].unsqueeze(2).to_broadcast([128, 8, 64]), MUL,
                    ['SA', P[hf][1]], ['TMP'])
            vtt(S[:], S[:], TMP[:], ADD, sk + ['TMP'], ['S'])
            P = bcast(QK, ['QK'], t)
            for hf in range(2):
                vtt(T4[:, 2 * hf:2 * hf + 2], P[hf][0].rearrange("p (a b) k -> p a b k", a=2),
                    VV[:, 2 * hf:2 * hf + 2, t:128:32].unsqueeze(3).to_broadcast([128, 2, 4, 64]), MUL, ['XM', P[hf][1]], ['TMP'])
            vtt(S[:], S[:], TMP[:], ADD, sk + ['TMP'], ['S'])
            P = bcast(QR, ['XM'], t)
            for hf in range(2):
                vtt(TMP[:, 8 * hf:8 * hf + 8, :], S[:, 8 * hf:8 * hf + 8, :], P[hf][0], MUL, sk + [P[hf][1]], ['TMP'])
            Vop(lambda e, t=t: e.tensor_reduce(out=YF[:, :, t:128:32], in_=T4, axis=AX.X, op=ADD), ['TMP'], ['YF'])
        for hl in range(2):
            for hh in range(4):
                fw.dma('sp', nds[o_, :, 2 * hh + hl].rearrange("b v k -> v b k"), S[64 * hl:64 * hl + 64, hh * 4:hh * 4 + 4, :],
                       reads=sk, writes=[f'nds{hl}{hh}'])
        flat = lambda tl: tl[:].rearrange("p k n -> p (k n)")
        Pop(lambda e: e.matmul(banks[4][:, 0:512], lhsT=blk[:], rhs=flat(YF), start=True, stop=True), ['blk', 'YF'], ['bank4'])
        vstt(flat(W1), banks[4][:, 0:512], -1.0 / 64, flat(YF), MUL, ADD, ['bank4', 'YF'], ['W1'])
        vtt(W2[:], W1[:], W1[:], MUL, ['W1'], ['W2'])
        Pop(lambda e: e.matmul(banks[5][:, 0:512], lhsT=blk[:], rhs=flat(W2), start=True, stop=True), ['blk', 'W2'], ['bank5'])
        act(flat(W2), banks[5][:, 0:512], AF.Sqrt, ['bank5', 'gneps'], ['W2'], scale=1.0 / 64, bias=gneps[:])
        Vop(lambda e: e.reciprocal(out=W2[:], in_=W2[:]), ['W2'], ['W2'])
        vtt(W1[:], W1[:], W2[:], MUL, ['W1', 'W2'], ['W1'])
        for k in range(4):
            vts(W1[:, k, :], W1[:, k, :], dv(34 + k), MUL, ['W1', 'DV'], ['W1'], dv(38 + k), ADD)
        vtt(W2[:], QR, QK[:], MUL, ['XM', 'QK'], ['W2'])
        for k in range(4):
            vts(W2[:, k, :], W2[:, k, :], dv(30 + k), MUL, ['W2', 'DV'], ['W2'])
        Pop(lambda e: e.matmul(banks[4][:, 0:512], lhsT=blk[:], rhs=flat(W2), start=True, stop=True), ['blk', 'W2'], ['bank4'])
        vcopy(W2[:], VV, ['XM', 'bank4'], ['W2'])
        vtt(flat(W2), banks[4][:, 0:512], flat(W2), MUL, ['bank4', 'W2'], ['W2'])
        vtt(W1[:], W1[:], W2[:], ADD, ['W1', 'W2'], ['W1'])
        vtt(mixF[:, 4:8, :], W1[:], GG[:], MUL, ['W1', 'GG'], ['mixF'])

    def s5_prompt(o_):
        A = lambda n, shp, dt=F32: fw.sb(n + f"_s5p{o_}", shp, dt)
        stg3 = A("stg3", [48, 128]); stgd = A("stgd", [16, 32]); d32 = A("d32", [32, 16])
        Z = A("Z", [128, 20, 16]); ZI = A("ZI", [128, 16], mybir.dt.int32)
        BRI = A("BRI", [128, 2, 16, 16]); T1 = A("T1", [128, 16, 16]); T2 = A("T2", [128, 16, 16])
        XB = A("XB", [128, 2, 16, 32]); BBT = A("BBT", [32, 2, 16, 128]); ZC = A("ZC", [128, 128]); CZ = A("CZ", [128, 2, 4, 128])
        CT = A("CT", [128, 16, 128]); ST = A("ST", [128, 16, 128]); PW = A("PW", [128, 2, 16]); PT2 = A("PT2", [128, 3, 16])
        u32 = A("u32", [32, 16, 128]); BU = A("BU", [128, 2, 512]); ZZ = A("ZZ", [128, 2, 16, 128]); XX = A("XX", [128, 2, 16, 128])
        E1 = A("E1", [128, 1024]); E2 = A("E2", [128, 1024]); zprev = A("zprev", [128, 2, 16]); xprev = A("xprev", [128, 2, 16])
        y32 = A("y32", [32, 16, 128]); G1 = A("G1", [32, 16, 128]); XF = A("XF", [128, 32]); stgo = A("stgo", [32, 128])
        z = lambda i: Z[:, i, :]
        zk = ['Z']
        fw.dma('sp', stg3[:], s5p[o_], writes=['stg3'])
        tpose(banks[4][:, 0:48], stg3[0:48, :], ident[0:48, 0:48], ['stg3'], ['bank4'])
        acopy(Z[:, 0:3, :], banks[4][:, 0:48].rearrange("p (a i) -> p a i", a=3), ['bank4'], zk)
        fw.dma('sp', stgd[:], cd16[o_], writes=['stgd'])
        tpose(banks[5][0:32, 0:16], stgd[0:16, :], ident[0:16, 0:16], ['stgd'], ['bank5'])
        acopy(d32[:], banks[5][0:32, 0:16], ['bank5'], ['d32'])
        act(z(3), z(2), AF.Exp, zk, zk)
        vtt(z(4), z(0), z(3), MUL, zk, zk)
        act(z(5), z(4), AF.Exp, zk, zk)
        vtt(z(6), z(1), z(3), MUL, zk, zk)
        vts(z(7), z(6), 1.0 / (2 * math.pi), MUL, zk, zk, 0.5, ADD)
        vcopy(ZI[:], z(7), zk, ['ZI'])
        vcopy(z(7), ZI[:], ['ZI'], zk)
        vstt(z(8), z(7), -2 * math.pi, z(6), MUL, ADD, zk, zk)
        act(z(9), z(8), AF.Sin, zk, zk, scale=0.25)
        act(z(10), z(8), AF.Sin, zk + ['halfpi'], zk, scale=0.25, bias=halfpi[:])
        vstt(z(11), z(9), 2.0, z(10), MUL, MUL, zk, zk)
        vtt(z(12), z(9), z(9), MUL, zk, zk)
        vts(z(12), z(12), -2.0, MUL, zk, zk, 1.0, ADD)
        vstt(z(13), z(11), 2.0, z(12), MUL, MUL, zk, zk)
        vtt(z(14), z(11), z(11), MUL, zk, zk)
        vts(z(14), z(14), -2.0, MUL, zk, zk, 1.0, ADD)
        vtt(z(15), z(5), z(14), MUL, zk, zk)
        vtt(z(16), z(5), z(13), MUL, zk, zk)
        vtt(z(17), z(0), z(0), MUL, zk, zk)
        vtt(z(18), z(1), z(1), MUL, zk, zk)
        vtt(z(17), z(17), z(18), ADD, zk, zk)
        Vop(lambda e: e.reciprocal(out=z(17), in_=z(17)), zk, zk)
        vts(z(18), z(15), -1.0, ADD, zk, zk)
        vtt(z(19), z(18), z(0), MUL, zk, zk)
        vtt(z(7), z(16), z(1), MUL, zk, zk)
        vtt(z(19), z(19), z(7), ADD, zk, zk)
        vtt(z(19), z(19), z(17), MUL, zk, zk)
        vtt(z(7), z(16), z(0), MUL, zk, zk)
        vtt(z(8), z(18), z(1), MUL, zk, zk)
        vtt(z(7), z(7), z(8), SUB, zk, zk)
        vtt(z(18), z(7), z(17), MUL, zk, zk)
        for ri, src in ((0, c_b_re), (1, c_b_im)):
            v = src[o_].rearrange("(i a) p c -> (a p) i c", a=2)
            for i0 in range(0, 16, 4):
                fw.dma('sp', BRI[:, ri, i0:i0 + 4, :], v[:, i0:i0 + 4, :], writes=[f'BRI{ri}{i0}'], allow_slow_non_contiguous=True)
        brk = [f'BRI{ri}{i0}' for ri in range(2) for i0 in range(0, 16, 4)]
        crb = z(19).unsqueeze(2).to_broadcast([128, 16, 16])
        cib = z(18).unsqueeze(2).to_broadcast([128, 16, 16])
        Vop(lambda e: e.memset(XB[:], 0.0), [], ['XB'])
        for ri in range(2):
            if ri == 0:
                vtt(T1[:], BRI[:, 0], crb, MUL, brk + zk, ['T1'])
                vtt(T2[:], BRI[:, 1], cib, MUL, brk + zk, ['T2'])
                vtt(T1[:], T1[:], T2[:], SUB, ['T1', 'T2'], ['T1'])
            else:
                vtt(T1[:], BRI[:, 1], crb, MUL, brk + zk + ['XB'], ['T1'])
                vtt(T2[:], BRI[:, 0], cib, MUL, brk + zk, ['T2'])
                vtt(T1[:], T1[:], T2[:], ADD, ['T1', 'T2'], ['T1'])
            vcopy(XB[0:64, ri, :, 0:16], T1[0:64], ['T1'], ['XB'])
            vcopy(XB[64:128, ri, :, 16:32], T1[64:128], ['T1'], ['XB'])
        for ri in range(2):
            for i0 in range(0, 16, 4):
                bi = 4 + (i0 // 4) % 2
                for i in range(i0, i0 + 4):
                    tpose(banks[bi][0:32, (i % 4) * 128:(i % 4 + 1) * 128], XB[:, ri, i, :], ident[:], ['XB'], [f'bank{bi}'])
                acopy(BBT[0:32, ri, i0:i0 + 4, :], banks[bi][0:32, 0:512].rearrange("p (i q) -> p i q", i=4), [f'bank{bi}'], ['BBT'])
        for ri, src in ((0, c_c_re), (1, c_c_im)):
            for ct in range(4):
                Vop(lambda e: e.memset(ZC[:], 0.0), [], ['ZC'])
                for gl in range(8):
                    a = gl % 2
                    fw.dma('sp', ZC[gl * 16:(gl + 1) * 16, 64 * a:64 * a + 64], src[o_, 8 * ct + gl], reads=['ZC'], writes=[f'ZCd{gl}'])
                tpose(banks[4][:, 0:128], ZC[:], ident[:], [f'ZCd{gl}' for gl in range(8)] + ['ZC'], ['bank4'])
                if ri == 0:
                    acopy(CZ[:, 0, ct, :], banks[4][:, 0:128], ['bank4'], ['CZ'])
                else:
                    Aop(lambda e, ct=ct: e.mul(out=CZ[:, 1, ct, :], in_=banks[4][:, 0:128], mul=-1.0), ['bank4'], ['CZ'])

        Vop(lambda e: e.memset(CT[:, :, 0:1], 1.0), [], ['CT'])
        Vop(lambda e: e.memset(ST[:, :, 0:1], 0.0), [], ['ST'])
        vcopy(PW[:, 0, :], z(14), zk, ['PW'])
        vcopy(PW[:, 1, :], z(13), zk, ['PW'])
        for L in range(7):
            n = 1 << L
            pc = PW[:, 0, :].unsqueeze(2).to_broadcast([128, 16, n])
            ps_ = PW[:, 1, :].unsqueeze(2).to_broadcast([128, 16, n])
            vtt(E1[:, 0:16 * n].rearrange("p (i t) -> p i t", t=n), CT[:, :, 0:n], pc, MUL, ['CT', 'PW'], ['E1'])
            vtt(E2[:, 0:16 * n].rearrange("p (i t) -> p i t", t=n), ST[:, :, 0:n], ps_, MUL, ['ST', 'PW'], ['E2'])
            vtt(CT[:, :, n:2 * n], E1[:, 0:16 * n].rearrange("p (i t) -> p i t", t=n), E2[:, 0:16 * n].rearrange("p (i t) -> p i t", t=n), SUB,
                ['E1', 'E2'], ['CT'])
            vtt(E1[:, 0:16 * n].rearrange("p (i t) -> p i t", t=n), CT[:, :, 0:n], ps_, MUL, ['CT', 'PW'], ['E1'])
            vtt(E2[:, 0:16 * n].rearrange("p (i t) -> p i t", t=n), ST[:, :, 0:n], pc, MUL, ['ST', 'PW'], ['E2'])
            vtt(ST[:, :, n:2 * n], E1[:, 0:16 * n].rearrange("p (i t) -> p i t", t=n), E2[:, 0:16 * n].rearrange("p (i t) -> p i t", t=n), ADD,
                ['E1', 'E2'], ['ST'])
            if L < 6:
                vtt(PT2[:, 0, :], PW[:, 0, :], PW[:, 0, :], MUL, ['PW'], ['PT2'])
                vtt(PT2[:, 1, :], PW[:, 1, :], PW[:, 1, :], MUL, ['PW'], ['PT2'])
                vtt(PT2[:, 2, :], PW[:, 0, :], PW[:, 1, :], MUL, ['PW'], ['PT2'])
                vtt(PW[:, 0, :], PT2[:, 0, :], PT2[:, 1, :], SUB, ['PT2'], ['PW'])
                vts(PW[:, 1, :], PT2[:, 2, :], 2.0, MUL, ['PT2'], ['PW'])
        Vop(lambda e: e.memset(xprev[:], 0.0), [], ['xprev'])
        RHOb = A("RHOb", [128, 16, 128])
        vcopy(RHOb[:], z(5).unsqueeze(2).to_broadcast([128, 16, 128]), zk, ['RHOb'])
        ur = z(14)
        ui = z(13)
        for j in range(NTILE):
            ts_ = slice(128 * j, 128 * j + 128)
            for c in range(4):
                fw.dma('sp', u32[:, 4 * c:4 * c + 4, :], projD[c, :, ts_].rearrange("(q r) n -> r q n", r=32), writes=[f'u32_{c}'])
            uk = [f'u32_{c}' for c in range(4)]
            vtt(PT2[:, 0, :], ur, xprev[:, 0, :], MUL, zk + ['xprev'], ['PT2'])
            vtt(PT2[:, 1, :], ui, xprev[:, 1, :], MUL, zk + ['xprev'], ['PT2'])
            vtt(zprev[:, 0, :], PT2[:, 0, :], PT2[:, 1, :], SUB, ['PT2'], ['zprev'])
            vtt(PT2[:, 0, :], ur, xprev[:, 1, :], MUL, zk + ['xprev'], ['PT2'])
            vtt(PT2[:, 1, :], ui, xprev[:, 0, :], MUL, zk + ['xprev'], ['PT2'])
            vtt(zprev[:, 1, :], PT2[:, 0, :], PT2[:, 1, :], ADD, ['PT2'], ['zprev'])
            for qd in range(4):
                i0 = 4 * qd
                cq = CT[:, i0:i0 + 4, :].rearrange("p i t -> p (i t)")
                sq_ = ST[:, i0:i0 + 4, :].rearrange("p i t -> p (i t)")
                for ri in range(2):
                    for ii in range(4):
                        Pop(lambda e, ri=ri, ii=ii, i0=i0: e.matmul(banks[4 + ri][:, ii * 128:(ii + 1) * 128], lhsT=BBT[0:32, ri, i0 + ii, :],
                                                                    rhs=u32[0:32, i0 + ii, :], start=True, stop=True), ['BBT'] + uk, [f'bank{4 + ri}'])
                vtt(E1[:, 0:512], cq, banks[4][:, 0:512], MUL, ['CT', 'bank4'], ['E1'])
                vtt(E2[:, 0:512], sq_, banks[5][:, 0:512], MUL, ['ST', 'bank5'], ['E2'])
                vtt(BU[:, 0, :], E1[:, 0:512], E2[:, 0:512], ADD, ['E1', 'E2'], ['BU'])
                vtt(E1[:, 0:512], cq, banks[5][:, 0:512], MUL, ['CT', 'bank5'], ['E1'])
                vtt(E2[:, 0:512], sq_, banks[4][:, 0:512], MUL, ['ST', 'bank4'], ['E2'])
                vtt(BU[:, 1, :], E1[:, 0:512], E2[:, 0:512], SUB, ['E1', 'E2'], ['BU'])
                for ii in range(4):
                    i = i0 + ii
                    for ri in range(2):
                        Vop(lambda e, ri=ri, ii=ii, i=i: e.tensor_tensor_scan(
                            out=ZZ[:, ri, i, :], data0=RHOb[:, i, :], data1=BU[:, ri, ii * 128:(ii + 1) * 128],
                            initial=zprev[:, ri, i:i + 1], op0=MUL, op1=ADD), ['BU', 'zprev', 'RHOb'], ['ZZ'])
                zr = ZZ[:, 0, i0:i0 + 4, :].rearrange("p i t -> p (i t)")
                zi_ = ZZ[:, 1, i0:i0 + 4, :].rearrange("p i t -> p (i t)")
                xr = XX[:, 0, i0:i0 + 4, :].rearrange("p i t -> p (i t)")
                xi_ = XX[:, 1, i0:i0 + 4, :].rearrange("p i t -> p (i t)")
                vtt(E1[:, 0:512], cq, zr, MUL, ['CT', 'ZZ'], ['E1'])
                vtt(E2[:, 0:512], sq_, zi_, MUL, ['ST', 'ZZ'], ['E2'])
                vtt(xr, E1[:, 0:512], E2[:, 0:512], SUB, ['E1', 'E2'], ['XX'])
                vtt(E1[:, 0:512], cq, zi_, MUL, ['CT', 'ZZ'], ['E1'])
                vtt(E2[:, 0:512], sq_, zr, MUL, ['ST', 'ZZ'], ['E2'])
                vtt(xi_, E1[:, 0:512], E2[:, 0:512], ADD, ['E1', 'E2'], ['XX'])
                for ii in range(4):
                    i = i0 + ii
                    ct, jq = i // 4, i % 4
                    Pop(lambda e, i=i, ct=ct, jq=jq, ii=ii: e.matmul(banks[6][0:32, ii * 128:(ii + 1) * 128], lhsT=CZ[:, 0, ct, 32 * jq:32 * jq + 32],
                                                                     rhs=XX[:, 0, i, :], start=True, stop=False), ['CZ', 'XX'], ['bank6'])
                    Pop(lambda e, i=i, ct=ct, jq=jq, ii=ii: e.matmul(banks[6][0:32, ii * 128:(ii + 1) * 128], lhsT=CZ[:, 1, ct, 32 * jq:32 * jq + 32],
                                                                     rhs=XX[:, 1, i, :], start=False, stop=True), ['CZ', 'XX'], ['bank6'])
                for ii in range(4):
                    i = i0 + ii
                    vstt(y32[0:32, i, :], u32[0:32, i, :], d32[0:32, i:i + 1], banks[6][0:32, ii * 128:(ii + 1) * 128], MUL, ADD,
                         uk + ['d32', 'bank6'], ['y32'])
            last = 15 if j == NTILE - 1 else 127
            vcopy(xprev[:, 0, :], XX[:, 0, :, last], ['XX'], ['xprev'])
            vcopy(xprev[:, 1, :], XX[:, 1, :, last], ['XX'], ['xprev'])
            vtt(G1[:], y32[:], y32[:], MUL, ['y32'], ['G1'])
            vts(G1[:], G1[:], 0.044715, MUL, ['G1'], ['G1'], 1.0, ADD)
            vtt(G1[:], G1[:], y32[:], MUL, ['G1', 'y32'], ['G1'])
            act(G1[:], G1[:], AF.Sigmoid, ['G1'], ['G1'], scale=2.0 * math.sqrt(2.0 / math.pi))
            vtt(y32[:], y32[:], G1[:], MUL, ['y32', 'G1'], ['y32'])
            for c in range(4):
                fw.dma('sp', mixD[c, :, ts_].rearrange("(q r) n -> r q n", r=32), y32[:, 4 * c:4 * c + 4, :], reads=['y32'], writes=['mixD'])
        vcopy(XF[:, 0:16], xprev[:, 0, :], ['xprev'], ['XF'])
        vcopy(XF[:, 16:32], xprev[:, 1, :], ['xprev'], ['XF'])
        tpose(banks[5][0:32, 0:128], XF[:], ident[:], ['XF'], ['bank5'])
        acopy(stgo[:], banks[5][0:32, 0:128], ['bank5'], ['stgo'])
        fw.dma('sp', pcs[o_], stgo[:], reads=['stgo'], writes=['pcs'])

    def rwkv_prompt(o_):
        CH = 256
        A = lambda n, shp, dt=F32: fw.sb(n + f"_rp{o_}", shp, dt)
        STG = A("STG", [48, 128]); DV = A("DV", [128, 42]); carry = A("carry", [128, 14, 1])
        dcF = A("dcF", [128, 14, CH]); XM = A("XM", [128, 14, CH]); Q5 = A("Q5", [128, 5, 4, CH])
        AA = A("AA", [128, 4, CH]); GG = A("GG", [128, 4, CH]); W1 = A("W1", [128, 4, CH]); W2 = A("W2", [128, 4, CH]); YF = A("YF", [128, 4, CH])
        BRs = A("BRs", [128, 4, CH]); KRs = A("KRs", [128, 4, CH])
        TL = A("TL", [128, CH], BF16); A16 = A("A16", [128, CH], BF16); SG = A("SG", [128, CH], BF16)
        S = A("S", [128, 4, 64]); RQ = [A(f"RQ{i}", [128, 5, 4, 64]) for i in range(2)]
        TMP2 = A("TMP2", [128, 2, 4, 64]); BIG = A("BIG", [128, CH, 3, 4]); sh14 = A("sh14", [128, 14]); sho = A("sho", [16, 128])
        dv = lambda r: DV[:, r:r + 1]
        state['nt'] = CH
        fw.dma('sp', STG[0:42, 0:128], dvecs[o_], writes=['STG'])
        tpose(banks[4][:, 0:42], STG[0:42, 0:128], ident[0:42, 0:42], ['STG'], ['bank4'])
        acopy(DV[:], banks[4][:, 0:42], ['bank4'], ['DV'])
        Vop(lambda e: e.memset(carry[:], 0.0), [], ['carry'])
        Vop(lambda e: e.memset(S[:], 0.0), [], ['S'])
        id2b5 = id2[:].unsqueeze(1).unsqueeze(1).to_broadcast([128, 5, 4, 64])
        flat = lambda tl: tl[:].rearrange("p k n -> p (k n)")

        def blksum(dst, src, scale=None, other=None):
            for hf in range(2):
                cs = slice(hf * 512, (hf + 1) * 512)
                Pop(lambda e, hf=hf, cs=cs: e.matmul(banks[6 + hf][:, 0:512], lhsT=blk[:], rhs=flat(src)[:, cs], start=True, stop=True),
                    ['blk', src_key[id(src)]], [f'bank{6 + hf}'])
                acopy(flat(dst)[:, cs], banks[6 + hf][:, 0:512], [f'bank{6 + hf}'], [src_key[id(dst)]])
        src_key = {id(W1): 'W1', id(W2): 'W2', id(YF): 'YF', id(BRs): 'BRs', id(KRs): 'KRs'}
        nchunk = (NVALID + CH - 1) // CH
        for c in range(nchunk):
            t0 = c * CH
            nst = min(CH, NVALID - t0)
            fw.dma('sp', dcF[:], projD[4:18, :, t0:t0 + CH].rearrange("k p n -> p k n"), writes=['dcF'])
            vcopy(XM[:, :, 1:CH], dcF[:, :, 0:CH - 1], ['dcF'], ['XM'])
            vcopy(XM[:, :, 0:1], carry[:], ['carry', 'XM'], ['XM'])
            vcopy(carry[:], dcF[:, :, CH - 1:CH], ['dcF', 'XM'], ['carry'])
            vtt(XM[:], XM[:], dcF[:], SUB, ['XM', 'dcF'], ['XM'])
            for k in range(14):
                vstt(XM[:, k, :], XM[:, k, :], dv(k), dcF[:, k, :], MUL, ADD, ['XM', 'DV', 'dcF'], ['XM'])
            act(TL[0:64, :], XM[0:64, 12, :], AF.Tanh, ['XM'], ['TL'])
            acopy(A16[64:128, :], XM[64:128, 12, :], ['XM'], ['A16'])
            act(SG[:], XM[:, 13, :], AF.Sigmoid, ['XM'], ['SG'])

            def cons_w(mt, msz, p, pkey):
                act(Q5[:, 4, mt, :], p, AF.Sigmoid, [pkey, 'DV'], ['Q5w'], bias=dv(14 + mt), scale=1.0)
            linear([(0, 64, 0, TL[0:64, :])], 'TL', d_w_w2[o_], 512, cons_w)
            act(Q5[:, 4], Q5[:, 4], AF.Exp, ['Q5w'], ['Q5w'], scale=-math.exp(-0.5))

            def cons_a(mt, msz, p, pkey):
                act(AA[:, mt, :], p, AF.Sigmoid, [pkey, 'DV'], ['AA'], bias=dv(18 + mt), scale=1.0)
            linear([(0, 64, 64, A16[64:128, :])], 'A16', d_w_a2[o_], 512, cons_a)
            linear([(0, 128, 0, SG[:, :])], 'SG', d_w_g2[o_], 512, to_tile(GG, 'GG'))
            for k in range(4):
                vts(W1[:, k, :], XM[:, 4 + k, :], dv(22 + k), MUL, ['XM', 'DV'], ['W1'])
            vtt(W2[:], W1[:], W1[:], MUL, ['W1'], ['W2'])
            blksum(YF, W2)
            act(flat(YF), flat(YF), AF.Sqrt, ['YF'], ['YF'])
            vts(YF[:], YF[:], 1e-12, ALU.max, ['YF'], ['YF'])
            Vop(lambda e: e.reciprocal(out=YF[:], in_=YF[:]), ['YF'], ['YF'])
            vtt(W1[:], W1[:], YF[:], MUL, ['W1', 'YF'], ['W1'])
            vts(Q5[:, 0], W1[:], -1.0, MUL, ['W1'], ['Q5a'])
            vtt(Q5[:, 2], W1[:], AA[:], MUL, ['W1', 'AA'], ['Q5b'])
            for k in range(4):
                vts(W2[:, k, :], AA[:, k, :], -1.0, ADD, ['AA', 'DV', 'W2'], ['W2'], dv(26 + k), MUL)
            vts(W2[:], W2[:], 1.0, ADD, ['W2'], ['W2'])
            vtt(Q5[:, 3], XM[:, 4:8, :], W2[:], MUL, ['XM', 'W2'], ['Q5b'])
            vtt(Q5[:, 1], Q5[:, 4], XM[:, 0:4, :], MUL, ['Q5w', 'XM'], ['Q5a'])
            vtt(W1[:], Q5[:, 2], XM[:, 0:4, :], MUL, ['Q5b', 'XM'], ['W1'])
            blksum(BRs, W1)
            vtt(W2[:], Q5[:, 3], XM[:, 0:4, :], MUL, ['Q5b', 'XM'], ['W2'])
            blksum(KRs, W2)
            vcopy(BIG[:, :, 1, :], XM[:, 8:12, :].rearrange("p h t -> p t h"), ['XM'], ['BIG'])
            qk = ['Q5a', 'Q5b', 'Q5w']
            for t in range(nst):
                pp = t % 2
                b0 = 3 * pp
                Vop(lambda e, pp=pp, t=t: e.tensor_tensor(out=RQ[pp][:], in0=id2b5, in1=Q5[:, :, :, t:t + 1].to_broadcast([128, 5, 4, 64]),
                                                          op=MUL), ['id2'] + qk, [f'RQ{pp}'])
                Pop(lambda e, pp=pp, b0=b0: e.matmul(banks[b0][:, 0:512], lhsT=blk[:], rhs=RQ[pp][:, 0:2].rearrange("p a h k -> p (a h k)"),
                                                     start=True, stop=True), ['blk', f'RQ{pp}'], [f'bank{b0}'])
                Pop(lambda e, pp=pp, b0=b0: e.matmul(banks[b0 + 1][:, 0:512], lhsT=blk[:], rhs=RQ[pp][:, 2:4].rearrange("p a h k -> p (a h k)"),
                                                     start=True, stop=True), ['blk', f'RQ{pp}'], [f'bank{b0 + 1}'])
                Pop(lambda e, pp=pp, b0=b0: e.matmul(banks[b0 + 2][:, 0:256], lhsT=blk[:], rhs=RQ[pp][:, 4].rearrange("p h k -> p (h k)"),
                                                     start=True, stop=True), ['blk', f'RQ{pp}'], [f'bank{b0 + 2}'])
                vtt(TMP2[:], S[:].unsqueeze(1).to_broadcast([128, 2, 4, 64]), banks[b0][:, 0:512].rearrange("p (a h k) -> p a h k", a=2, h=4), MUL,
                    ['S', f'bank{b0}'], ['TMP2'])
                Vop(lambda e, t=t: e.tensor_reduce(out=BIG[:, t, 0:3:2, :], in_=TMP2[:], axis=AX.X, op=ADD), ['TMP2'], ['BIG'])
                vtt(TMP2[:], banks[b0 + 1][:, 0:512].rearrange("p (a h k) -> p a h k", a=2, h=4),
                    BIG[:, t, 0:2, :].unsqueeze(3).to_broadcast([128, 2, 4, 64]), MUL, ['BIG', f'bank{b0 + 1}'], ['TMP2'])
                vtt(S[:], S[:], banks[b0 + 2][:, 0:256].rearrange("p (h k) -> p h k", h=4), MUL, ['S', f'bank{b0 + 2}'], ['S'])
                vtt(S[:], S[:], TMP2[:, 0], ADD, ['S', 'TMP2'], ['S'])
                vtt(S[:], S[:], TMP2[:, 1], ADD, ['S', 'TMP2'], ['S'])
            vtt(W1[:], BIG[:, :, 0, :].rearrange("p t h -> p h t"), BRs[:], MUL, ['BIG', 'BRs'], ['W1'])
            vtt(W2[:], BIG[:, :, 1, :].rearrange("p t h -> p h t"), KRs[:], MUL, ['BIG', 'KRs'], ['W2'])
            vtt(YF[:], BIG[:, :, 2, :].rearrange("p t h -> p h t"), W1[:], ADD, ['BIG', 'W1'], ['YF'])
            vtt(YF[:], YF[:], W2[:], ADD, ['YF', 'W2'], ['YF'])
            blksum(W1, YF)
            vstt(flat(W1), flat(W1), -1.0 / 64, flat(YF), MUL, ADD, ['W1', 'YF'], ['W1'])
            vtt(W2[:], W1[:], W1[:], MUL, ['W1'], ['W2'])
            blksum(YF, W2)
            act(flat(YF), flat(YF), AF.Sqrt, ['YF', 'gneps'], ['YF'], scale=1.0 / 64, bias=gneps[:])
            Vop(lambda e: e.reciprocal(out=YF[:], in_=YF[:]), ['YF'], ['YF'])
            vtt(W1[:], W1[:], YF[:], MUL, ['W1', 'YF'], ['W1'])
            for k in range(4):
                vts(W1[:, k, :], W1[:, k, :], dv(34 + k), MUL, ['W1', 'DV'], ['W1'], dv(38 + k), ADD)
            vtt(W2[:], XM[:, 0:4, :], Q5[:, 3], MUL, ['XM', 'Q5b'], ['W2'])
            for k in range(4):
                vts(W2[:, k, :], W2[:, k, :], dv(30 + k), MUL, ['W2', 'DV'], ['W2'])
            blksum(YF, W2)
            vtt(YF[:], YF[:], XM[:, 8:12, :], MUL, ['YF', 'XM'], ['YF'])
            vtt(W1[:], W1[:], YF[:], ADD, ['W1', 'YF'], ['W1'])
            vtt(W1[:], W1[:], GG[:], MUL, ['W1', 'GG'], ['W1'])
            fw.dma('sp', mixD[4:8, :, t0:t0 + CH].rearrange("k p n -> p k n"), W1[:], reads=['W1'], writes=['mixD'])
            if c == nchunk - 1:
                vcopy(sh14[:], dcF[:, :, nst - 1], ['dcF'], ['sh14'])
        tpose(banks[4][0:14, 0:128], sh14[:], ident[:], ['sh14'], ['bank4'])
        acopy(sho[0:14, :], banks[4][0:14, 0:128], ['bank4'], ['sho'])
        fw.dma('sp', pdsh[o_].rearrange("(k p) -> k p", p=128), sho[0:14, :], reads=['sho'], writes=['pdsh'])
        for hl in range(2):
            fw.dma('sp', pds[o_].rearrange("(hh hl) v k -> hl v hh k", hl=2)[hl], S[64 * hl:64 * hl + 64, :, :], reads=['S'], writes=[f'pds{hl}'])
        state['nt'] = NTOK

    EV_NAMES = ("kc kT vst Vaug PT ktok vtok bvtok botok lq_bc lamc qb16 LI LF BR MT nig subs_bc xcF qkF cacc cwb stg "
                "tmp12 Caug m0bc igfg_bc bnorm_bc mrow WK CL").split()
    for layer in range(0 if SKIP_SAMPLE else DEPTH):
        rmsnorm(layer, hF, 'hF')
        if layer % 2 == 0:
            e_ = layer // 2
            with fw.scope():
                tl = alloc_even(f"_{layer}")
                (kc, kT, vst, Vaug, PT, ktok, vtok, bvtok, botok, lq_bc, lamc, qb16, LI, LF, BR, MT, nig, subs_bc, xcF, qkF,
                 cacc, cwb, stg, tmp12, Caug, m0bc, igfg_bc, bnorm_bc, mrow, WK, CL) = [tl[n] for n in EV_NAMES]
                fw.op('dve', lambda e: e.memset(Vaug[:], 1.0), writes=['Vaug'])
                fw.op('dve', lambda e: e.memset(bvtok[:], 1.0), writes=['bvtok'])
                linear(std(hF, 8), 'hF', ev_w_in[e_], EV_COLS, to_tile(projF, 'projF'), tok_groups=even_tok_groups(e_))
                attention(e_, layer)
                mlstm(e_)
                if layer == 0:
                    Vop(lambda e: e.tensor_copy(out=sq[:], in_=mixF[:]), ['mixF'], ['sq'])
                    fw.dma('sp', dbg, sq[:].rearrange("p k n -> p (k n)"), reads=['sq'], writes=['dbg'])
                linear(std(mixF, 8), 'mixF', ev_w_out[e_], D, add_resid)
        else:
            o_ = layer // 2
            with fw.scope():
                oc32 = fw.sb(f"oc32_{layer}", [32, 16, NTOK], BF16)
                with fw.scope():
                    s5(o_, f"_{layer}", oc32)
                with fw.scope():
                    rwkv(o_, f"_{layer}")
                chunks = [(i * 32, 32, 0, oc32[0:32, i, :]) for i in range(16)] + \
                         [(512 + j * 128, 128, 0, mixF[:, 4 + j, :]) for j in range(4)]
                linear(chunks, ('oc32', 'mixF'), od_w_out[o_], D, add_resid)
        rmsnorm(4 + layer, hF, 'hF')
        linear(std(hF, 8), 'hF', ffn_w1[layer], FFN, to_tile(projF, 'projF'))
        fw.op('act', lambda e: e.activation(out=projF[:, 0:22, :], in_=projF[:, 0:22, :], func=AF.Silu), reads=['projF'], writes=['projF'])

        def gate(mt, msz, p, pkey):
            fw.op('dve', lambda e: e.tensor_tensor(out=gF[0:msz, mt, :], in0=projF[0:msz, mt, :], in1=p, op=ALU.mult),
                  reads=[pkey, 'projF'], writes=['gF'])
        linear(std(hF, 8), 'hF', ffn_w3[layer], FFN, gate)
        linear(std(gF, 22), 'gF', ffn_w2[layer], D, add_resid)

    rmsnorm(8, sq, 'sq2')
    for k in range(8):
        b = banks[k % 2]
        fw.op('pe', lambda e, k=k, b=b: e.transpose(b[:, 0:128], sq[:, k, :], ident[:]),
              reads=['sq2', 'ident'], writes=[f'bank{k % 2}'])
        fw.op('act', lambda e, k=k, b=b: e.copy(out=xtok[:, k * 128:(k + 1) * 128], in_=b[:, 0:128]),
              reads=[f'bank{k % 2}'], writes=['xtok'])
    fw.dma('sp', ys, xtok[:], reads=['xtok'], writes=['ys'])
    sc_sample.__exit__(None, None, None)

    sc_prompt = fw.scope()
    sc_prompt.__enter__()
    negc = fw.sb("negc", [128, 128]); bkp = fw.sb("bkp", [128, 384]); mskp = fw.sb("mskp", [128, 384])
    ADall = fw.sb("ADall", [128, 4, 384]); adtmp = fw.sb("adtmp", [128, 384])
    ones16 = fw.sb("ones16", [128, 128], BF16); zeros16 = fw.sb("zeros16", [128, 128], BF16)
    stage = [fw.sb(f"stage{i}", [128, 512]) for i in range(3)]
    fw.dma('sp', negc[:], negc_in, writes=['negc'])
    fw.dma('sp', bkp[:], bkp_in, writes=['bkp'])
    fw.dma('sp', mskp[:], mskp_in, writes=['mskp'])
    Vop(lambda e: e.memset(ones16[:], 1.0), [], ['ones16'])
    Vop(lambda e: e.memset(zeros16[:], 0.0), [], ['zeros16'])
    for h in range(4):
        vcopy(ADall[:, h, :], mskp[:], ['mskp'], ['ADall'])
        for bkt in range(32):
            vts(adtmp[:], bkp[:], float(bkt), ALU.is_equal, ['bkp', 'rb_bc'], ['adtmp'], rb_bc[:, bkt * 4 + h:bkt * 4 + h + 1], MUL)
            vtt(ADall[:, h, :], ADall[:, h, :], adtmp[:], ADD, ['adtmp', 'ADall'], ['ADall'])
    if TP > NTILE * 128:
        Vop(lambda e: e.memset(stage[0][:], 0.0), [], ['stage0'])
        for k in range(8):
            fw.dma('sp', mixD[k, :, NTILE * 128:TP], stage[0][:, 0:TP - NTILE * 128], reads=['stage0'], writes=['mixD'])
    stg_i = [0]

    def to_dram(t0):
        def c(mt, msz, p, pkey):
            si = stg_i[0] % 3
            stg_i[0] += 1
            acopy(stage[si][0:msz, :], p, [pkey], [f'stage{si}'])
            fw.dma('sp', projD[mt, 0:msz, t0:t0 + 512], stage[si][0:msz, :], reads=[f'stage{si}'], writes=['projD'])
        return c

    def dense_phase(layer):
        nonlocal xF, sq, rstd, hF, mixF, gF, xtok
        state['nt'] = 512
        with fw.scope():
            _, xF, sq, rstd, hF, _, mixF, gF = alloc_dense(512, f"_p{layer}", False)
            xtok = fw.sb(f"xtokp{layer}", [128, D])
            for gi in range(NGRP):
                t0 = gi * 512
                src = xp0 if layer == 0 else xD
                fw.dma('sp', xF[:], src[:, :, t0:t0 + 512].rearrange("k p n -> p k n"), writes=['xF'])
                if layer > 0:
                    pl = layer - 1
                    fw.dma('sp', sq[:], mixD[:, :, t0:t0 + 512].rearrange("k p n -> p k n"), writes=['sq'])
                    vcopy(mixF[:], sq[:], ['sq'], ['mixF'])
                    if pl % 2 == 1:
                        def cons_glu(mt, msz, p, pkey):
                            act(stage[2][:, :], p, AF.Sigmoid, [pkey], ['stage2'])
                            vtt(gF[:, mt, :], sq[:, mt, :], stage[2][:, :], MUL, ['sq', 'stage2'], ['gF'])
                        linear(std(mixF, 4), 'mixF', c_w_glu[pl // 2], 512, cons_glu)
                        vcopy(mixF[:, 0:4, :], gF[:, 0:4, :], ['gF'], ['mixF'])
                        linear(std(mixF, 8), 'mixF', od_w_out[pl // 2], D, add_resid)
                    else:
                        linear(std(mixF, 8), 'mixF', ev_w_out[pl // 2], D, add_resid)
                    rmsnorm(4 + pl, hF, 'hF')

                    def cons_silu(mt, msz, p, pkey):
                        act(gF[0:msz, mt, :], p, AF.Silu, [pkey], ['gF'])

                    def cons_gate(mt, msz, p, pkey):
                        vtt(gF[0:msz, mt, :], gF[0:msz, mt, :], p, MUL, [pkey, 'gF'], ['gF'])
                    linear(std(hF, 8), 'hF', ffn_w1[pl], FFN, cons_silu)
                    linear(std(hF, 8), 'hF', ffn_w3[pl], FFN, cons_gate)
                    linear(std(gF, 22), 'gF', ffn_w2[pl], D, add_resid)
                if layer < DEPTH:
                    rmsnorm(layer, hF, 'hF')
                    W = ev_w_in[layer // 2] if layer % 2 == 0 else od_w_in[layer // 2]
                    linear(std(hF, 8), 'hF', W, EV_COLS if layer % 2 == 0 else OD_COLS, to_dram(t0))
                    fw.dma('sp', xD[:, :, t0:t0 + 512].rearrange("k p n -> p k n"), xF[:], reads=['xF'], writes=['xD'])
                else:
                    rmsnorm(8, sq, 'sq2')
                    for jj in range(4):
                        j = gi * 4 + jj
                        if j >= NTILE:
                            break
                        for k in range(8):
                            b = banks[4 + k % 2]
                            tpose(b[:, 0:128], sq[:, k, jj * 128:(jj + 1) * 128], ident[:], ['sq2'], [f'bank{4 + k % 2}'])
                            acopy(xtok[:, k * 128:(k + 1) * 128], b[:, 0:128], [f'bank{4 + k % 2}'], ['xtok'])
                        r0 = 16 if j == 0 else 0
                        r1 = 16 if j == NTILE - 1 else 128
                        fw.dma('sp', yp[128 * j - 16 + r0:128 * j - 16 + r1, :], xtok[r0:r1, :], reads=['xtok'], writes=['yp'])
        state['nt'] = NTOK

    def mlstm_prompt(e_):
        A = lambda n, shp, dt=F32: fw.sb(n + f"_mp{e_}", shp, dt)
        g8 = A("g8", [8, 128]); xcP = A("xcP", [128, 8, 131]); cacc = A("cacc", [128, 8, 128]); qkF = A("qkF", [128, 8, 128])
        cwb = A("cwb", [128, 8, 5]); stg = A("stg", [16, 1024]); vfm = A("vfm", [128, 8, 128])
        bvtok = A("bvtok", [128, 4, 129]); botok = A("botok", [128, 512])
        LI = A("LI", [128, 4, 128]); LF = A("LF", [128, 4, 128]); BR = A("BR", [128, 4, 128]); MT = A("MT", [128, 4, 128])
        Cst = A("Cst", [128, 4, 129]); mcar = A("mcar", [128, 4]); igfg_bc = A("igfg", [128, 8]); nig = A("nig", [128, 8])
        bnorm_bc = A("bnorm", [128, 512]); mrow = A("mrow", [1, 4]); hst = A("hst", [128, 128]); tmp3 = A("tmp3", [128, 8, 3])
        WK = [A(f"wk{i}", [128, 129]) for i in range(14)]
        CL = A("cl", [128, 24])

        def dg(src_ap, src_key, col_ap):
            Vop(lambda e: e.scalar_tensor_tensor(out=WK[11][:, 0:128], in0=src_ap, scalar=1.0, in1=ident[:], op0=MUL, op1=MUL,
                                                 accum_out=col_ap), [src_key, 'ident'], ['wk11', 'cl'])
        fw.dma('sp', stg[0:5, :], convwb[e_], writes=['stg'])
        for k in range(8):
            tpose(banks[4][:, k * 8:k * 8 + 5], stg[0:5, k * 128:(k + 1) * 128], ident[0:5, 0:5], ['stg'], ['bank4'])
        for k in range(8):
            acopy(cwb[:, k, :], banks[4][:, k * 8:k * 8 + 5], ['bank4'], ['cwb'])
        fw.dma('sp', igfg_bc[:], igfg[e_].partition_broadcast(128), writes=['igfg_bc'])
        fw.dma('sp', bnorm_bc[:], b_norm[e_].partition_broadcast(128), writes=['bnorm_bc'])
        vts(nig[:], igfg_bc[:], -1.0, MUL, ['igfg_bc'], ['nig'])
        Vop(lambda e: e.memset(Cst[:], 0.0), [], ['Cst'])
        Vop(lambda e: e.memset(mcar[:], 0.0), [], ['mcar'])
        Vop(lambda e: e.memset(xcP[:], 0.0), [], ['xcP'])
        Vop(lambda e: e.memset(bvtok[:], 1.0), [], ['bvtok'])
        for j in range(NTILE):
            t0 = 128 * j
            ts_ = slice(t0, t0 + 128)
            fw.dma('sp', xcP[:, :, 3:131], projD[12:20, :, ts_].rearrange("k p n -> p k n"), writes=['xcP'])
            fw.dma('sp', g8[:], projD[28, 0:8, ts_], writes=['g8'])
            fw.dma('sp', vfm[:], projD[20:28, :, ts_].rearrange("k p n -> p k n"), writes=['vfm'])
            for h in range(4):
                tpose(banks[6][:, h * 128:(h + 1) * 128], vfm[:, h, :], ident[:], ['vfm'], ['bank6'])
            acopy(bvtok[:, :, 0:128], banks[6][:, 0:512].rearrange("p (h v) -> p h v", v=128), ['bank6'], ['bvtok'])
            for h in range(4):
                tpose(banks[6][:, h * 128:(h + 1) * 128], vfm[:, 4 + h, :], ident[:], ['vfm'], ['bank6'])
            acopy(botok[:], banks[6][:, 0:512], ['bank6'], ['botok'])
            for k in range(8):
                vts(cacc[:, k, :], xcP[:, k, 0:128], cwb[:, k, 0:1], MUL, ['xcP', 'cwb'], ['cacc'], cwb[:, k, 4:5], ADD)
                for jj in range(1, 4):
                    vstt(cacc[:, k, :], xcP[:, k, jj:jj + 128], cwb[:, k, jj:jj + 1], cacc[:, k, :], MUL, ADD, ['xcP', 'cwb', 'cacc'], ['cacc'])
            act(qkF[:], cacc[:], AF.Silu, ['cacc'], ['qkF'])
            Aop(lambda e: e.mul(out=qkF[:, 0:4, :], in_=qkF[:, 0:4, :], mul=128 ** -0.5), ['qkF'], ['qkF'])
            if j == NTILE - 1:
                vcopy(tmp3[:], xcP[:, :, 16:19], ['xcP'], ['tmp3'])
            vcopy(xcP[:, :, 0:3], xcP[:, :, 128:131], ['xcP', 'cacc'], ['xcP'])
            for q in range(8):
                bi = 4 + q // 4
                Pop(lambda e, q=q, bi=bi: e.matmul(banks[bi][:, (q % 4) * 128:(q % 4 + 1) * 128], lhsT=selr[0:8, q * 128:(q + 1) * 128],
                                                   rhs=g8[0:8, :], start=True, stop=True), ['selr', 'g8'], [f'bank{bi}'])
            for h in range(4):
                vts(LI[:, h, :], banks[4][:, h * 128:(h + 1) * 128], igfg_bc[:, h:h + 1], ADD, ['bank4', 'igfg_bc'], ['LI'])
                act(LF[:, h, :], banks[5][:, h * 128:(h + 1) * 128], AF.Exp, ['bank5', 'nig'], ['LF'], bias=nig[:, 4 + h:5 + h], scale=-1.0)
            act(LF[:], LF[:], AF.Ln, ['LF', 'onec'], ['LF'], bias=onec[:], scale=1.0)
            vts(LF[:], LF[:], -1.0, MUL, ['LF'], ['LF'])
            if j == NTILE - 1:
                Vop(lambda e: e.memset(LI[:, :, 16:128], -1e30), ['LI'], ['LI'])
                Vop(lambda e: e.memset(LF[:, :, 16:128], 0.0), ['LF'], ['LF'])
            for h in range(4):
                Vop(lambda e, h=h: e.tensor_tensor_scan(out=BR[:, h, :], data0=ones[:], data1=LF[:, h, :], initial=0.0,
                                                        op0=MUL, op1=ADD), ['ones', 'LF'], ['BR'])
                Vop(lambda e, h=h: e.tensor_tensor_scan(out=MT[:, h, :], data0=LF[:, h, :], data1=LI[:, h, :], initial=mcar[:, h:h + 1],
                                                        op0=ADD, op1=ALU.max), ['LF', 'LI', 'mcar'], ['MT'])
            for h in range(4):
                hs = slice(h * 128, (h + 1) * 128)
                Pop(lambda e, h=h: e.matmul(banks[4][:, 0:128], lhsT=qkF[:, 4 + h, :], rhs=qkF[:, h, :], start=True, stop=True), ['qkF'], ['bank4'])
                vtt(WK[5][:, 0:128], BR[:, h, :], MT[:, h, :], SUB, ['BR', 'MT'], ['wk5'])
                vtt(WK[5][:, 0:128], WK[5][:, 0:128], negc[:], ADD, ['wk5', 'negc'], ['wk5'])
                vtt(WK[6][:, 0:128], LI[:, h, :], BR[:, h, :], SUB, ['LI', 'BR'], ['wk6'])
                dg(WK[6][:, 0:128], 'wk6', CL[:, 8:9])
                act(WK[5][:, 0:128], WK[5][:, 0:128], AF.Exp, ['wk5', 'cl'], ['wk5'], bias=CL[:, 8:9], scale=1.0)
                vtt(WK[6][:, 0:128], banks[4][:, 0:128], WK[5][:, 0:128], MUL, ['bank4', 'wk5', 'wk6'], ['wk6'])
                Pop(lambda e, h=h: e.matmul(banks[5][:, 0:129], lhsT=WK[6][:, 0:128], rhs=bvtok[:, h, :], start=True, stop=True), ['wk6', 'bvtok'], ['bank5'])
                Pop(lambda e, h=h: e.matmul(banks[6][:, 0:129], lhsT=qkF[:, h, :], rhs=Cst[:, h, :], start=True, stop=True), ['qkF', 'Cst'], ['bank6'])
                acopy(WK[7][:, 0:129], banks[6][:, 0:129], ['bank6'], ['wk7'])
                vstt(WK[8][:, 0:128], BR[:, h, :], mcar[:, h:h + 1], MT[:, h, :], ADD, SUB, ['BR', 'MT', 'mcar'], ['wk8'])
                act(WK[8][:, 0:128], WK[8][:, 0:128], AF.Exp, ['wk8'], ['wk8'])
                dg(WK[8][:, 0:128], 'wk8', CL[:, 9:10])
                vstt(WK[9][:, 0:129], WK[7][:, 0:129], CL[:, 9:10], banks[5][:, 0:129], MUL, ADD, ['wk7', 'cl', 'bank5'], ['wk9'])
                vts(CL[:, 20:21], WK[9][:, 128:129], -1.0, MUL, ['wk9'], ['cl'])
                vtt(CL[:, 10:11], CL[:, 20:21], WK[9][:, 128:129], ALU.max, ['wk9', 'cl'], ['cl'])
                dg(MT[:, h, :], 'MT', CL[:, 11:12])
                act(CL[:, 12:13], CL[:, 11:12], AF.Exp, ['cl'], ['cl'], scale=-1.0)
                vtt(CL[:, 13:14], CL[:, 10:11], CL[:, 12:13], ALU.max, ['cl'], ['cl'])
                Vop(lambda e: e.reciprocal(out=CL[:, 14:15], in_=CL[:, 13:14]), ['cl'], ['cl'])
                vts(WK[10][:, 0:128], WK[9][:, 0:128], CL[:, 14:15], MUL, ['wk9', 'cl'], ['wk10'])
                Vop(lambda e: e.scalar_tensor_tensor(out=WK[11][:, 0:128], in0=WK[10][:, 0:128], scalar=1.0, in1=WK[10][:, 0:128],
                                                     op0=MUL, op1=MUL, accum_out=CL[:, 15:16]), ['wk10'], ['wk11', 'cl'])
                act(CL[:, 16:17], CL[:, 15:16], AF.Sqrt, ['cl', 'epsc'], ['cl'], scale=1.0 / 128, bias=epsc[:])
                Vop(lambda e: e.reciprocal(out=CL[:, 17:18], in_=CL[:, 16:17]), ['cl'], ['cl'])
                act(WK[11][:, 0:128], botok[:, hs], AF.Sigmoid, ['botok', 'wk11'], ['wk11'])
                vstt(WK[10][:, 0:128], WK[10][:, 0:128], CL[:, 17:18], bnorm_bc[:, hs], MUL, MUL, ['wk10', 'cl', 'bnorm_bc'], ['wk10'])
                vtt(WK[10][:, 0:128], WK[10][:, 0:128], WK[11][:, 0:128], MUL, ['wk10', 'wk11'], ['wk10'])
                tpose(banks[5][:, 256:384], WK[10][:, 0:128], ident[:], ['wk10'], ['bank5'])
                acopy(hst[:], banks[5][:, 256:384], ['bank5'], ['hst'])
                fw.dma('sp', mixD[4 + h, :, ts_], hst[:], reads=['hst'], writes=['mixD'])
                tpose(banks[4][:, 128:256], qkF[:, 4 + h, :], ident[:], ['qkF'], ['bank4'])
                acopy(WK[12][:, 0:128], banks[4][:, 128:256], ['bank4'], ['wk12'])
                vtt(WK[13][:, 0:128], LI[:, h, :], BR[:, h, :], SUB, ['LI', 'BR'], ['wk13'])
                vts(WK[13][:, 0:128], WK[13][:, 0:128], BR[:, h, 127:128], ADD, ['wk13', 'BR', 'MT'], ['wk13'], MT[:, h, 127:128], SUB)
                act(WK[13][:, 0:128], WK[13][:, 0:128], AF.Exp, ['wk13'], ['wk13'])
                dg(WK[13][:, 0:128], 'wk13', CL[:, 18:19])
                vts(WK[12][:, 0:128], WK[12][:, 0:128], CL[:, 18:19], MUL, ['wk12', 'cl'], ['wk12'])
                Pop(lambda e, h=h: e.matmul(banks[5][:, 0:129], lhsT=WK[12][:, 0:128], rhs=bvtok[:, h, :], start=True, stop=True), ['wk12', 'bvtok'], ['bank5'])
                vstt(CL[:, 19:20], BR[:, h, 127:128], mcar[:, h:h + 1], MT[:, h, 127:128], ADD, SUB, ['BR', 'MT', 'mcar'], ['cl'])
                act(CL[:, 19:20], CL[:, 19:20], AF.Exp, ['cl'], ['cl'])
                vstt(Cst[:, h, :], Cst[:, h, :], CL[:, 19:20], banks[5][:, 0:129], MUL, ADD, ['Cst', 'cl', 'bank5'], ['Cst'])
                vcopy(mcar[:, h:h + 1], MT[:, h, 127:128], ['MT', 'mcar'], ['mcar'])
        for h in range(4):
            fw.dma('sp', pbc[e_, h], Cst[:, h, 0:128], reads=['Cst'], writes=[f'pbc{h}'])
            fw.dma('sp', pbn[e_, h].rearrange("(k o) -> k o", o=1), Cst[:, h, 128:129], reads=['Cst'], writes=[f'pbn{h}'])
        acopy(mrow[0:1, :], mcar[0:1, :], ['mcar'], ['mrow'])
        fw.dma('sp', pbm[e_].rearrange("(o k) -> o k", o=1), mrow[0:1, :], reads=['mrow'], writes=['pbm'])
        for k in range(8):
            bi = 4 + k // 4
            tpose(banks[bi][0:3, (k % 4) * 128:(k % 4 + 1) * 128], tmp3[:, k, :], ident[:], ['tmp3'], [f'bank{bi}'])
        acopy(stg[0:3, 0:512], banks[4][0:3, 0:512], ['bank4'], ['stg'])
        acopy(stg[0:3, 512:1024], banks[5][0:3, 0:512], ['bank5'], ['stg'])
        fw.dma('sp', pbconv[e_], stg[0:3, :], reads=['stg'], writes=['pbconv'])

    def attention_prompt(e_, layer):
        lam_init = LAM_INIT[layer]
        A = lambda n, shp, dt=F32: fw.sb(n + f"_ap{e_}", shp, dt)
        kT16 = A("kT16", [128, NTILE * 128], BF16); Vt16 = A("Vt16", [128, NTILE, 128], BF16)
        st32 = A("st32", [128, 512]); q16 = A("q16", [128, 512], BF16); PTp = A("PTp", [128, 2, 512], BF16)
        tk = [A(f"tk{i}", [128, 128]) for i in range(2)]
        O1 = A("O1", [128, 512]); O2 = A("O2", [128, 512]); O3 = A("O3", [128, 512]); ntmp = A("ntmp", [128, 128])
        lq_bc = A("lq_bc", [128, 256]); lamc = A("lamc", [128, 4]); subc = A("subc", [128, 1]); w0 = A("w0", [128, 64])
        fw.dma('sp', lq_bc[:], lqk[e_].rearrange("a d -> (a d)").partition_broadcast(128), writes=['lq_bc'])
        fw.dma('sp', subc[:], a_subln[e_].rearrange("(k o) -> k o", o=1), writes=['subc'])
        Vop(lambda e: e.scalar_tensor_tensor(out=w0[:], in0=lq_bc[:, 0:64], scalar=1.0, in1=lq_bc[:, 64:128], op0=MUL, op1=MUL,
                                             accum_out=lamc[:, 0:1]), ['lq_bc'], ['w0', 'lamc'])
        Vop(lambda e: e.scalar_tensor_tensor(out=w0[:], in0=lq_bc[:, 128:192], scalar=1.0, in1=lq_bc[:, 192:256], op0=MUL, op1=MUL,
                                             accum_out=lamc[:, 1:2]), ['lq_bc', 'w0'], ['w0', 'lamc'])
        act(lamc[:, 0:2], lamc[:, 0:2], AF.Exp, ['lamc'], ['lamc'])
        vtt(lamc[:, 2:3], lamc[:, 0:1], lamc[:, 1:2], SUB, ['lamc'], ['lamc'])
        vts(lamc[:, 3:4], lamc[:, 2:3], -1.0, MUL, ['lamc'], ['lamc'], -lam_init, ADD)
        vts(subc[:], subc[:], 1.0 - lam_init, MUL, ['subc'], ['subc'])
        tki = [0]
        for h in range(ATT_HEADS):
            hs = slice(h * 128, (h + 1) * 128)
            for which, ptile in ((0, 4 + h), (1, 8 + h)):
                if 'kv'[which] not in PREP:
                    continue
                for c in range(NGRP):
                    c0 = c * 512
                    w = min(512, NTILE * 128 - c0)
                    if w <= 0:
                        break
                    fw.dma('sp', st32[:, 0:w], projD[ptile, :, c0:c0 + w], writes=['st32'])
                    if which == 0:
                        vcopy(kT16[:, c0:c0 + w], st32[:, 0:w], ['st32'], ['kT16'])
                    for jj in range(w // 128):
                        j = c * 4 + jj
                        bi = 6 + jj % 2
                        tpose(banks[bi][:, 0:128], st32[:, jj * 128:(jj + 1) * 128], ident[:], ['st32'], [f'bank{bi}'])
                        ti = tki[0] % 2
                        tki[0] += 1
                        acopy(tk[ti][:], banks[bi][:, 0:128], [f'bank{bi}'], [f'tk{ti}'])
                        if which == 1:
                            vcopy(Vt16[:, j, :], tk[ti][:], [f'tk{ti}'], ['Vt16'])
                        nv = 16 if j == NTILE - 1 else 128
                        dst = pak if which == 0 else pav
                        fw.dma('sp', dst[e_, 128 * j:128 * j + nv, hs], tk[ti][0:nv, :], reads=[f'tk{ti}'], writes=[f'pakv{which}'])
            for g in range(NGRP):
                ntl = min(4, NTILE - 4 * g)
                if ntl <= 0 or ATT_MODE < 2:
                    break
                w = ntl * 128
                fw.dma('sp', st32[:, 0:w], projD[h, :, g * 512:g * 512 + w], writes=['st32'])
                vcopy(q16[:, 0:w], st32[:, 0:w], ['st32'], ['q16'])
                for m in range(2):
                    Pop(lambda e, m=m, w=w: e.matmul(banks[m][:, 0:w], lhsT=zeros16[:], rhs=q16[:, 0:w], start=True, stop=False),
                        ['zeros16', 'q16'], [f'bank{m}'])
                    Pop(lambda e, m=m, w=w: e.matmul(banks[2 + m][:, 0:w], lhsT=zeros16[:], rhs=q16[:, 0:w], start=True, stop=False),
                        ['zeros16', 'q16'], [f'bank{2 + m}'])

                def pv(m, kt, nk, cs, ncol):
                    Pop(lambda e: e.matmul(banks[m][:, cs], lhsT=Vt16[0:nk, kt, :], rhs=PTp[0:nk, m, 0:ncol], start=False, stop=False),
                        ['Vt16', 'PTp'], [f'bank{m}'])
                    Pop(lambda e: e.matmul(banks[2 + m][:, cs], lhsT=ones16[0:nk, :], rhs=PTp[0:nk, m, 0:ncol], start=False, stop=False),
                        ['ones16', 'PTp'], [f'bank{2 + m}'])
                for kt in range(0, 4 * g - 1 if ATT_MODE >= 3 else 0):
                    for m in range(2):
                        ms = slice(64 * m, 64 * m + 64)
                        sb_ = 4 + m + 2 * (kt % 2)
                        Pop(lambda e, m=m, ms=ms, kt=kt, sb_=sb_, w=w: e.matmul(banks[sb_][:, 0:w], lhsT=kT16[ms, kt * 128:(kt + 1) * 128],
                                                                                rhs=q16[ms, 0:w], start=True, stop=True), ['kT16', 'q16'], [f'bank{sb_}'])
                        act(PTp[:, m, 0:w], banks[sb_][:, 0:w], AF.Exp, [f'bank{sb_}', 'rb_bc'], ['PTp'], bias=rb_bc[:, 60 + h:61 + h], scale=0.125)
                        pv(m, kt, 128, slice(0, w), w)
                for jj in range(ntl if ATT_MODE >= 4 else 0):
                    i = 4 * g + jj
                    cs = slice(jj * 128, (jj + 1) * 128)
                    for kt in range(max(0, 4 * g - 1), min(i + 2, NTILE)):
                        nk = 16 if kt == i + 1 else 128
                        adi = None if kt <= i - 2 else (1 if kt == i - 1 else (0 if kt == i else 2))
                        for m in range(2):
                            ms = slice(64 * m, 64 * m + 64)
                            sb_ = 4 + m + 2 * (kt % 2)
                            Pop(lambda e, m=m, ms=ms, kt=kt, sb_=sb_, nk=nk, cs=cs: e.matmul(
                                banks[sb_][0:nk, 0:128], lhsT=kT16[ms, kt * 128:kt * 128 + nk], rhs=q16[ms, cs], start=True, stop=True),
                                ['kT16', 'q16'], [f'bank{sb_}'])
                            if adi is None:
                                act(PTp[0:nk, m, 0:128], banks[sb_][0:nk, 0:128], AF.Exp, [f'bank{sb_}', 'rb_bc'], ['PTp'],
                                    bias=rb_bc[0:nk, 60 + h:61 + h], scale=0.125)
                            else:
                                vstt(ntmp[0:nk, :], banks[sb_][0:nk, 0:128], 0.125, ADall[0:nk, h, adi * 128:(adi + 1) * 128], MUL, ADD,
                                     [f'bank{sb_}', 'ADall'], ['ntmp'])
                                act(PTp[0:nk, m, 0:128], ntmp[0:nk, :], AF.Exp, ['ntmp'], ['PTp'])
                            pv(m, kt, nk, cs, 128)
                for m in range(2):
                    Pop(lambda e, m=m, w=w: e.matmul(banks[m][:, 0:w], lhsT=zeros16[:], rhs=q16[:, 0:w], start=False, stop=True),
                        ['zeros16', 'q16'], [f'bank{m}'])
                    Pop(lambda e, m=m, w=w: e.matmul(banks[2 + m][:, 0:w], lhsT=zeros16[:], rhs=q16[:, 0:w], start=False, stop=True),
                        ['zeros16', 'q16'], [f'bank{2 + m}'])
                ws = slice(0, w)
                Vop(lambda e, ws=ws: e.reciprocal(out=O1[:, ws], in_=banks[2][:, ws]), ['bank2'], ['O1'])
                vtt(O2[:, ws], banks[0][:, ws], O1[:, ws], MUL, ['bank0', 'O1'], ['O2'])
                Vop(lambda e, ws=ws: e.reciprocal(out=O1[:, ws], in_=banks[3][:, ws]), ['bank3', 'O1'], ['O1'])
                vts(O1[:, ws], O1[:, ws], lamc[:, 3:4], MUL, ['O1', 'lamc'], ['O1'])
                vtt(O3[:, ws], banks[1][:, ws], O1[:, ws], MUL, ['bank1', 'O1'], ['O3'])
                vtt(O2[:, ws], O2[:, ws], O3[:, ws], ADD, ['O2', 'O3'], ['O2'])
                vtt(O1[:, ws], O2[:, ws], O2[:, ws], MUL, ['O2', 'O1'], ['O1'])
                Pop(lambda e, ws=ws: e.matmul(banks[7][:, ws], lhsT=ones[:], rhs=O1[:, ws], start=True, stop=True), ['ones', 'O1'], ['bank7'])
                act(O3[:, ws], banks[7][:, ws], AF.Sqrt, ['bank7', 'epsc', 'O3'], ['O3'], scale=1.0 / 128, bias=epsc[:])
                Vop(lambda e, ws=ws: e.reciprocal(out=O3[:, ws], in_=O3[:, ws]), ['O3'], ['O3'])
                vstt(O2[:, ws], O2[:, ws], subc[:, 0:1], O3[:, ws], MUL, MUL, ['O2', 'subc', 'O3'], ['O2'])
                fw.dma('sp', mixD[h, :, g * 512:g * 512 + w], O2[:, ws], reads=['O2'], writes=['mixD'])

    for layer in range(PROMPT_LAYERS + 1):
        if layer > DEPTH:
            break
        if layer == PROMPT_LAYERS and PROMPT_LAYERS < DEPTH:
            break
        dense_phase(layer)
        if layer < DEPTH:
            if layer % 2 == 0:
                if 'm' in P_STAGES:
                    with fw.scope():
                        mlstm_prompt(layer // 2)
                if 'a' in P_STAGES:
                    with fw.scope():
                        attention_prompt(layer // 2, layer)
            else:
                if 's' in P_STAGES:
                    with fw.scope():
                        s5_prompt(layer // 2)
                if 'o' in P_STAGES:
                    with fw.scope():
                        rwkv_prompt(layer // 2)
    if DBG:
        dbgmix = dout("dbgmix", [8, 128, TP])
        fw.barrier()
        for k in range(8):
            fw.dma('sp', dbgmix[k], mixD[k], writes=[f'dbgmix{k}'])
    sc_prompt.__exit__(None, None, None)
    fw.barrier()
    fw.close()
    return nc


_REF_SHAPES = None


def _out_shapes():
    N_EVEN, N_ODD, B, S, T = 2, 2, 32, 16384, 32
    p = [(1, S, D), (B, T, D),
         (N_EVEN, 1, 16 + S, 4, 128), (N_EVEN, 1, 16 + S, 4, 128), (N_EVEN, 1, 4, 128, 128), (N_EVEN, 1, 4, 128),
         (N_EVEN, 1, 4), (N_EVEN, 1, 3, 1024), (N_ODD, 1, 32, 64), (N_ODD, 1, 32, 64), (N_ODD, 1, 8, 64, 64),
         (N_ODD, 1, 1, 1792),
         (N_EVEN, B, T, 4, 128), (N_EVEN, B, T, 4, 128), (N_EVEN, B, 4, 128, 128), (N_EVEN, B, 4, 128),
         (N_EVEN, B, 4), (N_EVEN, B, 3, 1024), (N_ODD, B, 32, 64), (N_ODD, B, 32, 64), (N_ODD, B, 8, 64, 64),
         (N_ODD, B, 1, 1792)]
    return p


def _t5_bucket_np(rel):
    rel = np.asarray(rel, np.int64)
    nb, max_exact = 16, 8
    ret = np.where(rel > 0, nb, 0)
    n = np.abs(rel)
    nf = np.maximum(n, 1).astype(np.float32)
    large = max_exact + (np.log(nf / np.float32(max_exact)) / np.float32(math.log(128 / max_exact))
                         * np.float32(nb - max_exact)).astype(np.int32)
    large = np.minimum(large, nb - 1)
    return ret + np.where(n < max_exact, n, large)


def _consts():
    c = {}
    c["ident"] = np.eye(128, dtype=np.float32)
    n_ = np.arange(128)
    kl = np.arange(128)[:, None]
    qi = np.arange(32)[None, :]
    bkm = np.full((128, 96), -1.0, np.float32)
    bkm[:, 0:32] = _t5_bucket_np((1920 + kl) - (2064 + qi))
    bkm[0:16, 32:64] = _t5_bucket_np((2048 + kl[:16]) - (2064 + qi))
    bkm[0:32, 64:96] = _t5_bucket_np(kl[:32] - qi)
    c["bk"] = bkm
    c["negc"] = np.where(n_[None, :] >= n_[:, None], 0.0, -1e30).astype(np.float32)
    kr = n_[:, None]; qr = n_[None, :]
    c["bkp"] = np.concatenate([_t5_bucket_np(kr - qr), _t5_bucket_np(kr - 128 - qr), _t5_bucket_np(kr + 128 - qr)], axis=1).astype(np.float32)
    thr = np.where(qr < 16, 16, np.where(qr < 80, 80, 128))
    m0 = np.where(kr < thr, 0.0, -30000.0)
    m1 = np.where((kr < 16) & (qr >= 80), 0.0, -30000.0)
    c["mskp"] = np.concatenate([m0, np.zeros((128, 128)), m1], axis=1).astype(np.float32)
    c["blk"] = np.kron(np.eye(2, dtype=np.float32), np.ones((64, 64), np.float32))
    c["id2"] = np.concatenate([np.eye(64, dtype=np.float32)] * 2, axis=0)
    selr = np.zeros((8, 8, 128), np.float32)
    for j in range(8):
        selr[j, j, :] = 1.0
    c["selr"] = selr.reshape(8, 1024)
    n = np.arange(128)
    same = (n[:, None] // 32) == (n[None, :] // 32)
    c["negbd"] = np.where(same & (n[None, :] >= n[:, None]), 0.0, -1e30).astype(np.float32)
    c["bmask"] = ((n[:, None] // 32) == np.arange(4)[None, :]).astype(np.float32)
    c["scanmask"] = np.broadcast_to((n % 32 != 0).astype(np.float32)[None, :], (128, 128)).copy()
    return c


def kernel(**inputs):
    f = lambda a: np.ascontiguousarray(np.asarray(a, dtype=np.float32))
    nc = build_program()
    shared = {k: f(inputs[k]) for k in ("norm_mix", "norm_ffn", "ev_w_in", "ev_w_out", "od_w_in", "od_w_out",
                                        "ffn_w1", "ffn_w3", "ffn_w2", "a_subln", "b_norm")}
    shared["norm_final"] = f(inputs["norm_final"]).reshape(1, D)
    shared["relb"] = f(inputs["rel_bias"]).reshape(1, 128)
    shared["lqk"] = f(np.stack([inputs["a_lq1"], inputs["a_lk1"], inputs["a_lq2"], inputs["a_lk2"]], axis=1))
    shared["convwb"] = f(np.concatenate([inputs["b_conv_w"], np.asarray(inputs["b_conv_b"])[:, None, :]], axis=1))
    shared["igfg"] = f(np.concatenate([inputs["b_ig_bias"], inputs["b_fg_bias"]], axis=-1))
    for k in ("c_b_re", "c_b_im", "c_c_re", "c_c_im", "c_w_glu", "d_w_w2", "d_w_a2", "d_w_g2"):
        shared[k] = f(inputs[k])
    ldt = np.repeat(np.asarray(inputs["c_log_dt"], np.float32)[:, :, None], 64, axis=2)
    shared["s5p"] = f(np.stack([inputs["c_lam_re"], inputs["c_lam_im"], ldt], axis=1)).reshape(2, 48, 128)
    shared["cd16"] = f(inputs["c_d"]).reshape(2, 16, 32)
    shared["dvecs"] = f(np.concatenate([np.asarray(inputs[k], np.float32) for k in
                                        ("d_mu", "d_w0", "d_a0", "d_k_k", "d_k_a", "d_r_k", "d_ln_g", "d_ln_b")], axis=1)).reshape(2, 42, 128)
    shared.update(_consts())
    xs = f(inputs["x_sample"])
    xfull = np.zeros((TP, D), np.float32)
    xfull[0:16] = np.asarray(inputs["meta_tokens"], np.float32)
    xfull[16:NVALID] = np.asarray(inputs["x_prompt"], np.float32)[0][:NVALID - 16]
    shared["xp0"] = np.ascontiguousarray(xfull.T).reshape(8, 128, TP)
    in_maps = []
    for c in range(NCORES):
        bs = slice(4 * c, 4 * c + 4)
        m = dict(shared)
        m["xs"] = xs[bs].reshape(NTOK, D)
        m["cak"] = f(inputs["cache_a_k"][:, bs]).reshape(2, 4, 2064, 512)
        m["cav"] = f(inputs["cache_a_v"][:, bs]).reshape(2, 4, 2064, 512)
        m["sbc"] = f(inputs["state_b_c"][:, bs])
        m["sbn"] = f(inputs["state_b_n"][:, bs]).reshape(2, 16, 128)
        m["sbm"] = f(inputs["state_b_m"][:, bs]).reshape(2, 16)
        m["sbconv"] = f(inputs["state_b_conv"][:, bs]).reshape(2, 12, 1024)
        m["sc0"] = f(np.stack([inputs["state_c_re"][:, bs], inputs["state_c_im"][:, bs]], axis=1)).reshape(2, 128, 128)
        m["sds"] = f(inputs["state_d_s"][:, bs])
        m["sdsh"] = f(inputs["state_d_shift"][:, bs]).reshape(2, 4, 1792)
        in_maps.append(m)
    res = run_bass_kernel_spmd(nc, in_maps, core_ids=list(range(NCORES)))
    R = res.results
    outs = [np.zeros(s, np.float32) for s in _out_shapes()]
    cat = lambda name, shp: np.concatenate([r[name].reshape(shp) for r in R], axis=1)
    outs[1] = np.concatenate([r["ys"].reshape(4, 32, D) for r in R], axis=0)
    outs[12] = cat("nak", (2, 4, 32, 4, 128))
    outs[13] = cat("nav", (2, 4, 32, 4, 128))
    outs[14] = cat("nbc", (2, 4, 4, 128, 128))
    outs[15] = cat("nbn", (2, 4, 4, 128))
    outs[16] = cat("nbm", (2, 4, 4))
    outs[17] = cat("nbconv", (2, 4, 3, 1024))
    cs = np.concatenate([r["ncs"].reshape(2, 2, 4, 32, 64) for r in R], axis=2)
    outs[18] = np.ascontiguousarray(cs[:, 0])
    outs[19] = np.ascontiguousarray(cs[:, 1])
    outs[20] = cat("nds", (2, 4, 8, 64, 64))
    outs[21] = cat("ndsh", (2, 4, 1, 1792))
    r0 = R[0]
    outs[0] = r0["yp"].reshape(1, NVALID - 16, D)
    outs[2] = r0["pak"].reshape(2, 1, NVALID, 4, 128)
    outs[3] = r0["pav"].reshape(2, 1, NVALID, 4, 128)
    outs[4] = r0["pbc"].reshape(2, 1, 4, 128, 128)
    outs[5] = r0["pbn"].reshape(2, 1, 4, 128)
    outs[6] = r0["pbm"].reshape(2, 1, 4)
    outs[7] = r0["pbconv"].reshape(2, 1, 3, 1024)
    pc = r0["pcs"].reshape(2, 2, 1, 32, 64)
    outs[8] = np.ascontiguousarray(pc[:, 0])
    outs[9] = np.ascontiguousarray(pc[:, 1])
    outs[10] = r0["pds"].reshape(2, 1, 8, 64, 64)
    outs[11] = r0["pdsh"].reshape(2, 1, 1, 1792)
    global _DBGMIX
    _DBGMIX = r0["dbgmix"] if DBG else None
    global _DBG
    _DBG = [r["dbg"] for r in R]
    return tuple(outs)
```

```python
import contextlib
import math
import numpy as np
import concourse.bass as bass
import concourse.mybir as mybir
from concourse.bass_utils import run_bass_kernel_spmd

F32 = mybir.dt.float32
BF16 = mybir.dt.bfloat16
ALU = mybir.AluOpType
AF = mybir.ActivationFunctionType
AX = mybir.AxisListType

D = 1024
DEPTH = 4
NTOK = 128
EV_COLS = 3592
OD_COLS = 2304
FFN = 2816
EPS = 1e-6
EPOCH = 1000000
NCORES = 8
import os
NTILE = int(os.environ.get('K_NTILE', '129'))
NGRP = (NTILE + 3) // 4
TP = NGRP * 512
NVALID = 128 * (NTILE - 1) + 16
PROMPT_LAYERS = int(os.environ.get('K_PROMPT_LAYERS', '4'))
SKIP_SAMPLE = int(os.environ.get('K_SKIP_SAMPLE', '0'))
P_STAGES = os.environ.get('K_P_STAGES', 'dmaso')
ATT_HEADS = int(os.environ.get('K_ATT_HEADS', '4'))
DBG = int(os.environ.get('K_DBG', '0'))
ATT_MODE = int(os.environ.get('K_ATT_MODE', '4'))
PREP = os.environ.get('K_PREP', 'kv')


class FW:
    def __init__(self, nc, n_dma_sems=6):
        self.nc = nc
        self.es = contextlib.ExitStack()
        self.cur = self.es
        self.eng = {'pe': nc.tensor, 'act': nc.scalar, 'dve': nc.vector, 'pool': nc.gpsimd, 'sp': nc.sync}
        self.cnt = {e: 0 for e in ('pe', 'act', 'dve', 'pool')}
        self.sems = {e: [] for e in self.cnt}
        self.seen = {e: {} for e in self.eng}
        self.res = {}
        self.dma_ring = {}
        self.dma_i = {}
        self.n_dma_sems = n_dma_sems
        self.semobj = {}
        self.n_instr = 0
        for q in ('sp', 'pool'):
            self.dma_ring[q] = [self._newsem(f'dma_{q}_{i}') for i in range(n_dma_sems)]
            self.dma_i[q] = 0

    def _newsem(self, name):
        s = self.es.enter_context(self.nc.semaphore(name))
        self.semobj[name] = s
        return name

    def sb(self, name, shape, dt=F32):
        return self.cur.enter_context(self.nc.sbuf_tensor("s_" + name, list(shape), dt))

    @contextlib.contextmanager
    def scope(self):
        old = self.cur
        self.cur = contextlib.ExitStack()
        try:
            yield
        finally:
            self.barrier()
            self.cur.close()
            self.cur = old

    def barrier(self):
        toks = []
        for e, c in self.cnt.items():
            if c > 0:
                ep = (c - 1) // EPOCH
                toks.append((self.sems[e][ep], c - ep * EPOCH))
        for q, ring in self.dma_ring.items():
            n = self.n_dma_sems
            for j, sname in enumerate(ring):
                issued = (self.dma_i[q] - j + n - 1) // n if self.dma_i[q] > j else 0
                if issued > 0:
                    toks.append((sname, 16 * issued))
        for e in self.eng:
            for t in toks:
                self._wait(e, t)
        self.res = {}

    def ps(self, name, shape, dt=F32):
        return self.es.enter_context(self.nc.psum_tensor("p_" + name, list(shape), dt))

    def _wait(self, e, tok):
        sname, val = tok
        if self.seen[e].get(sname, 0) >= val:
            return
        self.eng[e].wait_ge(self.semobj[sname], val)
        self.seen[e][sname] = val
        self.n_instr += 1

    def _deps(self, e, reads, writes):
        toks = []
        for k in reads:
            r = self.res.get(k)
            if r and r['w'] is not None:
                toks.append(r['w'])
        for k in writes:
            r = self.res.get(k)
            if r:
                if r['w'] is not None:
                    toks.append(r['w'])
                toks.extend(r['r'].items())
        for t in toks:
            self._wait(e, t)

    def _commit(self, tok, reads, writes):
        for k in reads:
            r = self.res.setdefault(k, {'w': None, 'r': {}})
            if r['r'].get(tok[0], 0) < tok[1]:
                r['r'][tok[0]] = tok[1]
        for k in writes:
            self.res[k] = {'w': tok, 'r': {}}

    def op(self, e, fn, reads=(), writes=()):
        self._deps(e, reads, writes)
        ep = self.cnt[e] // EPOCH
        while len(self.sems[e]) <= ep:
            self.sems[e].append(self._newsem(f'c_{e}_{len(self.sems[e])}'))
        sname = self.sems[e][ep]
        ins = fn(self.eng[e])
        ins.then_inc(self.semobj[sname], 1)
        self.cnt[e] += 1
        tok = (sname, self.cnt[e] - ep * EPOCH)
        self._commit(tok, reads, writes)
        self.n_instr += 1
        return tok

    def dma(self, q, out, in_, reads=(), writes=(), **kw):
        self._deps(q, reads, writes)
        i = self.dma_i[q]
        n = self.n_dma_sems
        sname = self.dma_ring[q][i % n]
        rnd = i // n
        if rnd > 0:
            self._wait(q, (sname, 16 * rnd))
        self.eng[q].dma_start(out=out, in_=in_, **kw).then_inc(self.semobj[sname], 16)
        self.dma_i[q] = i + 1
        tok = (sname, 16 * (rnd + 1))
        self._commit(tok, reads, writes)
        self.n_instr += 1
        return tok

    def finish(self, keys, e='sp'):
        for k in keys:
            r = self.res.get(k)
            if r and r['w'] is not None:
                self._wait(e, r['w'])

    def close(self):
        self.es.close()


def build_program():
    nc = bass.Bass("TRN2", target_bir_lowering=False)
    fw = FW(nc)

    def din(name, shape):
        return nc.dram_tensor(name, list(shape), F32, kind="ExternalInput").ap()

    def dout(name, shape):
        return nc.dram_tensor(name, list(shape), F32, kind="ExternalOutput").ap()

    xs = din("xs", [NTOK, D])
    norm_mix = din("norm_mix", [DEPTH, D])
    norm_ffn = din("norm_ffn", [DEPTH, D])
    norm_final = din("norm_final", [1, D])
    ev_w_in = din("ev_w_in", [2, D, EV_COLS])
    ev_w_out = din("ev_w_out", [2, D, D])
    od_w_in = din("od_w_in", [2, D, OD_COLS])
    od_w_out = din("od_w_out", [2, D, D])
    ffn_w1 = din("ffn_w1", [DEPTH, D, FFN])
    ffn_w3 = din("ffn_w3", [DEPTH, D, FFN])
    ffn_w2 = din("ffn_w2", [DEPTH, FFN, D])
    ident_in = din("ident", [128, 128])
    ys = dout("ys", [NTOK, D])
    cak = din("cak", [2, 4, 2064, 512])
    cav = din("cav", [2, 4, 2064, 512])
    sbc = din("sbc", [2, 4, 4, 128, 128])
    sbn = din("sbn", [2, 16, 128])
    sbm = din("sbm", [2, 16])
    sbconv = din("sbconv", [2, 12, 1024])
    relb = din("relb", [1, 128])
    lqk = din("lqk", [2, 4, 64])
    a_subln = din("a_subln", [2, 128])
    convwb = din("convwb", [2, 5, 1024])
    igfg = din("igfg", [2, 8])
    b_norm = din("b_norm", [2, 512])
    bk_in = din("bk", [128, 96])
    selr_in = din("selr", [8, 8 * 128])
    negbd_in = din("negbd", [128, 128])
    bmask_in = din("bmask", [128, 4])
    scanmask_in = din("scanmask", [128, 128])
    blk_in = din("blk", [128, 128])
    id2_in = din("id2", [128, 64])
    s5p = din("s5p", [2, 48, 128])
    cd16 = din("cd16", [2, 16, 32])
    sc0 = din("sc0", [2, 128, 128])
    c_b_re = din("c_b_re", [2, 32, 64, 16]); c_b_im = din("c_b_im", [2, 32, 64, 16])
    c_c_re = din("c_c_re", [2, 32, 16, 64]); c_c_im = din("c_c_im", [2, 32, 16, 64])
    c_w_glu = din("c_w_glu", [2, 512, 512])
    sds = din("sds", [2, 4, 8, 64, 64])
    sdsh = din("sdsh", [2, 4, 1792])
    dvecs = din("dvecs", [2, 42, 128])
    d_w_w2 = din("d_w_w2", [2, 64, 512]); d_w_a2 = din("d_w_a2", [2, 64, 512]); d_w_g2 = din("d_w_g2", [2, 128, 512])
    ncs = dout("ncs", [2, 128, 128])
    nds = dout("nds", [2, 4, 8, 64, 64])
    ndsh = dout("ndsh", [2, 4, 1792])
    nak = dout("nak", [2, NTOK, 512])
    nav = dout("nav", [2, NTOK, 512])
    nbc = dout("nbc", [2, 4, 4, 128, 128])
    nbn = dout("nbn", [2, 16, 128])
    nbm = dout("nbm", [2, 16])
    nbconv = dout("nbconv", [2, 12, 1024])
    dbg = dout("dbg", [128, 8 * NTOK])
    xp0 = din("xp0", [8, 128, TP])
    negc_in = din("negc", [128, 128])
    bkp_in = din("bkp", [128, 384])
    mskp_in = din("mskp", [128, 384])
    yp = dout("yp", [NVALID - 16, D])
    pak = dout("pak", [2, NVALID, 512]); pav = dout("pav", [2, NVALID, 512])
    pbc = dout("pbc", [2, 4, 128, 128]); pbn = dout("pbn", [2, 4, 128]); pbm = dout("pbm", [2, 4]); pbconv = dout("pbconv", [2, 3, 1024])
    pcs = dout("pcs", [2, 32, 128]); pds = dout("pds", [2, 8, 64, 64]); pdsh = dout("pdsh", [2, 1792])
    xD = nc.dram_tensor("xD", [8, 128, TP], F32, kind="Internal").ap()
    projD = nc.dram_tensor("projD", [29, 128, TP], F32, kind="Internal").ap()
    mixD = nc.dram_tensor("mixD", [8, 128, TP], F32, kind="Internal").ap()

    ident = fw.sb("ident", [128, 128])
    ones = fw.sb("ones", [128, 128])
    epsc = fw.sb("epsc", [128, 1])
    xtok = xF = sq = rstd = hF = projF = mixF = gF = None

    def alloc_dense(nt, sfx, sample):
        t = [fw.sb("xtok" + sfx, [128, D]) if sample else None,
             fw.sb("xF" + sfx, [128, 8, nt]), fw.sb("sq" + sfx, [128, 8, nt]), fw.sb("rstd" + sfx, [128, nt]),
             fw.sb("hF" + sfx, [128, 8, nt], BF16),
             fw.sb("projF" + sfx, [128, 29, nt]) if sample else None,
             fw.sb("mixF" + sfx, [128, 8, nt], BF16), fw.sb("gF" + sfx, [128, 22, nt], BF16)]
        return t

    gvec = fw.sb("gvec", [128, 9, 8])
    SLAB = 4096
    slab32 = [fw.sb(f"slab32_{i}", [128, SLAB]) for i in range(2)]
    slab16 = [fw.sb(f"slab16_{i}", [128, SLAB], BF16) for i in range(2)]
    banks = [fw.ps(f"bank{i}", [128, 512]) for i in range(8)]
    BT = fw.sb("BT", [128, 4, 96])
    bk = fw.sb("bk", [128, 96]); rb_bc = fw.sb("rb_bc", [128, 128])
    selr = fw.sb("selr", [8, 8 * 128]); negbd = fw.sb("negbd", [128, 128])
    bmask = fw.sb("bmask", [128, 4]); scanmask = fw.sb("scanmask", [128, 128])
    onec = fw.sb("onec", [128, 1]); bttmp = fw.sb("bttmp", [128, 96])
    blk = fw.sb("blk", [128, 128]); id2 = fw.sb("id2", [128, 64]); halfpi = fw.sb("halfpi", [128, 1])
    gneps = fw.sb("gneps", [128, 1])
    kc = kT = vst = Vaug = PT = ktok = vtok = bvtok = botok = lq_bc = lamc = qb16 = LI = LF = BR = MT = nig = None
    subs_bc = xcF = qkF = cacc = cwb = stg = tmp12 = Caug = m0bc = igfg_bc = bnorm_bc = mrow = WK = CL = None

    def alloc_even(sfx):
        A = lambda n, shp, dt=F32: fw.sb(n + sfx, shp, dt)
        t = dict(kc=A("kc", [128, 17, 128]), kT=A("kT", [128, 2096], BF16), vst=A("vst", [128, 18, 128]),
                 Vaug=A("Vaug", [128, 18, 129], BF16), PT=A("PT", [128, 2, 576], BF16), ktok=A("ktok", [128, 512]),
                 vtok=A("vtok", [128, 512]), bvtok=A("bvtok", [128, 4, 129]), botok=A("botok", [128, 512]),
                 lq_bc=A("lq_bc", [128, 256]), lamc=A("lamc", [128, 4]), qb16=A("qb16", [128, 4, NTOK], BF16),
                 LI=A("LI", [128, 4, NTOK]), LF=A("LF", [128, 4, NTOK]), BR=A("BR", [128, 4, NTOK]), MT=A("MT", [128, 4, NTOK]),
                 nig=A("nig", [128, 8]), subs_bc=A("subs_bc", [128, 128]), xcF=A("xcF", [128, 8, 4, 35]),
                 qkF=A("qkF", [128, 8, NTOK]), cacc=A("cacc", [128, 8, NTOK]), cwb=A("cwb", [128, 8, 5]),
                 stg=A("stg", [16, 1024]), tmp12=A("tmp12", [128, 8, 12]), Caug=A("Caug", [128, 16, 129]),
                 m0bc=A("m0bc", [128, 16]), igfg_bc=A("igfg_bc", [128, 8]), bnorm_bc=A("bnorm_bc", [128, 512]),
                 mrow=A("mrow", [1, 16]), WK=[A(f"wk{i}", [128, 129]) for i in range(14)], CL=A("cl", [128, 24]))
        return t

    state = {'slab': 0, 'bank': 0, 'nt': NTOK}

    fw.dma('sp', ident[:], ident_in, writes=['ident'])
    fw.op('dve', lambda e: e.memset(ones[:], 1.0), writes=['ones'])
    fw.op('dve', lambda e: e.memset(epsc[:], EPS), writes=['epsc'])
    fw.op('dve', lambda e: e.memset(onec[:], 1.0), writes=['onec'])
    fw.dma('sp', bk[:], bk_in, writes=['bk'])
    fw.dma('sp', blk[:], blk_in, writes=['blk'])
    fw.dma('sp', id2[:], id2_in, writes=['id2'])
    fw.op('dve', lambda e: e.memset(halfpi[:], math.pi / 2), writes=['halfpi'])
    fw.op('dve', lambda e: e.memset(gneps[:], 64e-5), writes=['gneps'])
    fw.dma('sp', selr[:], selr_in, writes=['selr'])
    fw.dma('sp', negbd[:], negbd_in, writes=['negbd'])
    fw.dma('sp', bmask[:], bmask_in, writes=['bmask'])
    fw.dma('sp', scanmask[:], scanmask_in, writes=['scanmask'])
    fw.dma('sp', rb_bc[:], relb[0].partition_broadcast(128), writes=['rb_bc'])
    fw.op('dve', lambda e: e.memset(BT[:], 0.0), writes=['BT'])
    for h in range(4):
        for bkt in range(32):
            fw.op('dve', lambda e, h=h, bkt=bkt: e.tensor_scalar(
                out=bttmp[:], in0=bk[:], scalar1=float(bkt), scalar2=rb_bc[:, bkt * 4 + h:bkt * 4 + h + 1],
                op0=ALU.is_equal, op1=ALU.mult), reads=['bk', 'rb_bc'], writes=['bttmp'])
            fw.op('dve', lambda e, h=h: e.tensor_tensor(out=BT[:, h, :], in0=BT[:, h, :], in1=bttmp[:], op=ALU.add),
                  reads=['bttmp', 'BT'], writes=['BT'])
    for i in range(DEPTH):
        fw.dma('sp', gvec[:, i, :], norm_mix[i].rearrange("(k p) -> p k", p=128), writes=['gvec'],
               allow_slow_non_contiguous=True)
        fw.dma('sp', gvec[:, 4 + i, :], norm_ffn[i].rearrange("(k p) -> p k", p=128), writes=['gvec'],
               allow_slow_non_contiguous=True)
    fw.dma('sp', gvec[:, 8, :], norm_final[0].rearrange("(k p) -> p k", p=128), writes=['gvec'],
           allow_slow_non_contiguous=True)

    sc_sample = fw.scope()
    sc_sample.__enter__()
    xtok, xF, sq, rstd, hF, projF, mixF, gF = alloc_dense(NTOK, "_s", True)
    fw.op('dve', lambda e: e.memset(mixF[:], 0.0), writes=['mixF'])
    fw.dma('sp', xtok[:], xs, writes=['xtok'])
    for k in range(8):
        b = banks[k % 2]
        fw.op('pe', lambda e, k=k, b=b: e.transpose(b[:, 0:128], xtok[:, k * 128:(k + 1) * 128], ident[:]),
              reads=['xtok', 'ident'], writes=[f'bank{k % 2}'])
        fw.op('act', lambda e, k=k, b=b: e.copy(out=xF[:, k, :], in_=b[:, 0:128]),
              reads=[f'bank{k % 2}'], writes=['xF'])

    def rmsnorm(which, out_tile, out_key):
        nt = state['nt']
        fw.op('act', lambda e: e.activation(out=sq[:], in_=xF[:], func=AF.Square), reads=['xF'], writes=['sq'])
        b = banks[7]
        for k in range(8):
            fw.op('pe', lambda e, k=k: e.matmul(b[:, 0:nt], lhsT=ones[:], rhs=sq[:, k, :],
                                                start=(k == 0), stop=(k == 7)),
                  reads=['sq', 'ones'], writes=['bank7'])
        fw.op('act', lambda e: e.activation(out=rstd[:], in_=b[:, 0:nt], func=AF.Sqrt, scale=1.0 / D,
                                            bias=epsc[:]), reads=['bank7', 'epsc'], writes=['rstd'])
        fw.op('dve', lambda e: e.reciprocal(out=rstd[:], in_=rstd[:]), reads=['rstd'], writes=['rstd'])
        for k in range(8):
            fw.op('dve', lambda e, k=k: e.scalar_tensor_tensor(
                out=out_tile[:, k, :], in0=xF[:, k, :], scalar=gvec[:, which, k:k + 1], in1=rstd[:],
                op0=ALU.mult, op1=ALU.mult), reads=['xF', 'gvec', 'rstd'], writes=[out_key])

    def std(src, n):
        return [(k * 128, 128, 0, src[:, k, :]) for k in range(n)]

    def linear(chunks, src_key, W, M, consume, tok_groups=None, mtile=128):
        nk = len(chunks)
        nt = state['nt']
        skeys = [src_key] if isinstance(src_key, str) else list(src_key)
        gw = 512 if nk * 512 <= SLAB else (256 if nk * 256 <= SLAB else 128)
        for g0 in range(0, M, gw):
            w = min(gw, M - g0)
            si = state['slab']
            state['slab'] ^= 1
            s32, s16 = slab32[si], slab16[si]
            for ci, (row0, kp, p0, src) in enumerate(chunks):
                fw.dma('sp', s32[p0:p0 + kp, ci * w:(ci + 1) * w], W[row0:row0 + kp, g0:g0 + w],
                       writes=[f's32_{si}_{ci}'])
            fw.op('pool', lambda e, s32=s32, s16=s16, n=nk * w: e.tensor_copy(out=s16[:, 0:n], in_=s32[:, 0:n]),
                  reads=[f's32_{si}_{ci}' for ci in range(nk)], writes=[f's16_{si}'])
            if tok_groups and g0 in tok_groups:
                b = banks[6]
                for ci, (row0, kp, p0, src) in enumerate(chunks):
                    fw.op('pe', lambda e, ci=ci, kp=kp, p0=p0, src=src, b=b, s16=s16, w=w: e.matmul(
                        b[:, 0:w], lhsT=src, rhs=s16[p0:p0 + kp, ci * w:(ci + 1) * w],
                        start=(ci == 0), stop=(ci == nk - 1)), reads=[f's16_{si}'] + skeys, writes=['bank6'])
                tok_groups[g0](b)
            for m0 in range(0, w, mtile):
                msz = min(mtile, w - m0)
                bi = state['bank']
                state['bank'] = (bi + 1) % 4
                b = banks[bi]
                for ci, (row0, kp, p0, src) in enumerate(chunks):
                    fw.op('pe', lambda e, ci=ci, kp=kp, p0=p0, src=src, m0=m0, msz=msz, b=b, s16=s16, w=w: e.matmul(
                        b[0:msz, 0:nt], lhsT=s16[p0:p0 + kp, ci * w + m0:ci * w + m0 + msz], rhs=src,
                        start=(ci == 0), stop=(ci == nk - 1)),
                        reads=[f's16_{si}'] + skeys, writes=[f'bank{bi}'])
                consume((g0 + m0) // mtile, msz, b[0:msz, 0:nt], f'bank{bi}')

    def to_tile(dst, dst_key, eng='act'):
        def c(mt, msz, p, pkey):
            if eng == 'act':
                fw.op('act', lambda e: e.copy(out=dst[0:msz, mt, :], in_=p), reads=[pkey], writes=[dst_key])
            else:
                fw.op('dve', lambda e: e.tensor_copy(out=dst[0:msz, mt, :], in_=p), reads=[pkey], writes=[dst_key])
        return c

    def add_resid(mt, msz, p, pkey):
        fw.op('dve', lambda e: e.tensor_tensor(out=xF[0:msz, mt, :], in0=xF[0:msz, mt, :], in1=p, op=ALU.add),
              reads=[pkey, 'xF'], writes=['xF'])


    LAM_INIT = [0.8 - 0.6 * math.exp(-0.3 * l) for l in range(DEPTH)]

    def Vop(fn, r, w):
        return fw.op('dve', fn, reads=r, writes=w)

    def Aop(fn, r, w):
        return fw.op('act', fn, reads=r, writes=w)

    def Pop(fn, r, w):
        return fw.op('pe', fn, reads=r, writes=w)

    def diag(src_ap, src_key, col_ap, scratch=11):
        Vop(lambda e: e.scalar_tensor_tensor(out=WK[scratch][:, 0:128], in0=src_ap, scalar=1.0, in1=ident[:],
                                             op0=ALU.mult, op1=ALU.mult, accum_out=col_ap),
            [src_key, 'ident'], [f'wk{scratch}', 'cl'])

    def even_tok_groups(e_):
        def kcopy(b):
            Aop(lambda e: e.copy(out=ktok[:], in_=b[:, 0:512]), ['bank6'], ['ktok'])
            fw.dma('sp', nak[e_], ktok[:], reads=['ktok'], writes=['nak'])

        def vcopy(b):
            Aop(lambda e: e.copy(out=vtok[:], in_=b[:, 0:512]), ['bank6'], ['vtok'])
            fw.dma('sp', nav[e_], vtok[:], reads=['vtok'], writes=['nav'])

        def bvcopy(b):
            Aop(lambda e: e.copy(out=bvtok[:, :, 0:128], in_=b[:, 0:512].rearrange("p (h v) -> p h v", v=128)),
                ['bank6'], ['bvtok'])

        def bocopy(b):
            Aop(lambda e: e.copy(out=botok[:], in_=b[:, 0:512]), ['bank6'], ['botok'])
        return {512: kcopy, 1024: vcopy, 2560: bvcopy, 3072: bocopy}

    def attention(e_, layer):
        lam_init = LAM_INIT[layer]
        fw.dma('sp', lq_bc[:], lqk[e_].rearrange("a d -> (a d)").partition_broadcast(128), writes=['lq_bc'])
        fw.dma('sp', subs_bc[:], a_subln[e_].partition_broadcast(128), writes=['subs_bc'])
        Vop(lambda e: e.scalar_tensor_tensor(out=WK[0][:, 0:64], in0=lq_bc[:, 0:64], scalar=1.0, in1=lq_bc[:, 64:128],
                                             op0=ALU.mult, op1=ALU.mult, accum_out=lamc[:, 0:1]), ['lq_bc'], ['wk0', 'lamc'])
        Vop(lambda e: e.scalar_tensor_tensor(out=WK[0][:, 0:64], in0=lq_bc[:, 128:192], scalar=1.0, in1=lq_bc[:, 192:256],
                                             op0=ALU.mult, op1=ALU.mult, accum_out=lamc[:, 1:2]), ['lq_bc', 'wk0'], ['wk0', 'lamc'])
        Aop(lambda e: e.activation(out=lamc[:, 0:2], in_=lamc[:, 0:2], func=AF.Exp), ['lamc'], ['lamc'])
        Vop(lambda e: e.tensor_tensor(out=lamc[:, 2:3], in0=lamc[:, 0:1], in1=lamc[:, 1:2], op=ALU.subtract), ['lamc'], ['lamc'])
        Vop(lambda e: e.tensor_scalar(out=lamc[:, 3:4], in0=lamc[:, 2:3], scalar1=-1.0, scalar2=-lam_init,
                                      op0=ALU.mult, op1=ALU.add), ['lamc'], ['lamc'])
        Vop(lambda e: e.tensor_scalar(out=subs_bc[:], in0=subs_bc[:], scalar1=1.0 - lam_init, scalar2=None, op0=ALU.mult),
            ['subs_bc'], ['subs_bc'])
        Vop(lambda e: e.tensor_copy(out=qb16[:], in_=projF[:, 0:4, :]), ['projF'], ['qb16'])
        for b in range(4):
            seg = slice(b * 32, (b + 1) * 32)
            for h in range(4):
                hs = slice(h * 128, (h + 1) * 128)
                fw.dma('sp', kc[:, 0:16, :], cak[e_, b, 0:2048, hs].rearrange("(blk p) d -> p blk d", p=128),
                       writes=['kc'])
                fw.dma('sp', kc[0:16, 16, :], cak[e_, b, 2048:2064, hs], writes=['kc'])
                fw.dma('sp', vst[:, 0:16, :], cav[e_, b, 0:2048, hs].rearrange("(blk p) d -> p blk d", p=128),
                       writes=['vst'])
                fw.dma('sp', vst[0:16, 16, :], cav[e_, b, 2048:2064, hs], writes=['vst'])
                fw.dma('sp', vst[0:32, 17, :], vtok[seg, hs], reads=['vtok'], writes=['vst'])
                for grp in range(5):
                    bi = 4 + grp % 2
                    bb = banks[bi]
                    for blk in range(grp * 4, min(grp * 4 + 4, 17)):
                        npart = 128 if blk < 16 else 16
                        c0 = (blk % 4) * 128
                        Pop(lambda e, bb=bb, blk=blk, npart=npart, c0=c0: e.transpose(
                            bb[:, c0:c0 + npart], kc[0:npart, blk, :], ident[0:npart, 0:npart]),
                            ['kc', 'ident'], [f'bank{bi}'])
                    if grp < 4:
                        Aop(lambda e, bb=bb, grp=grp: e.copy(out=kT[:, grp * 512:(grp + 1) * 512], in_=bb[:, 0:512]),
                            [f'bank{bi}'], ['kT'])
                    else:
                        Aop(lambda e, bb=bb: e.copy(out=kT[:, 2048:2064], in_=bb[:, 0:16]), [f'bank{bi}'], ['kT'])
                Aop(lambda e, h=h, seg=seg: e.copy(out=kT[:, 2064:2096], in_=projF[:, 4 + h, seg]), ['projF'], ['kT'])
                fw.op('pool', lambda e: e.tensor_copy(out=Vaug[:, :, 0:128], in_=vst[:]), reads=['vst'], writes=['Vaug'])
                for m in range(2):
                    ms = slice(64 * m, 64 * m + 64)
                    for blk in range(15):
                        Pop(lambda e, m=m, ms=ms, blk=blk, h=h, seg=seg: e.matmul(
                            banks[m][:, blk * 32:(blk + 1) * 32], lhsT=kT[ms, blk * 128:(blk + 1) * 128],
                            rhs=qb16[ms, h, seg], start=True, stop=True), ['kT', 'qb16'], [f'bank{m}'])
                    for j, (c0, nk) in enumerate(((1920, 128), (2048, 16), (2064, 32))):
                        Pop(lambda e, m=m, ms=ms, j=j, c0=c0, nk=nk, h=h, seg=seg: e.matmul(
                            banks[2 + m][0:nk, j * 32:(j + 1) * 32], lhsT=kT[ms, c0:c0 + nk],
                            rhs=qb16[ms, h, seg], start=True, stop=True), ['kT', 'qb16'], [f'bank{2 + m}'])
                    Aop(lambda e, m=m, h=h: e.activation(out=PT[:, m, 0:480], in_=banks[m][:, 0:480], func=AF.Exp,
                                                         bias=rb_bc[:, 60 + h:61 + h], scale=0.125),
                        [f'bank{m}', 'rb_bc'], ['PT'])
                    Vop(lambda e, m=m, h=h: e.scalar_tensor_tensor(out=WK[1][:, 0:96], in0=banks[2 + m][:, 0:96], scalar=0.125,
                                                                   in1=BT[:, h, :], op0=ALU.mult, op1=ALU.add),
                        [f'bank{2 + m}', 'BT'], ['wk1'])
                    Aop(lambda e, m=m: e.activation(out=PT[:, m, 480:576], in_=WK[1][:, 0:96], func=AF.Exp),
                        ['wk1'], ['PT'])
                for m in range(2):
                    for blk in range(18):
                        nk = 128 if blk < 16 else (16 if blk == 16 else 32)
                        off = blk * 32 if blk < 15 else 480 + (blk - 15) * 32
                        Pop(lambda e, m=m, blk=blk, nk=nk, off=off: e.matmul(
                            banks[7][0:32, m * 129:(m + 1) * 129], lhsT=PT[0:nk, m, off:off + 32], rhs=Vaug[0:nk, blk, :],
                            start=(blk == 0), stop=(blk == 17)), ['PT', 'Vaug'], ['bank7'])
                o7 = banks[7]
                Vop(lambda e: e.reciprocal(out=CL[0:32, 0:1], in_=o7[0:32, 128:129]), ['bank7'], ['cl'])
                Vop(lambda e: e.reciprocal(out=CL[0:32, 1:2], in_=o7[0:32, 257:258]), ['bank7'], ['cl'])
                Vop(lambda e: e.tensor_tensor(out=CL[0:32, 2:3], in0=CL[0:32, 1:2], in1=lamc[0:32, 3:4], op=ALU.mult),
                    ['cl', 'lamc'], ['cl'])
                Vop(lambda e: e.tensor_scalar(out=WK[2][0:32, 0:128], in0=o7[0:32, 0:128], scalar1=CL[0:32, 0:1], scalar2=None,
                                              op0=ALU.mult), ['bank7', 'cl'], ['wk2'])
                Vop(lambda e: e.scalar_tensor_tensor(out=WK[3][0:32, 0:128], in0=o7[0:32, 129:257], scalar=CL[0:32, 2:3],
                                                     in1=WK[2][0:32, 0:128], op0=ALU.mult, op1=ALU.add),
                    ['bank7', 'cl', 'wk2'], ['wk3'])
                Vop(lambda e: e.scalar_tensor_tensor(out=WK[4][0:32, 0:128], in0=WK[3][0:32, 0:128], scalar=1.0,
                                                     in1=WK[3][0:32, 0:128], op0=ALU.mult, op1=ALU.mult,
                                                     accum_out=CL[0:32, 3:4]), ['wk3'], ['wk4', 'cl'])
                Aop(lambda e: e.activation(out=CL[0:32, 4:5], in_=CL[0:32, 3:4], func=AF.Sqrt, scale=1.0 / 128,
                                           bias=epsc[0:32, :]), ['cl', 'epsc'], ['cl'])
                Vop(lambda e: e.reciprocal(out=CL[0:32, 5:6], in_=CL[0:32, 4:5]), ['cl'], ['cl'])
                Vop(lambda e: e.scalar_tensor_tensor(out=WK[2][0:32, 0:128], in0=WK[3][0:32, 0:128], scalar=CL[0:32, 5:6],
                                                     in1=subs_bc[0:32, :], op0=ALU.mult, op1=ALU.mult),
                    ['wk3', 'cl', 'subs_bc'], ['wk2'])
                Pop(lambda e: e.transpose(banks[5][:, 0:32], WK[2][0:32, 0:128], ident[0:32, 0:32]),
                    ['wk2', 'ident'], ['bank5'])
                Aop(lambda e, h=h, seg=seg: e.copy(out=mixF[:, h, seg], in_=banks[5][:, 0:32]), ['bank5'], ['mixF'])

    def mlstm(e_):
        fw.dma('sp', stg[0:5, :], convwb[e_], writes=['stg'])
        for k in range(8):
            Pop(lambda e, k=k: e.transpose(banks[4][:, k * 8:k * 8 + 5], stg[0:5, k * 128:(k + 1) * 128], ident[0:5, 0:5]),
                ['stg', 'ident'], ['bank4'])
        for k in range(8):
            Aop(lambda e, k=k: e.copy(out=cwb[:, k, :], in_=banks[4][:, k * 8:k * 8 + 5]), ['bank4'], ['cwb'])
        fw.dma('sp', stg[0:12, :], sbconv[e_], reads=[], writes=['stg'])
        for k in range(8):
            Pop(lambda e, k=k: e.transpose(banks[5][:, k * 16:k * 16 + 12], stg[0:12, k * 128:(k + 1) * 128], ident[0:12, 0:12]),
                ['stg', 'ident'], ['bank5'])
        for k in range(8):
            Aop(lambda e, k=k: e.copy(out=xcF[:, k, :, 0:3], in_=banks[5][:, k * 16:k * 16 + 12].rearrange("p (b j) -> p b j", j=3)),
                ['bank5'], ['xcF'])
            Vop(lambda e, k=k: e.tensor_copy(out=xcF[:, k, :, 3:35], in_=projF[:, 12 + k, :].rearrange("p (b s) -> p b s", s=32)),
                ['projF'], ['xcF'])
        for k in range(8):
            cv = cacc[:, k, :].rearrange("p (b s) -> p b s", s=32)
            Vop(lambda e, k=k, cv=cv: e.tensor_scalar(out=cv, in0=xcF[:, k, :, 0:32], scalar1=cwb[:, k, 0:1], scalar2=cwb[:, k, 4:5],
                                                      op0=ALU.mult, op1=ALU.add), ['xcF', 'cwb'], ['cacc'])
            for j in range(1, 4):
                Vop(lambda e, k=k, j=j, cv=cv: e.scalar_tensor_tensor(out=cv, in0=xcF[:, k, :, j:j + 32], scalar=cwb[:, k, j:j + 1],
                                                                      in1=cv, op0=ALU.mult, op1=ALU.add),
                    ['xcF', 'cwb', 'cacc'], ['cacc'])
        Aop(lambda e: e.activation(out=qkF[:], in_=cacc[:], func=AF.Silu), ['cacc'], ['qkF'])
        Aop(lambda e: e.mul(out=qkF[:, 0:4, :], in_=qkF[:, 0:4, :], mul=128 ** -0.5), ['qkF'], ['qkF'])
        Vop(lambda e: e.tensor_copy(out=tmp12[:].rearrange("p k (b j) -> p k b j", j=3), in_=xcF[:, :, :, 32:35]),
            ['xcF'], ['tmp12'])
        for k in range(8):
            bi = 4 + k // 4
            Pop(lambda e, k=k, bi=bi: e.transpose(banks[bi][0:12, (k % 4) * 128:(k % 4 + 1) * 128], tmp12[:, k, :], ident[:]),
                ['tmp12', 'ident'], [f'bank{bi}'])
        Aop(lambda e: e.copy(out=stg[0:12, 0:512], in_=banks[4][0:12, 0:512]), ['bank4'], ['stg'])
        Aop(lambda e: e.copy(out=stg[0:12, 512:1024], in_=banks[5][0:12, 0:512]), ['bank5'], ['stg'])
        fw.dma('sp', nbconv[e_], stg[0:12, :], reads=['stg'], writes=['nbconv'])
        fw.dma('sp', Caug[:, :, 0:128], sbc[e_].rearrange("b h k v -> k (b h) v"), writes=['Caug'])
        fw.dma('sp', stg[0:16, 0:128], sbn[e_], writes=['stg'])
        Pop(lambda e: e.transpose(banks[4][:, 0:16], stg[0:16, 0:128], ident[0:16, 0:16]), ['stg', 'ident'], ['bank4'])
        Aop(lambda e: e.copy(out=Caug[:, :, 128], in_=banks[4][:, 0:16]), ['bank4'], ['Caug'])
        fw.dma('sp', m0bc[:], sbm[e_].partition_broadcast(128), writes=['m0bc'])
        fw.dma('sp', igfg_bc[:], igfg[e_].partition_broadcast(128), writes=['igfg_bc'])
        fw.dma('sp', bnorm_bc[:], b_norm[e_].partition_broadcast(128), writes=['bnorm_bc'])
        Vop(lambda e: e.tensor_scalar(out=nig[:], in0=igfg_bc[:], scalar1=-1.0, scalar2=None, op0=ALU.mult), ['igfg_bc'], ['nig'])
        for j in range(8):
            bi = 4 + j // 4
            Pop(lambda e, j=j, bi=bi: e.matmul(banks[bi][:, (j % 4) * 128:(j % 4 + 1) * 128], lhsT=selr[0:8, j * 128:(j + 1) * 128],
                                               rhs=projF[0:8, 28, :], start=True, stop=True), ['selr', 'projF'], [f'bank{bi}'])
        for h in range(4):
            Vop(lambda e, h=h: e.tensor_scalar(out=LI[:, h, :], in0=banks[4][:, h * 128:(h + 1) * 128], scalar1=igfg_bc[:, h:h + 1],
                                               scalar2=None, op0=ALU.add), ['bank4', 'igfg_bc'], ['LI'])
            Aop(lambda e, h=h: e.activation(out=LF[:, h, :], in_=banks[5][:, h * 128:(h + 1) * 128], func=AF.Exp,
                                            bias=nig[:, 4 + h:5 + h], scale=-1.0), ['bank5', 'nig'], ['LF'])
        Aop(lambda e: e.activation(out=LF[:], in_=LF[:], func=AF.Ln, bias=onec[:], scale=1.0), ['LF', 'onec'], ['LF'])
        Vop(lambda e: e.tensor_scalar(out=LF[:], in0=LF[:], scalar1=-1.0, scalar2=None, op0=ALU.mult), ['LF'], ['LF'])
        for h in range(4):
            Vop(lambda e, h=h: e.tensor_tensor_scan(out=BR[:, h, :], data0=scanmask[:], data1=LF[:, h, :], initial=0.0,
                                                    op0=ALU.mult, op1=ALU.add), ['scanmask', 'LF'], ['BR'])
            for b in range(4):
                seg = slice(b * 32, (b + 1) * 32)
                Vop(lambda e, h=h, b=b, seg=seg: e.tensor_tensor_scan(
                    out=MT[:, h, seg], data0=LF[:, h, seg], data1=LI[:, h, seg], initial=m0bc[:, b * 4 + h:b * 4 + h + 1],
                    op0=ALU.add, op1=ALU.max), ['LF', 'LI', 'm0bc'], ['MT'])
        for h in range(4):
            hs = slice(h * 128, (h + 1) * 128)
            Pop(lambda e, h=h: e.matmul(banks[4][:, 0:128], lhsT=qkF[:, 4 + h, :], rhs=qkF[:, h, :], start=True, stop=True),
                ['qkF'], ['bank4'])
            Vop(lambda e, h=h: e.tensor_tensor(out=WK[5][:, 0:128], in0=BR[:, h, :], in1=MT[:, h, :], op=ALU.subtract),
                ['BR', 'MT'], ['wk5'])
            Vop(lambda e: e.tensor_tensor(out=WK[5][:, 0:128], in0=WK[5][:, 0:128], in1=negbd[:], op=ALU.add),
                ['wk5', 'negbd'], ['wk5'])
            Vop(lambda e, h=h: e.tensor_tensor(out=WK[6][:, 0:128], in0=LI[:, h, :], in1=BR[:, h, :], op=ALU.subtract),
                ['LI', 'BR'], ['wk6'])
            diag(WK[6][:, 0:128], 'wk6', CL[:, 8:9])
            Aop(lambda e: e.activation(out=WK[5][:, 0:128], in_=WK[5][:, 0:128], func=AF.Exp, bias=CL[:, 8:9], scale=1.0),
                ['wk5', 'cl'], ['wk5'])
            Vop(lambda e: e.tensor_tensor(out=WK[6][:, 0:128], in0=banks[4][:, 0:128], in1=WK[5][:, 0:128], op=ALU.mult),
                ['bank4', 'wk5', 'wk6'], ['wk6'])
            Pop(lambda e, h=h: e.matmul(banks[5][:, 0:129], lhsT=WK[6][:, 0:128], rhs=bvtok[:, h, :], start=True, stop=True),
                ['wk6', 'bvtok'], ['bank5'])
            for b in range(4):
                dst = banks[6][:, b * 129:(b + 1) * 129] if b < 3 else banks[4][:, 256:385]
                dk = 'bank6' if b < 3 else 'bank4'
                Pop(lambda e, h=h, b=b, dst=dst: e.matmul(dst, lhsT=qkF[:, h, :], rhs=Caug[:, b * 4 + h, :], start=True, stop=True),
                    ['qkF', 'Caug'], [dk])
            Vop(lambda e: e.tensor_scalar(out=WK[7][:, 0:129], in0=banks[6][:, 0:129], scalar1=bmask[:, 0:1], scalar2=None,
                                          op0=ALU.mult), ['bank6', 'bmask'], ['wk7'])
            for b in range(1, 4):
                src = banks[6][:, b * 129:(b + 1) * 129] if b < 3 else banks[4][:, 256:385]
                sk = 'bank6' if b < 3 else 'bank4'
                Vop(lambda e, b=b, src=src: e.scalar_tensor_tensor(out=WK[7][:, 0:129], in0=src, scalar=bmask[:, b:b + 1],
                                                                   in1=WK[7][:, 0:129], op0=ALU.mult, op1=ALU.add),
                    [sk, 'bmask', 'wk7'], ['wk7'])
            for b in range(4):
                seg = slice(b * 32, (b + 1) * 32)
                Vop(lambda e, h=h, b=b, seg=seg: e.scalar_tensor_tensor(
                    out=WK[8][:, seg], in0=BR[:, h, seg], scalar=m0bc[:, b * 4 + h:b * 4 + h + 1], in1=MT[:, h, seg],
                    op0=ALU.add, op1=ALU.subtract), ['BR', 'MT', 'm0bc'], ['wk8'])
            Aop(lambda e: e.activation(out=WK[8][:, 0:128], in_=WK[8][:, 0:128], func=AF.Exp), ['wk8'], ['wk8'])
            diag(WK[8][:, 0:128], 'wk8', CL[:, 9:10])
            Vop(lambda e: e.scalar_tensor_tensor(out=WK[9][:, 0:129], in0=WK[7][:, 0:129], scalar=CL[:, 9:10],
                                                 in1=banks[5][:, 0:129], op0=ALU.mult, op1=ALU.add),
                ['wk7', 'cl', 'bank5'], ['wk9'])
            Vop(lambda e: e.tensor_scalar(out=CL[:, 20:21], in0=WK[9][:, 128:129], scalar1=-1.0, scalar2=None, op0=ALU.mult),
                ['wk9'], ['cl'])
            Vop(lambda e: e.tensor_tensor(out=CL[:, 10:11], in0=CL[:, 20:21], in1=WK[9][:, 128:129], op=ALU.max),
                ['wk9', 'cl'], ['cl'])
            diag(MT[:, h, :], 'MT', CL[:, 11:12])
            Aop(lambda e: e.activation(out=CL[:, 12:13], in_=CL[:, 11:12], func=AF.Exp, scale=-1.0), ['cl'], ['cl'])
            Vop(lambda e: e.tensor_tensor(out=CL[:, 13:14], in0=CL[:, 10:11], in1=CL[:, 12:13], op=ALU.max), ['cl'], ['cl'])
            Vop(lambda e: e.reciprocal(out=CL[:, 14:15], in_=CL[:, 13:14]), ['cl'], ['cl'])
            Vop(lambda e: e.tensor_scalar(out=WK[10][:, 0:128], in0=WK[9][:, 0:128], scalar1=CL[:, 14:15], scalar2=None,
                                          op0=ALU.mult), ['wk9', 'cl'], ['wk10'])
            Vop(lambda e: e.scalar_tensor_tensor(out=WK[11][:, 0:128], in0=WK[10][:, 0:128], scalar=1.0, in1=WK[10][:, 0:128],
                                                 op0=ALU.mult, op1=ALU.mult, accum_out=CL[:, 15:16]), ['wk10'], ['wk11', 'cl'])
            Aop(lambda e: e.activation(out=CL[:, 16:17], in_=CL[:, 15:16], func=AF.Sqrt, scale=1.0 / 128, bias=epsc[:]),
                ['cl', 'epsc'], ['cl'])
            Vop(lambda e: e.reciprocal(out=CL[:, 17:18], in_=CL[:, 16:17]), ['cl'], ['cl'])
            Aop(lambda e, hs=hs: e.activation(out=WK[11][:, 0:128], in_=botok[:, hs], func=AF.Sigmoid), ['botok', 'wk11'], ['wk11'])
            Vop(lambda e, hs=hs: e.scalar_tensor_tensor(out=WK[10][:, 0:128], in0=WK[10][:, 0:128], scalar=CL[:, 17:18],
                                                        in1=bnorm_bc[:, hs], op0=ALU.mult, op1=ALU.mult),
                ['wk10', 'cl', 'bnorm_bc'], ['wk10'])
            Vop(lambda e: e.tensor_tensor(out=WK[10][:, 0:128], in0=WK[10][:, 0:128], in1=WK[11][:, 0:128], op=ALU.mult),
                ['wk10', 'wk11'], ['wk10'])
            Pop(lambda e: e.transpose(banks[5][:, 256:384], WK[10][:, 0:128], ident[:]), ['wk10', 'ident'], ['bank5'])
            Aop(lambda e, h=h: e.copy(out=mixF[:, 4 + h, :], in_=banks[5][:, 256:384]), ['bank5'], ['mixF'])
            Pop(lambda e, h=h: e.transpose(banks[4][:, 128:256], qkF[:, 4 + h, :], ident[:]), ['qkF', 'ident'], ['bank4'])
            Aop(lambda e: e.copy(out=WK[12][:, 0:128], in_=banks[4][:, 128:256]), ['bank4'], ['wk12'])
            Vop(lambda e, h=h: e.tensor_tensor(out=WK[13][:, 0:128], in0=LI[:, h, :], in1=BR[:, h, :], op=ALU.subtract),
                ['LI', 'BR'], ['wk13'])
            for b in range(4):
                seg = slice(b * 32, (b + 1) * 32)
                last = b * 32 + 31
                Vop(lambda e, h=h, seg=seg, last=last: e.tensor_scalar(
                    out=WK[13][:, seg], in0=WK[13][:, seg], scalar1=BR[:, h, last:last + 1], scalar2=MT[:, h, last:last + 1],
                    op0=ALU.add, op1=ALU.subtract), ['wk13', 'BR', 'MT'], ['wk13'])
            Aop(lambda e: e.activation(out=WK[13][:, 0:128], in_=WK[13][:, 0:128], func=AF.Exp), ['wk13'], ['wk13'])
            diag(WK[13][:, 0:128], 'wk13', CL[:, 18:19])
            Vop(lambda e: e.tensor_scalar(out=WK[12][:, 0:128], in0=WK[12][:, 0:128], scalar1=CL[:, 18:19], scalar2=None,
                                          op0=ALU.mult), ['wk12', 'cl'], ['wk12'])
            for b in range(4):
                last = b * 32 + 31
                idx = b * 4 + h
                Vop(lambda e, b=b: e.tensor_scalar(out=WK[11][:, 0:128], in0=WK[12][:, 0:128], scalar1=bmask[:, b:b + 1], scalar2=None,
                                                   op0=ALU.mult), ['wk12', 'bmask', 'wk11'], ['wk11'])
                Pop(lambda e, h=h: e.matmul(banks[5][:, 0:129], lhsT=WK[11][:, 0:128], rhs=bvtok[:, h, :], start=True, stop=True),
                    ['wk11', 'bvtok'], ['bank5'])
                Vop(lambda e, h=h, last=last, idx=idx: e.scalar_tensor_tensor(
                    out=CL[:, 19:20], in0=BR[:, h, last:last + 1], scalar=m0bc[:, idx:idx + 1], in1=MT[:, h, last:last + 1],
                    op0=ALU.add, op1=ALU.subtract), ['BR', 'MT', 'm0bc'], ['cl'])
                Aop(lambda e: e.activation(out=CL[:, 19:20], in_=CL[:, 19:20], func=AF.Exp), ['cl'], ['cl'])
                Vop(lambda e, idx=idx: e.scalar_tensor_tensor(out=WK[9][:, 0:129], in0=Caug[:, idx, :], scalar=CL[:, 19:20],
                                                              in1=banks[5][:, 0:129], op0=ALU.mult, op1=ALU.add),
                    ['Caug', 'cl', 'bank5', 'wk9'], ['wk9'])
                fw.dma('sp', nbc[e_, b, h], WK[9][:, 0:128], reads=['wk9'], writes=['nbc'])
                fw.dma('sp', nbn[e_, idx].rearrange("(k o) -> k o", o=1), WK[9][:, 128:129], reads=['wk9'], writes=['nbn'])
                Aop(lambda e, h=h, last=last, idx=idx: e.copy(out=mrow[0:1, idx:idx + 1], in_=MT[0:1, h, last:last + 1]),
                    ['MT'], ['mrow'])
        fw.dma('sp', nbm[e_].rearrange("(o k) -> o k", o=1), mrow[0:1, :], reads=['mrow'], writes=['nbm'])

    def vts(out, in0, s1, op0, r, w, s2=None, op1=None):
        if op1 is None:
            Vop(lambda e: e.tensor_scalar(out=out, in0=in0, scalar1=s1, scalar2=None, op0=op0), r, w)
        else:
            Vop(lambda e: e.tensor_scalar(out=out, in0=in0, scalar1=s1, scalar2=s2, op0=op0, op1=op1), r, w)

    def vtt(out, in0, in1, op, r, w):
        Vop(lambda e: e.tensor_tensor(out=out, in0=in0, in1=in1, op=op), r, w)

    def vstt(out, in0, sc, in1, op0, op1, r, w):
        Vop(lambda e: e.scalar_tensor_tensor(out=out, in0=in0, scalar=sc, in1=in1, op0=op0, op1=op1), r, w)

    def act(out, in_, func, r, w, **kw):
        Aop(lambda e: e.activation(out=out, in_=in_, func=func, **kw), r, w)

    def vcopy(out, in_, r, w):
        Vop(lambda e: e.tensor_copy(out=out, in_=in_), r, w)

    def acopy(out, in_, r, w):
        Aop(lambda e: e.copy(out=out, in_=in_), r, w)

    def tpose(out, in_, idn, r, w):
        Pop(lambda e: e.transpose(out, in_, idn), list(r) + ['ident'], w)

    MUL, ADD, SUB = ALU.mult, ALU.add, ALU.subtract

    def s5(o_, sfx, oc32):
        A = lambda n, shp, dt=F32: fw.sb(n + sfx, shp, dt)
        stg3 = A("stg3", [48, 128]); stgd = A("stgd", [16, 32]); d32 = A("d32", [32, 16]); stgx = A("stgx", [128, 128])
        X0T = A("X0T", [128, 128]); Z = A("Z", [128, 20, 16]); ZI = A("ZI", [128, 16], mybir.dt.int32)
        BRI = A("BRI", [128, 2, 16, 16]); T1 = A("T1", [128, 16, 16]); T2 = A("T2", [128, 16, 16])
        XB = A("XB", [128, 2, 16, 32]); BBT = A("BBT", [32, 2, 16, 128]); u32 = A("u32", [32, 16, NTOK])
        XR = A("XR", [128, 16, NTOK]); XI = A("XI", [128, 16, NTOK]); ARb = A("ARb", [128, 16, 4]); AIb = A("AIb", [128, 16, 4])
        TT = A("TT", [128, 4, 16, 4]); ZC = A("ZC", [128, 128]); CZ = A("CZ", [128, 2, 4, 128])
        y32 = A("y32", [32, 16, NTOK]); G1 = A("G1", [32, 16, NTOK]); yg16 = A("yg16", [32, 16, NTOK], BF16)
        tmp32 = A("tmp32", [32, NTOK]); XF = A("XF", [128, 128])
        z = lambda i: Z[:, i, :]
        zk = ['Z']
        fw.dma('sp', stg3[:], s5p[o_], writes=['stg3'])
        tpose(banks[4][:, 0:48], stg3[0:48, :], ident[0:48, 0:48], ['stg3'], ['bank4'])
        acopy(Z[:, 0:3, :], banks[4][:, 0:48].rearrange("p (a i) -> p a i", a=3), ['bank4'], zk)
        fw.dma('sp', stgd[:], cd16[o_], writes=['stgd'])
        tpose(banks[5][0:32, 0:16], stgd[0:16, :], ident[0:16, 0:16], ['stgd'], ['bank5'])
        acopy(d32[:], banks[5][0:32, 0:16], ['bank5'], ['d32'])
        fw.dma('sp', stgx[:], sc0[o_], writes=['stgx'])
        tpose(banks[4][:, 128:256], stgx[:], ident[:], ['stgx'], ['bank4'])
        acopy(X0T[:], banks[4][:, 128:256], ['bank4'], ['X0T'])
        act(z(3), z(2), AF.Exp, zk, zk)
        vtt(z(4), z(0), z(3), MUL, zk, zk)
        act(z(5), z(4), AF.Exp, zk, zk)
        vtt(z(6), z(1), z(3), MUL, zk, zk)
        vts(z(7), z(6), 1.0 / (2 * math.pi), MUL, zk, zk, 0.5, ADD)
        vcopy(ZI[:], z(7), zk, ['ZI'])
        vcopy(z(7), ZI[:], ['ZI'], zk)
        vstt(z(8), z(7), -2 * math.pi, z(6), MUL, ADD, zk, zk)
        act(z(9), z(8), AF.Sin, zk, zk, scale=0.25)
        act(z(10), z(8), AF.Sin, zk + ['halfpi'], zk, scale=0.25, bias=halfpi[:])
        vstt(z(11), z(9), 2.0, z(10), MUL, MUL, zk, zk)
        vtt(z(12), z(9), z(9), MUL, zk, zk)
        vts(z(12), z(12), -2.0, MUL, zk, zk, 1.0, ADD)
        vstt(z(13), z(11), 2.0, z(12), MUL, MUL, zk, zk)
        vtt(z(14), z(11), z(11), MUL, zk, zk)
        vts(z(14), z(14), -2.0, MUL, zk, zk, 1.0, ADD)
        vtt(z(15), z(5), z(14), MUL, zk, zk)
        vtt(z(16), z(5), z(13), MUL, zk, zk)
        vtt(z(17), z(0), z(0), MUL, zk, zk)
        vtt(z(18), z(1), z(1), MUL, zk, zk)
        vtt(z(17), z(17), z(18), ADD, zk, zk)
        Vop(lambda e: e.reciprocal(out=z(17), in_=z(17)), zk, zk)
        vts(z(18), z(15), -1.0, ADD, zk, zk)
        vtt(z(19), z(18), z(0), MUL, zk, zk)
        vtt(z(7), z(16), z(1), MUL, zk, zk)
        vtt(z(19), z(19), z(7), ADD, zk, zk)
        vtt(z(19), z(19), z(17), MUL, zk, zk)
        vtt(z(7), z(16), z(0), MUL, zk, zk)
        vtt(z(8), z(18), z(1), MUL, zk, zk)
        vtt(z(7), z(7), z(8), SUB, zk, zk)
        vtt(z(18), z(7), z(17), MUL, zk, zk)
        for ri, src in ((0, c_b_re), (1, c_b_im)):
            v = src[o_].rearrange("(i a) p c -> (a p) i c", a=2)
            for i0 in range(0, 16, 4):
                fw.dma('sp', BRI[:, ri, i0:i0 + 4, :], v[:, i0:i0 + 4, :], writes=[f'BRI{ri}{i0}'], allow_slow_non_contiguous=True)
        brk = [f'BRI{ri}{i0}' for ri in range(2) for i0 in range(0, 16, 4)]
        crb = z(19).unsqueeze(2).to_broadcast([128, 16, 16])
        cib = z(18).unsqueeze(2).to_broadcast([128, 16, 16])
        Vop(lambda e: e.memset(XB[:], 0.0), [], ['XB'])
        for ri in range(2):
            if ri == 0:
                vtt(T1[:], BRI[:, 0], crb, MUL, brk + zk, ['T1'])
                vtt(T2[:], BRI[:, 1], cib, MUL, brk + zk, ['T2'])
                vtt(T1[:], T1[:], T2[:], SUB, ['T1', 'T2'], ['T1'])
            else:
                vtt(T1[:], BRI[:, 1], crb, MUL, brk + zk + ['XB'], ['T1'])
                vtt(T2[:], BRI[:, 0], cib, MUL, brk + zk, ['T2'])
                vtt(T1[:], T1[:], T2[:], ADD, ['T1', 'T2'], ['T1'])
            vcopy(XB[0:64, ri, :, 0:16], T1[0:64], ['T1'], ['XB'])
            vcopy(XB[64:128, ri, :, 16:32], T1[64:128], ['T1'], ['XB'])
        for ri in range(2):
            for i0 in range(0, 16, 4):
                bi = 4 + (i0 // 4) % 2
                for i in range(i0, i0 + 4):
                    tpose(banks[bi][0:32, (i % 4) * 128:(i % 4 + 1) * 128], XB[:, ri, i, :], ident[:], ['XB'], [f'bank{bi}'])
                acopy(BBT[0:32, ri, i0:i0 + 4, :], banks[bi][0:32, 0:512].rearrange("p (i q) -> p i q", i=4), [f'bank{bi}'], ['BBT'])
        def cons_u(mt, msz, p, pkey):
            acopy(u32[0:32, mt, :], p, [pkey], ['u32'])
        linear(std(hF, 8), 'hF', od_w_in[o_][:, 0:512], 512, cons_u, mtile=32)
        for ri, X, xk in ((0, XR, 'XR'), (1, XI, 'XI')):
            for i0 in range(0, 16, 4):
                bi = 4 + (i0 // 4) % 2
                for i in range(i0, i0 + 4):
                    Pop(lambda e, ri=ri, i=i, bi=bi: e.matmul(banks[bi][:, (i % 4) * 128:(i % 4 + 1) * 128], lhsT=BBT[0:32, ri, i, :],
                                                              rhs=u32[0:32, i, :], start=True, stop=True), ['BBT', 'u32'], [f'bank{bi}'])
                acopy(X[:, i0:i0 + 4, :], banks[bi][:, 0:512].rearrange("p (i q) -> p i q", i=4), [f'bank{bi}'], [xk])
        vcopy(ARb[:], z(15).unsqueeze(2).to_broadcast([128, 16, 4]), zk, ['ARb'])
        vcopy(AIb[:], z(16).unsqueeze(2).to_broadcast([128, 16, 4]), zk, ['AIb'])
        for t in range(32):
            if t == 0:
                pr = X0T[:, 0:64].rearrange("p (b i) -> p i b", b=4)
                pi = X0T[:, 64:128].rearrange("p (b i) -> p i b", b=4)
                pk = ['X0T']
            else:
                pr = XR[:, :, t - 1:128:32]
                pi = XI[:, :, t - 1:128:32]
                pk = ['XR', 'XI']
            vtt(TT[:, 0], ARb[:], pr, MUL, ['ARb'] + pk, ['TT0'])
            vtt(TT[:, 1], AIb[:], pi, MUL, ['AIb'] + pk, ['TT1'])
            vtt(TT[:, 2], ARb[:], pi, MUL, ['ARb'] + pk, ['TT2'])
            vtt(TT[:, 3], AIb[:], pr, MUL, ['AIb'] + pk, ['TT3'])
            vtt(TT[:, 0], TT[:, 0], TT[:, 1], SUB, ['TT0', 'TT1'], ['TT0'])
            vtt(TT[:, 2], TT[:, 2], TT[:, 3], ADD, ['TT2', 'TT3'], ['TT2'])
            vtt(XR[:, :, t:128:32], XR[:, :, t:128:32], TT[:, 0], ADD, ['XR', 'TT0'], ['XR'])
            vtt(XI[:, :, t:128:32], XI[:, :, t:128:32], TT[:, 2], ADD, ['XI', 'TT2'], ['XI'])
        for ri, src in ((0, c_c_re), (1, c_c_im)):
            for ct in range(4):
                Vop(lambda e: e.memset(ZC[:], 0.0), [], ['ZC'])
                for gl in range(8):
                    a = gl % 2
                    fw.dma('sp', ZC[gl * 16:(gl + 1) * 16, 64 * a:64 * a + 64], src[o_, 8 * ct + gl], reads=['ZC'], writes=[f'ZCd{gl}'])
                tpose(banks[4][:, 0:128], ZC[:], ident[:], [f'ZCd{gl}' for gl in range(8)] + ['ZC'], ['bank4'])
                if ri == 0:
                    acopy(CZ[:, 0, ct, :], banks[4][:, 0:128], ['bank4'], ['CZ'])
                else:
                    Aop(lambda e, ct=ct: e.mul(out=CZ[:, 1, ct, :], in_=banks[4][:, 0:128], mul=-1.0), ['bank4'], ['CZ'])
        for i0 in range(0, 16, 4):
            bi = 4 + (i0 // 4) % 2
            for i in range(i0, i0 + 4):
                ct, j = i // 4, i % 4
                Pop(lambda e, i=i, ct=ct, j=j, bi=bi: e.matmul(banks[bi][0:32, j * 128:(j + 1) * 128], lhsT=CZ[:, 0, ct, 32 * j:32 * j + 32],
                                                               rhs=XR[:, i, :], start=True, stop=False), ['CZ', 'XR'], [f'bank{bi}'])
                Pop(lambda e, i=i, ct=ct, j=j, bi=bi: e.matmul(banks[bi][0:32, j * 128:(j + 1) * 128], lhsT=CZ[:, 1, ct, 32 * j:32 * j + 32],
                                                               rhs=XI[:, i, :], start=False, stop=True), ['CZ', 'XI'], [f'bank{bi}'])
            for i in range(i0, i0 + 4):
                j = i % 4
                vstt(y32[0:32, i, :], u32[0:32, i, :], d32[0:32, i:i + 1], banks[bi][0:32, j * 128:(j + 1) * 128], MUL, ADD,
                     ['u32', 'd32', f'bank{bi}'], ['y32'])
        vtt(G1[:], y32[:], y32[:], MUL, ['y32'], ['G1'])
        vts(G1[:], G1[:], 0.044715, MUL, ['G1'], ['G1'], 1.0, ADD)
        vtt(G1[:], G1[:], y32[:], MUL, ['G1', 'y32'], ['G1'])
        act(G1[:], G1[:], AF.Sigmoid, ['G1'], ['G1'], scale=2.0 * math.sqrt(2.0 / math.pi))
        vtt(y32[:], y32[:], G1[:], MUL, ['y32', 'G1'], ['y32'])
        vcopy(yg16[:], y32[:], ['y32'], ['yg16'])

        def cons_glu(mt, msz, p, pkey):
            act(tmp32[0:32, :], p, AF.Sigmoid, [pkey], ['tmp32'])
            vtt(oc32[0:32, mt, :], y32[0:32, mt, :], tmp32[0:32, :], MUL, ['y32', 'tmp32'], ['oc32'])
        linear([(i * 32, 32, 0, yg16[0:32, i, :]) for i in range(16)], 'yg16', c_w_glu[o_], 512, cons_glu, mtile=32)
        vcopy(XF[:, 0:64].rearrange("p (b i) -> p i b", b=4), XR[:, :, 31:128:32], ['XR'], ['XF'])
        vcopy(XF[:, 64:128].rearrange("p (b i) -> p i b", b=4), XI[:, :, 31:128:32], ['XI'], ['XF'])
        tpose(banks[5][:, 0:128], XF[:], ident[:], ['XF'], ['bank5'])
        acopy(stgx[:], banks[5][:, 0:128], ['bank5'], ['stgx'])
        fw.dma('sp', ncs[o_], stgx[:], reads=['stgx'], writes=['ncs'])

    def rwkv(o_, sfx):
        A = lambda n, shp, dt=F32: fw.sb(n + sfx, shp, dt)
        STG = A("STG", [48, 1792]); DV = A("DV", [128, 42]); SH0 = A("SH0", [128, 14, 4]); SHN = A("SHN", [128, 14, 4])
        dcF = A("dcF", [128, 14, NTOK]); XM = A("XM", [128, 14, NTOK])
        QN = A("QN", [128, 4, NTOK]); QW = A("QW", [128, 4, NTOK]); QB = A("QB", [128, 4, NTOK]); QK = A("QK", [128, 4, NTOK])
        AA = A("AA", [128, 4, NTOK]); GG = A("GG", [128, 4, NTOK]); W1 = A("W1", [128, 4, NTOK]); W2 = A("W2", [128, 4, NTOK])
        YF = A("YF", [128, 4, NTOK])
        TL = A("TL", [128, NTOK], BF16); A16 = A("A16", [128, NTOK], BF16); SG = A("SG", [128, NTOK], BF16)
        S = A("S", [128, 16, 64]); RQ = A("RQ", [128, 16, 64]); TMP = A("TMP", [128, 16, 64]); SA = A("SA", [128, 16])
        dv = lambda r: DV[:, r:r + 1]
        linear(std(hF, 8), 'hF', od_w_in[o_][:, 512:2304], 1792, to_tile(dcF, 'dcF'))
        fw.dma('sp', STG[0:42, 0:128], dvecs[o_], writes=['STG'])
        tpose(banks[4][:, 0:42], STG[0:42, 0:128], ident[0:42, 0:42], ['STG'], ['bank4'])
        acopy(DV[:], banks[4][:, 0:42], ['bank4'], ['DV'])
        fw.dma('sp', STG[0:4, :], sdsh[o_], writes=['STG'])
        for k in range(14):
            tpose(banks[5][:, k * 4:k * 4 + 4], STG[0:4, k * 128:(k + 1) * 128], ident[0:4, 0:4], ['STG'], ['bank5'])
        acopy(SH0[:], banks[5][:, 0:56].rearrange("p (k b) -> p k b", b=4), ['bank5'], ['SH0'])
        vcopy(XM[:, :, 1:128], dcF[:, :, 0:127], ['dcF'], ['XM'])
        vcopy(XM[:, :, 0:128:32], SH0[:], ['SH0', 'XM'], ['XM'])
        vtt(XM[:], XM[:], dcF[:], SUB, ['XM', 'dcF'], ['XM'])
        for k in range(14):
            vstt(XM[:, k, :], XM[:, k, :], dv(k), dcF[:, k, :], MUL, ADD, ['XM', 'DV', 'dcF'], ['XM'])
        vcopy(SHN[:], dcF[:, :, 31:128:32], ['dcF'], ['SHN'])
        for k0 in range(0, 14, 4):
            bi = 4 + (k0 // 4) % 2
            n = min(4, 14 - k0)
            for k in range(k0, k0 + n):
                tpose(banks[bi][0:4, (k % 4) * 128:(k % 4 + 1) * 128], SHN[:, k, :], ident[:], ['SHN'], [f'bank{bi}'])
            acopy(STG[0:4, k0 * 128:(k0 + n) * 128], banks[bi][0:4, 0:n * 128], [f'bank{bi}'], ['STG'])
        fw.dma('sp', ndsh[o_], STG[0:4, :], reads=['STG'], writes=['ndsh'])
        act(TL[0:64, :], XM[0:64, 12, :], AF.Tanh, ['XM'], ['TL'])
        acopy(A16[64:128, :], XM[64:128, 12, :], ['XM'], ['A16'])
        act(SG[:], XM[:, 13, :], AF.Sigmoid, ['XM'], ['SG'])

        def cons_w(mt, msz, p, pkey):
            act(QW[:, mt, :], p, AF.Sigmoid, [pkey, 'DV'], ['QW'], bias=dv(14 + mt), scale=1.0)
        linear([(0, 64, 0, TL[0:64, :])], 'TL', d_w_w2[o_], 512, cons_w)
        act(QW[:], QW[:], AF.Exp, ['QW'], ['QW'], scale=-math.exp(-0.5))

        def cons_a(mt, msz, p, pkey):
            act(AA[:, mt, :], p, AF.Sigmoid, [pkey, 'DV'], ['AA'], bias=dv(18 + mt), scale=1.0)
        linear([(0, 64, 64, A16[64:128, :])], 'A16', d_w_a2[o_], 512, cons_a)
        linear([(0, 128, 0, SG[:, :])], 'SG', d_w_g2[o_], 512, to_tile(GG, 'GG'))
        for k in range(4):
            vts(W1[:, k, :], XM[:, 4 + k, :], dv(22 + k), MUL, ['XM', 'DV'], ['W1'])
        vtt(W2[:], W1[:], W1[:], MUL, ['W1'], ['W2'])
        Pop(lambda e: e.matmul(banks[4][:, 0:512], lhsT=blk[:], rhs=W2[:].rearrange("p k n -> p (k n)"), start=True, stop=True),
            ['blk', 'W2'], ['bank4'])
        act(W2[:].rearrange("p k n -> p (k n)"), banks[4][:, 0:512], AF.Sqrt, ['bank4'], ['W2'])
        vts(W2[:], W2[:], 1e-12, ALU.max, ['W2'], ['W2'])
        Vop(lambda e: e.reciprocal(out=W2[:], in_=W2[:]), ['W2'], ['W2'])
        vtt(W1[:], W1[:], W2[:], MUL, ['W1', 'W2'], ['W1'])
        vts(QN[:], W1[:], -1.0, MUL, ['W1'], ['QN'])
        vtt(QB[:], W1[:], AA[:], MUL, ['W1', 'AA'], ['QB'])
        for k in range(4):
            vts(W2[:, k, :], AA[:, k, :], -1.0, ADD, ['AA', 'DV', 'W2'], ['W2'], dv(26 + k), MUL)
        vts(W2[:], W2[:], 1.0, ADD, ['W2'], ['W2'])
        vtt(QK[:], XM[:, 4:8, :], W2[:], MUL, ['XM', 'W2'], ['QK'])
        QR = XM[:, 0:4, :]
        VV = XM[:, 8:12, :]
        for hl in range(2):
            for hh in range(4):
                fw.dma('sp', S[64 * hl:64 * hl + 64, hh * 4:hh * 4 + 4, :], sds[o_, :, 2 * hh + hl].rearrange("b v k -> v b k"),
                       writes=[f'S{hl}{hh}'])
        id2b = id2[:].unsqueeze(1).unsqueeze(1).to_broadcast([128, 4, 4, 64])
        R4 = RQ[:].rearrange("p (a b) k -> p a b k", a=4)
        S4 = S[:].rearrange("p (a b) k -> p a b k", a=4)
        T4 = TMP[:].rearrange("p (a b) k -> p a b k", a=4)
        sk = [f'S{hl}{hh}' for hl in range(2) for hh in range(4)] + ['S']
        slot = [0]

        def bcast(Q, qkeys, t):
            b0 = 2 * (slot[0] % 3)
            slot[0] += 1
            vtt(R4, id2b, Q[:, :, t:128:32].unsqueeze(3).to_broadcast([128, 4, 4, 64]), MUL, ['id2'] + qkeys, ['RQ'])
            for hf in range(2):
                Pop(lambda e, hf=hf, b0=b0: e.matmul(banks[b0 + hf][:, 0:512], lhsT=blk[:],
                                                     rhs=RQ[:, 8 * hf:8 * hf + 8, :].rearrange("p a k -> p (a k)"),
                                                     start=True, stop=True), ['blk', 'RQ'], [f'bank{b0 + hf}'])
            return [(banks[b0 + hf][:, 0:512].rearrange("p (a k) -> p a k", k=64), f'bank{b0 + hf}') for hf in range(2)]

        for t in range(32):
            P = bcast(QN, ['QN'], t)
            for hf in range(2):
                vtt(TMP[:, 8 * hf:8 * hf + 8, :], S[:, 8 * hf:8 * hf + 8, :], P[hf][0], MUL, sk + [P[hf][1]], ['TMP'])
            Vop(lambda e: e.tensor_reduce(out=SA[:], in_=TMP[:], axis=AX.X, op=ADD), ['TMP'], ['SA'])
            P = bcast(QW, ['QW'], t)
            for hf in range(2):
                vtt(S[:, 8 * hf:8 * hf + 8, :], S[:, 8 * hf:8 * hf + 8, :], P[hf][0], MUL, sk + [P[hf][1]], ['S'])
            P = bcast(QB, ['QB'], t)
            for hf in range(2):
                vtt(TMP[:, 8 * hf:8 * hf + 8, :], P[hf][0], SA[:, 8 * hf:8 * hf + 8].unsqueeze(2).to_broadcast([128, 8, 64]), MUL,
                    ['SA', P[hf][1]], ['TMP'])
            vtt(S[:], S[:], TMP[:], ADD, sk + ['TMP'], ['S'])
            P = bcast(QK, ['QK'], t)
            for hf in range(2):
                vtt(T4[:, 2 * hf:2 * hf + 2], P[hf][0].rearrange("p (a b) k -> p a b k", a=2),
                    VV[:, 2 * hf:2 * hf + 2, t:128:32].unsqueeze(3).to_broadcast([128, 2, 4, 64]), MUL, ['XM', P[hf][1]], ['TMP'])
            vtt(S[:], S[:], TMP[:], ADD, sk + ['TMP'], ['S'])
            P = bcast(QR, ['XM'], t)
            for hf in range(2):
                vtt(TMP[:, 8 * hf:8 * hf + 8, :], S[:, 8 * hf:8 * hf + 8, :], P[hf][0], MUL, sk + [P[hf][1]], ['TMP'])
            Vop(lambda e, t=t: e.tensor_reduce(out=YF[:, :, t:128:32], in_=T4, axis=AX.X, op=ADD), ['TMP'], ['YF'])
        for hl in range(2):
            for hh in range(4):
                fw.dma('sp', nds[o_, :, 2 * hh + hl].rearrange("b v k -> v b k"), S[64 * hl:64 * hl + 64, hh * 4:hh * 4 + 4, :],
                       reads=sk, writes=[f'nds{hl}{hh}'])
        flat = lambda tl: tl[:].rearrange("p k n -> p (k n)")
        Pop(lambda e: e.matmul(banks[4][:, 0:512], lhsT=blk[:], rhs=flat(YF), start=True, stop=True), ['blk', 'YF'], ['bank4'])
        vstt(flat(W1), banks[4][:, 0:512], -1.0 / 64, flat(YF), MUL, ADD, ['bank4', 'YF'], ['W1'])
        vtt(W2[:], W1[:], W1[:], MUL, ['W1'], ['W2'])
        Pop(lambda e: e.matmul(banks[5][:, 0:512], lhsT=blk[:], rhs=flat(W2), start=True, stop=True), ['blk', 'W2'], ['bank5'])
        act(flat(W2), banks[5][:, 0:512], AF.Sqrt, ['bank5', 'gneps'], ['W2'], scale=1.0 / 64, bias=gneps[:])
        Vop(lambda e: e.reciprocal(out=W2[:], in_=W2[:]), ['W2'], ['W2'])
        vtt(W1[:], W1[:], W2[:], MUL, ['W1', 'W2'], ['W1'])
        for k in range(4):
            vts(W1[:, k, :], W1[:, k, :], dv(34 + k), MUL, ['W1', 'DV'], ['W1'], dv(38 + k), ADD)
        vtt(W2[:], QR, QK[:], MUL, ['XM', 'QK'], ['W2'])
        for k in range(4):
            vts(W2[:, k, :], W2[:, k, :], dv(30 + k), MUL, ['W2', 'DV'], ['W2'])
        Pop(lambda e: e.matmul(banks[4][:, 0:512], lhsT=blk[:], rhs=flat(W2), start=True, stop=True), ['blk', 'W2'], ['bank4'])
        vcopy(W2[:], VV, ['XM', 'bank4'], ['W2'])
        vtt(flat(W2), banks[4][:, 0:512], flat(W2), MUL, ['bank4', 'W2'], ['W2'])
        vtt(W1[:], W1[:], W2[:], ADD, ['W1', 'W2'], ['W1'])
        vtt(mixF[:, 4:8, :], W1[:], GG[:], MUL, ['W1', 'GG'], ['mixF'])

    def s5_prompt(o_):
        A = lambda n, shp, dt=F32: fw.sb(n + f"_s5p{o_}", shp, dt)
        stg3 = A("stg3", [48, 128]); stgd = A("stgd", [16, 32]); d32 = A("d32", [32, 16])
        Z = A("Z", [128, 20, 16]); ZI = A("ZI", [128, 16], mybir.dt.int32)
        BRI = A("BRI", [128, 2, 16, 16]); T1 = A("T1", [128, 16, 16]); T2 = A("T2", [128, 16, 16])
        XB = A("XB", [128, 2, 16, 32]); BBT = A("BBT", [32, 2, 16, 128]); ZC = A("ZC", [128, 128]); CZ = A("CZ", [128, 2, 4, 128])
        CT = A("CT", [128, 16, 128]); ST = A("ST", [128, 16, 128]); PW = A("PW", [128, 2, 16]); PT2 = A("PT2", [128, 3, 16])
        u32 = A("u32", [32, 16, 128]); BU = A("BU", [128, 2, 512]); ZZ = A("ZZ", [128, 2, 16, 128]); XX = A("XX", [128, 2, 16, 128])
        E1 = A("E1", [128, 1024]); E2 = A("E2", [128, 1024]); zprev = A("zprev", [128, 2, 16]); xprev = A("xprev", [128, 2, 16])
        y32 = A("y32", [32, 16, 128]); G1 = A("G1", [32, 16, 128]); XF = A("XF", [128, 32]); stgo = A("stgo", [32, 128])
        z = lambda i: Z[:, i, :]
        zk = ['Z']
        fw.dma('sp', stg3[:], s5p[o_], writes=['stg3'])
        tpose(banks[4][:, 0:48], stg3[0:48, :], ident[0:48, 0:48], ['stg3'], ['bank4'])
        acopy(Z[:, 0:3, :], banks[4][:, 0:48].rearrange("p (a i) -> p a i", a=3), ['bank4'], zk)
        fw.dma('sp', stgd[:], cd16[o_], writes=['stgd'])
        tpose(banks[5][0:32, 0:16], stgd[0:16, :], ident[0:16, 0:16], ['stgd'], ['bank5'])
        acopy(d32[:], banks[5][0:32, 0:16], ['bank5'], ['d32'])
        act(z(3), z(2), AF.Exp, zk, zk)
        vtt(z(4), z(0), z(3), MUL, zk, zk)
        act(z(5), z(4), AF.Exp, zk, zk)
        vtt(z(6), z(1), z(3), MUL, zk, zk)
        vts(z(7), z(6), 1.0 / (2 * math.pi), MUL, zk, zk, 0.5, ADD)
        vcopy(ZI[:], z(7), zk, ['ZI'])
        vcopy(z(7), ZI[:], ['ZI'], zk)
        vstt(z(8), z(7), -2 * math.pi, z(6), MUL, ADD, zk, zk)
        act(z(9), z(8), AF.Sin, zk, zk, scale=0.25)
        act(z(10), z(8), AF.Sin, zk + ['halfpi'], zk, scale=0.25, bias=halfpi[:])
        vstt(z(11), z(9), 2.0, z(10), MUL, MUL, zk, zk)
        vtt(z(12), z(9), z(9), MUL, zk, zk)
        vts(z(12), z(12), -2.0, MUL, zk, zk, 1.0, ADD)
        vstt(z(13), z(11), 2.0, z(12), MUL, MUL, zk, zk)
        vtt(z(14), z(11), z(11), MUL, zk, zk)
        vts(z(14), z(14), -2.0, MUL, zk, zk, 1.0, ADD)
        vtt(z(15), z(5), z(14), MUL, zk, zk)
        vtt(z(16), z(5), z(13), MUL, zk, zk)
        vtt(z(17), z(0), z(0), MUL, zk, zk)
        vtt(z(18), z(1), z(1), MUL, zk, zk)
        vtt(z(17), z(17), z(18), ADD, zk, zk)
        Vop(lambda e: e.reciprocal(out=z(17), in_=z(17)), zk, zk)
        vts(z(18), z(15), -1.0, ADD, zk, zk)
        vtt(z(19), z(18), z(0), MUL, zk, zk)
        vtt(z(7), z(16), z(1), MUL, zk, zk)
        vtt(z(19), z(19), z(7), ADD, zk, zk)
        vtt(z(19), z(19), z(17), MUL, zk, zk)
        vtt(z(7), z(16), z(0), MUL, zk, zk)
        vtt(z(8), z(18), z(1), MUL, zk, zk)
        vtt(z(7), z(7), z(8), SUB, zk, zk)
        vtt(z(18), z(7), z(17), MUL, zk, zk)
        for ri, src in ((0, c_b_re), (1, c_b_im)):
            v = src[o_].rearrange("(i a) p c -> (a p) i c", a=2)
            for i0 in range(0, 16, 4):
                fw.dma('sp', BRI[:, ri, i0:i0 + 4, :], v[:, i0:i0 + 4, :], writes=[f'BRI{ri}{i0}'], allow_slow_non_contiguous=True)
        brk = [f'BRI{ri}{i0}' for ri in range(2) for i0 in range(0, 16, 4)]
        crb = z(19).unsqueeze(2).to_broadcast([128, 16, 16])
        cib = z(18).unsqueeze(2).to_broadcast([128, 16, 16])
        Vop(lambda e: e.memset(XB[:], 0.0), [], ['XB'])
        for ri in range(2):
            if ri == 0:
                vtt(T1[:], BRI[:, 0], crb, MUL, brk + zk, ['T1'])
                vtt(T2[:], BRI[:, 1], cib, MUL, brk + zk, ['T2'])
                vtt(T1[:], T1[:], T2[:], SUB, ['T1', 'T2'], ['T1'])
            else:
                vtt(T1[:], BRI[:, 1], crb, MUL, brk + zk + ['XB'], ['T1'])
                vtt(T2[:], BRI[:, 0], cib, MUL, brk + zk, ['T2'])
                vtt(T1[:], T1[:], T2[:], ADD, ['T1', 'T2'], ['T1'])
            vcopy(XB[0:64, ri, :, 0:16], T1[0:64], ['T1'], ['XB'])
            vcopy(XB[64:128, ri, :, 16:32], T1[64:128], ['T1'], ['XB'])
        for ri in range(2):
            for i0 in range(0, 16, 4):
                bi = 4 + (i0 // 4) % 2
                for i in range(i0, i0 + 4):
                    tpose(banks[bi][0:32, (i % 4) * 128:(i % 4 + 1) * 128], XB[:, ri, i, :], ident[:], ['XB'], [f'bank{bi}'])
                acopy(BBT[0:32, ri, i0:i0 + 4, :], banks[bi][0:32, 0:512].rearrange("p (i q) -> p i q", i=4), [f'bank{bi}'], ['BBT'])
        for ri, src in ((0, c_c_re), (1, c_c_im)):
            for ct in range(4):
                Vop(lambda e: e.memset(ZC[:], 0.0), [], ['ZC'])
                for gl in range(8):
                    a = gl % 2
                    fw.dma('sp', ZC[gl * 16:(gl + 1) * 16, 64 * a:64 * a + 64], src[o_, 8 * ct + gl], reads=['ZC'], writes=[f'ZCd{gl}'])
                tpose(banks[4][:, 0:128], ZC[:], ident[:], [f'ZCd{gl}' for gl in range(8)] + ['ZC'], ['bank4'])
                if ri == 0:
                    acopy(CZ[:, 0, ct, :], banks[4][:, 0:128], ['bank4'], ['CZ'])
                else:
                    Aop(lambda e, ct=ct: e.mul(out=CZ[:, 1, ct, :], in_=banks[4][:, 0:128], mul=-1.0), ['bank4'], ['CZ'])

        Vop(lambda e: e.memset(CT[:, :, 0:1], 1.0), [], ['CT'])
        Vop(lambda e: e.memset(ST[:, :, 0:1], 0.0), [], ['ST'])
        vcopy(PW[:, 0, :], z(14), zk, ['PW'])
        vcopy(PW[:, 1, :], z(13), zk, ['PW'])
        for L in range(7):
            n = 1 << L
            pc = PW[:, 0, :].unsqueeze(2).to_broadcast([128, 16, n])
            ps_ = PW[:, 1, :].unsqueeze(2).to_broadcast([128, 16, n])
            vtt(E1[:, 0:16 * n].rearrange("p (i t) -> p i t", t=n), CT[:, :, 0:n], pc, MUL, ['CT', 'PW'], ['E1'])
            vtt(E2[:, 0:16 * n].rearrange("p (i t) -> p i t", t=n), ST[:, :, 0:n], ps_, MUL, ['ST', 'PW'], ['E2'])
            vtt(CT[:, :, n:2 * n], E1[:, 0:16 * n].rearrange("p (i t) -> p i t", t=n), E2[:, 0:16 * n].rearrange("p (i t) -> p i t", t=n), SUB,
                ['E1', 'E2'], ['CT'])
            vtt(E1[:, 0:16 * n].rearrange("p (i t) -> p i t", t=n), CT[:, :, 0:n], ps_, MUL, ['CT', 'PW'], ['E1'])
            vtt(E2[:, 0:16 * n].rearrange("p (i t) -> p i t", t=n), ST[:, :, 0:n], pc, MUL, ['ST', 'PW'], ['E2'])
            vtt(ST[:, :, n:2 * n], E1[:, 0:16 * n].rearrange("p (i t) -> p i t", t=n), E2[:, 0:16 * n].rearrange("p (i t) -> p i t", t=n), ADD,
                ['E1', 'E2'], ['ST'])
            if L < 6:
                vtt(PT2[:, 0, :], PW[:, 0, :], PW[:, 0, :], MUL, ['PW'], ['PT2'])
                vtt(PT2[:, 1, :], PW[:, 1, :], PW[:, 1, :], MUL, ['PW'], ['PT2'])
                vtt(PT2[:, 2, :], PW[:, 0, :], PW[:, 1, :], MUL, ['PW'], ['PT2'])
                vtt(PW[:, 0, :], PT2[:, 0, :], PT2[:, 1, :], SUB, ['PT2'], ['PW'])
                vts(PW[:, 1, :], PT2[:, 2, :], 2.0, MUL, ['PT2'], ['PW'])
        Vop(lambda e: e.memset(xprev[:], 0.0), [], ['xprev'])
        RHOb = A("RHOb", [128, 16, 128])
        vcopy(RHOb[:], z(5).unsqueeze(2).to_broadcast([128, 16, 128]), zk, ['RHOb'])
        ur = z(14)
        ui = z(13)
        for j in range(NTILE):
            ts_ = slice(128 * j, 128 * j + 128)
            for c in range(4):
                fw.dma('sp', u32[:, 4 * c:4 * c + 4, :], projD[c, :, ts_].rearrange("(q r) n -> r q n", r=32), writes=[f'u32_{c}'])
            uk = [f'u32_{c}' for c in range(4)]
            vtt(PT2[:, 0, :], ur, xprev[:, 0, :], MUL, zk + ['xprev'], ['PT2'])
            vtt(PT2[:, 1, :], ui, xprev[:, 1, :], MUL, zk + ['xprev'], ['PT2'])
            vtt(zprev[:, 0, :], PT2[:, 0, :], PT2[:, 1, :], SUB, ['PT2'], ['zprev'])
            vtt(PT2[:, 0, :], ur, xprev[:, 1, :], MUL, zk + ['xprev'], ['PT2'])
            vtt(PT2[:, 1, :], ui, xprev[:, 0, :], MUL, zk + ['xprev'], ['PT2'])
            vtt(zprev[:, 1, :], PT2[:, 0, :], PT2[:, 1, :], ADD, ['PT2'], ['zprev'])
            for qd in range(4):
                i0 = 4 * qd
                cq = CT[:, i0:i0 + 4, :].rearrange("p i t -> p (i t)")
                sq_ = ST[:, i0:i0 + 4, :].rearrange("p i t -> p (i t)")
                for ri in range(2):
                    for ii in range(4):
                        Pop(lambda e, ri=ri, ii=ii, i0=i0: e.matmul(banks[4 + ri][:, ii * 128:(ii + 1) * 128], lhsT=BBT[0:32, ri, i0 + ii, :],
                                                                    rhs=u32[0:32, i0 + ii, :], start=True, stop=True), ['BBT'] + uk, [f'bank{4 + ri}'])
                vtt(E1[:, 0:512], cq, banks[4][:, 0:512], MUL, ['CT', 'bank4'], ['E1'])
                vtt(E2[:, 0:512], sq_, banks[5][:, 0:512], MUL, ['ST', 'bank5'], ['E2'])
                vtt(BU[:, 0, :], E1[:, 0:512], E2[:, 0:512], ADD, ['E1', 'E2'], ['BU'])
                vtt(E1[:, 0:512], cq, banks[5][:, 0:512], MUL, ['CT', 'bank5'], ['E1'])
                vtt(E2[:, 0:512], sq_, banks[4][:, 0:512], MUL, ['ST', 'bank4'], ['E2'])
                vtt(BU[:, 1, :], E1[:, 0:512], E2[:, 0:512], SUB, ['E1', 'E2'], ['BU'])
                for ii in range(4):
                    i = i0 + ii
                    for ri in range(2):
                        Vop(lambda e, ri=ri, ii=ii, i=i: e.tensor_tensor_scan(
                            out=ZZ[:, ri, i, :], data0=RHOb[:, i, :], data1=BU[:, ri, ii * 128:(ii + 1) * 128],
                            initial=zprev[:, ri, i:i + 1], op0=MUL, op1=ADD), ['BU', 'zprev', 'RHOb'], ['ZZ'])
                zr = ZZ[:, 0, i0:i0 + 4, :].rearrange("p i t -> p (i t)")
                zi_ = ZZ[:, 1, i0:i0 + 4, :].rearrange("p i t -> p (i t)")
                xr = XX[:, 0, i0:i0 + 4, :].rearrange("p i t -> p (i t)")
                xi_ = XX[:, 1, i0:i0 + 4, :].rearrange("p i t -> p (i t)")
                vtt(E1[:, 0:512], cq, zr, MUL, ['CT', 'ZZ'], ['E1'])
                vtt(E2[:, 0:512], sq_, zi_, MUL, ['ST', 'ZZ'], ['E2'])
                vtt(xr, E1[:, 0:512], E2[:, 0:512], SUB, ['E1', 'E2'], ['XX'])
                vtt(E1[:, 0:512], cq, zi_, MUL, ['CT', 'ZZ'], ['E1'])
                vtt(E2[:, 0:512], sq_, zr, MUL, ['ST', 'ZZ'], ['E2'])
                vtt(xi_, E1[:, 0:512], E2[:, 0:512], ADD, ['E1', 'E2'], ['XX'])
                for ii in range(4):
                    i = i0 + ii
                    ct, jq = i // 4, i % 4
                    Pop(lambda e, i=i, ct=ct, jq=jq, ii=ii: e.matmul(banks[6][0:32, ii * 128:(ii + 1) * 128], lhsT=CZ[:, 0, ct, 32 * jq:32 * jq + 32],
                                                                     rhs=XX[:, 0, i, :], start=True, stop=False), ['CZ', 'XX'], ['bank6'])
                    Pop(lambda e, i=i, ct=ct, jq=jq, ii=ii: e.matmul(banks[6][0:32, ii * 128:(ii + 1) * 128], lhsT=CZ[:, 1, ct, 32 * jq:32 * jq + 32],
                                                                     rhs=XX[:, 1, i, :], start=False, stop=True), ['CZ', 'XX'], ['bank6'])
                for ii in range(4):
                    i = i0 + ii
                    vstt(y32[0:32, i, :], u32[0:32, i, :], d32[0:32, i:i + 1], banks[6][0:32, ii * 128:(ii + 1) * 128], MUL, ADD,
                         uk + ['d32', 'bank6'], ['y32'])
            last = 15 if j == NTILE - 1 else 127
            vcopy(xprev[:, 0, :], XX[:, 0, :, last], ['XX'], ['xprev'])
            vcopy(xprev[:, 1, :], XX[:, 1, :, last], ['XX'], ['xprev'])
            vtt(G1[:], y32[:], y32[:], MUL, ['y32'], ['G1'])
            vts(G1[:], G1[:], 0.044715, MUL, ['G1'], ['G1'], 1.0, ADD)
            vtt(G1[:], G1[:], y32[:], MUL, ['G1', 'y32'], ['G1'])
            act(G1[:], G1[:], AF.Sigmoid, ['G1'], ['G1'], scale=2.0 * math.sqrt(2.0 / math.pi))
            vtt(y32[:], y32[:], G1[:], MUL, ['y32', 'G1'], ['y32'])
            for c in range(4):
                fw.dma('sp', mixD[c, :, ts_].rearrange("(q r) n -> r q n", r=32), y32[:, 4 * c:4 * c + 4, :], reads=['y32'], writes=['mixD'])
        vcopy(XF[:, 0:16], xprev[:, 0, :], ['xprev'], ['XF'])
        vcopy(XF[:, 16:32], xprev[:, 1, :], ['xprev'], ['XF'])
        tpose(banks[5][0:32, 0:128], XF[:], ident[:], ['XF'], ['bank5'])
        acopy(stgo[:], banks[5][0:32, 0:128], ['bank5'], ['stgo'])
        fw.dma('sp', pcs[o_], stgo[:], reads=['stgo'], writes=['pcs'])

    def rwkv_prompt(o_):
        CH = 256
        A = lambda n, shp, dt=F32: fw.sb(n + f"_rp{o_}", shp, dt)
        STG = A("STG", [48, 128]); DV = A("DV", [128, 42]); carry = A("carry", [128, 14, 1])
        dcF = A("dcF", [128, 14, CH]); XM = A("XM", [128, 14, CH]); Q5 = A("Q5", [128, 5, 4, CH])
        AA = A("AA", [128, 4, CH]); GG = A("GG", [128, 4, CH]); W1 = A("W1", [128, 4, CH]); W2 = A("W2", [128, 4, CH]); YF = A("YF", [128, 4, CH])
        BRs = A("BRs", [128, 4, CH]); KRs = A("KRs", [128, 4, CH])
        TL = A("TL", [128, CH], BF16); A16 = A("A16", [128, CH], BF16); SG = A("SG", [128, CH], BF16)
        S = A("S", [128, 4, 64]); RQ = [A(f"RQ{i}", [128, 5, 4, 64]) for i in range(2)]
        TMP2 = A("TMP2", [128, 2, 4, 64]); BIG = A("BIG", [128, CH, 3, 4]); sh14 = A("sh14", [128, 14]); sho = A("sho", [16, 128])
        dv = lambda r: DV[:, r:r + 1]
        state['nt'] = CH
        fw.dma('sp', STG[0:42, 0:128], dvecs[o_], writes=['STG'])
        tpose(banks[4][:, 0:42], STG[0:42, 0:128], ident[0:42, 0:42], ['STG'], ['bank4'])
        acopy(DV[:], banks[4][:, 0:42], ['bank4'], ['DV'])
        Vop(lambda e: e.memset(carry[:], 0.0), [], ['carry'])
        Vop(lambda e: e.memset(S[:], 0.0), [], ['S'])
        id2b5 = id2[:].unsqueeze(1).unsqueeze(1).to_broadcast([128, 5, 4, 64])
        flat = lambda tl: tl[:].rearrange("p k n -> p (k n)")

        def blksum(dst, src, scale=None, other=None):
            for hf in range(2):
                cs = slice(hf * 512, (hf + 1) * 512)
                Pop(lambda e, hf=hf, cs=cs: e.matmul(banks[6 + hf][:, 0:512], lhsT=blk[:], rhs=flat(src)[:, cs], start=True, stop=True),
                    ['blk', src_key[id(src)]], [f'bank{6 + hf}'])
                acopy(flat(dst)[:, cs], banks[6 + hf][:, 0:512], [f'bank{6 + hf}'], [src_key[id(dst)]])
        src_key = {id(W1): 'W1', id(W2): 'W2', id(YF): 'YF', id(BRs): 'BRs', id(KRs): 'KRs'}
        nchunk = (NVALID + CH - 1) // CH
        for c in range(nchunk):
            t0 = c * CH
            nst = min(CH, NVALID - t0)
            fw.dma('sp', dcF[:], projD[4:18, :, t0:t0 + CH].rearrange("k p n -> p k n"), writes=['dcF'])
            vcopy(XM[:, :, 1:CH], dcF[:, :, 0:CH - 1], ['dcF'], ['XM'])
            vcopy(XM[:, :, 0:1], carry[:], ['carry', 'XM'], ['XM'])
            vcopy(carry[:], dcF[:, :, CH - 1:CH], ['dcF', 'XM'], ['carry'])
            vtt(XM[:], XM[:], dcF[:], SUB, ['XM', 'dcF'], ['XM'])
            for k in range(14):
                vstt(XM[:, k, :], XM[:, k, :], dv(k), dcF[:, k, :], MUL, ADD, ['XM', 'DV', 'dcF'], ['XM'])
            act(TL[0:64, :], XM[0:64, 12, :], AF.Tanh, ['XM'], ['TL'])
            acopy(A16[64:128, :], XM[64:128, 12, :], ['XM'], ['A16'])
            act(SG[:], XM[:, 13, :], AF.Sigmoid, ['XM'], ['SG'])

            def cons_w(mt, msz, p, pkey):
                act(Q5[:, 4, mt, :], p, AF.Sigmoid, [pkey, 'DV'], ['Q5w'], bias=dv(14 + mt), scale=1.0)
            linear([(0, 64, 0, TL[0:64, :])], 'TL', d_w_w2[o_], 512, cons_w)
            act(Q5[:, 4], Q5[:, 4], AF.Exp, ['Q5w'], ['Q5w'], scale=-math.exp(-0.5))

            def cons_a(mt, msz, p, pkey):
                act(AA[:, mt, :], p, AF.Sigmoid, [pkey, 'DV'], ['AA'], bias=dv(18 + mt), scale=1.0)
            linear([(0, 64, 64, A16[64:128, :])], 'A16', d_w_a2[o_], 512, cons_a)
            linear([(0, 128, 0, SG[:, :])], 'SG', d_w_g2[o_], 512, to_tile(GG, 'GG'))
            for k in range(4):
                vts(W1[:, k, :], XM[:, 4 + k, :], dv(22 + k), MUL, ['XM', 'DV'], ['W1'])
            vtt(W2[:], W1[:], W1[:], MUL, ['W1'], ['W2'])
            blksum(YF, W2)
            act(flat(YF), flat(YF), AF.Sqrt, ['YF'], ['YF'])
            vts(YF[:], YF[:], 1e-12, ALU.max, ['YF'], ['YF'])
            Vop(lambda e: e.reciprocal(out=YF[:], in_=YF[:]), ['YF'], ['YF'])
            vtt(W1[:], W1[:], YF[:], MUL, ['W1', 'YF'], ['W1'])
            vts(Q5[:, 0], W1[:], -1.0, MUL, ['W1'], ['Q5a'])
            vtt(Q5[:, 2], W1[:], AA[:], MUL, ['W1', 'AA'], ['Q5b'])
            for k in range(4):
                vts(W2[:, k, :], AA[:, k, :], -1.0, ADD, ['AA', 'DV', 'W2'], ['W2'], dv(26 + k), MUL)
            vts(W2[:], W2[:], 1.0, ADD, ['W2'], ['W2'])
            vtt(Q5[:, 3], XM[:, 4:8, :], W2[:], MUL, ['XM', 'W2'], ['Q5b'])
            vtt(Q5[:, 1], Q5[:, 4], XM[:, 0:4, :], MUL, ['Q5w', 'XM'], ['Q5a'])
            vtt(W1[:], Q5[:, 2], XM[:, 0:4, :], MUL, ['Q5b', 'XM'], ['W1'])
            blksum(BRs, W1)
            vtt(W2[:], Q5[:, 3], XM[:, 0:4, :], MUL, ['Q5b', 'XM'], ['W2'])
            blksum(KRs, W2)
            vcopy(BIG[:, :, 1, :], XM[:, 8:12, :].rearrange("p h t -> p t h"), ['XM'], ['BIG'])
            qk = ['Q5a', 'Q5b', 'Q5w']
            for t in range(nst):
                pp = t % 2
                b0 = 3 * pp
                fw.op('pool', lambda e, pp=pp, t=t: e.tensor_tensor(out=RQ[pp][:], in0=id2b5, in1=Q5[:, :, :, t:t + 1].to_broadcast([128, 5, 4, 64]),
                                                                    op=MUL), reads=['id2'] + qk, writes=[f'RQ{pp}'])
                Pop(lambda e, pp=pp, b0=b0: e.matmul(banks[b0][:, 0:512], lhsT=blk[:], rhs=RQ[pp][:, 0:2].rearrange("p a h k -> p (a h k)"),
                                                     start=True, stop=True), ['blk', f'RQ{pp}'], [f'bank{b0}'])
                Pop(lambda e, pp=pp, b0=b0: e.matmul(banks[b0 + 1][:, 0:512], lhsT=blk[:], rhs=RQ[pp][:, 2:4].rearrange("p a h k -> p (a h k)"),
                                                     start=True, stop=True), ['blk', f'RQ{pp}'], [f'bank{b0 + 1}'])
                Pop(lambda e, pp=pp, b0=b0: e.matmul(banks[b0 + 2][:, 0:256], lhsT=blk[:], rhs=RQ[pp][:, 4].rearrange("p h k -> p (h k)"),
                                                     start=True, stop=True), ['blk', f'RQ{pp}'], [f'bank{b0 + 2}'])
                vtt(TMP2[:], S[:].unsqueeze(1).to_broadcast([128, 2, 4, 64]), banks[b0][:, 0:512].rearrange("p (a h k) -> p a h k", a=2, h=4), MUL,
                    ['S', f'bank{b0}'], ['TMP2'])
                Vop(lambda e, t=t: e.tensor_reduce(out=BIG[:, t, 0:3:2, :], in_=TMP2[:], axis=AX.X, op=ADD), ['TMP2'], ['BIG'])
                vtt(TMP2[:], banks[b0 + 1][:, 0:512].rearrange("p (a h k) -> p a h k", a=2, h=4),
                    BIG[:, t, 0:2, :].unsqueeze(3).to_broadcast([128, 2, 4, 64]), MUL, ['BIG', f'bank{b0 + 1}'], ['TMP2'])
                vtt(S[:], S[:], banks[b0 + 2][:, 0:256].rearrange("p (h k) -> p h k", h=4), MUL, ['S', f'bank{b0 + 2}'], ['S'])
                vtt(S[:], S[:], TMP2[:, 0], ADD, ['S', 'TMP2'], ['S'])
                vtt(S[:], S[:], TMP2[:, 1], ADD, ['S', 'TMP2'], ['S'])
            vtt(W1[:], BIG[:, :, 0, :].rearrange("p t h -> p h t"), BRs[:], MUL, ['BIG', 'BRs'], ['W1'])
            vtt(W2[:], BIG[:, :, 1, :].rearrange("p t h -> p h t"), KRs[:], MUL, ['BIG', 'KRs'], ['W2'])
            vtt(YF[:], BIG[:, :, 2, :].rearrange("p t h -> p h t"), W1[:], ADD, ['BIG', 'W1'], ['YF'])
            vtt(YF[:], YF[:], W2[:], ADD, ['YF', 'W2'], ['YF'])
            blksum(W1, YF)
            vstt(flat(W1), flat(W1), -1.0 / 64, flat(YF), MUL, ADD, ['W1', 'YF'], ['W1'])
            vtt(W2[:], W1[:], W1[:], MUL, ['W1'], ['W2'])
            blksum(YF, W2)
            act(flat(YF), flat(YF), AF.Sqrt, ['YF', 'gneps'], ['YF'], scale=1.0 / 64, bias=gneps[:])
            Vop(lambda e: e.reciprocal(out=YF[:], in_=YF[:]), ['YF'], ['YF'])
            vtt(W1[:], W1[:], YF[:], MUL, ['W1', 'YF'], ['W1'])
            for k in range(4):
                vts(W1[:, k, :], W1[:, k, :], dv(34 + k), MUL, ['W1', 'DV'], ['W1'], dv(38 + k), ADD)
            vtt(W2[:], XM[:, 0:4, :], Q5[:, 3], MUL, ['XM', 'Q5b'], ['W2'])
            for k in range(4):
                vts(W2[:, k, :], W2[:, k, :], dv(30 + k), MUL, ['W2', 'DV'], ['W2'])
            blksum(YF, W2)
            vtt(YF[:], YF[:], XM[:, 8:12, :], MUL, ['YF', 'XM'], ['YF'])
            vtt(W1[:], W1[:], YF[:], ADD, ['W1', 'YF'], ['W1'])
            vtt(W1[:], W1[:], GG[:], MUL, ['W1', 'GG'], ['W1'])
            fw.dma('sp', mixD[4:8, :, t0:t0 + CH].rearrange("k p n -> p k n"), W1[:], reads=['W1'], writes=['mixD'])
            if c == nchunk - 1:
                vcopy(sh14[:], dcF[:, :, nst - 1], ['dcF'], ['sh14'])
        tpose(banks[4][0:14, 0:128], sh14[:], ident[:], ['sh14'], ['bank4'])
        acopy(sho[0:14, :], banks[4][0:14, 0:128], ['bank4'], ['sho'])
        fw.dma('sp', pdsh[o_].rearrange("(k p) -> k p", p=128), sho[0:14, :], reads=['sho'], writes=['pdsh'])
        for hl in range(2):
            fw.dma('sp', pds[o_].rearrange("(hh hl) v k -> hl v hh k", hl=2)[hl], S[64 * hl:64 * hl + 64, :, :], reads=['S'], writes=[f'pds{hl}'])
        state['nt'] = NTOK

    EV_NAMES = ("kc kT vst Vaug PT ktok vtok bvtok botok lq_bc lamc qb16 LI LF BR MT nig subs_bc xcF qkF cacc cwb stg "
                "tmp12 Caug m0bc igfg_bc bnorm_bc mrow WK CL").split()
    for layer in range(0 if SKIP_SAMPLE else DEPTH):
        rmsnorm(layer, hF, 'hF')
        if layer % 2 == 0:
            e_ = layer // 2
            with fw.scope():
                tl = alloc_even(f"_{layer}")
                (kc, kT, vst, Vaug, PT, ktok, vtok, bvtok, botok, lq_bc, lamc, qb16, LI, LF, BR, MT, nig, subs_bc, xcF, qkF,
                 cacc, cwb, stg, tmp12, Caug, m0bc, igfg_bc, bnorm_bc, mrow, WK, CL) = [tl[n] for n in EV_NAMES]
                fw.op('dve', lambda e: e.memset(Vaug[:], 1.0), writes=['Vaug'])
                fw.op('dve', lambda e: e.memset(bvtok[:], 1.0), writes=['bvtok'])
                linear(std(hF, 8), 'hF', ev_w_in[e_], EV_COLS, to_tile(projF, 'projF'), tok_groups=even_tok_groups(e_))
                attention(e_, layer)
                mlstm(e_)
                if layer == 0:
                    Vop(lambda e: e.tensor_copy(out=sq[:], in_=mixF[:]), ['mixF'], ['sq'])
                    fw.dma('sp', dbg, sq[:].rearrange("p k n -> p (k n)"), reads=['sq'], writes=['dbg'])
                linear(std(mixF, 8), 'mixF', ev_w_out[e_], D, add_resid)
        else:
            o_ = layer // 2
            with fw.scope():
                oc32 = fw.sb(f"oc32_{layer}", [32, 16, NTOK], BF16)
                with fw.scope():
                    s5(o_, f"_{layer}", oc32)
                with fw.scope():
                    rwkv(o_, f"_{layer}")
                chunks = [(i * 32, 32, 0, oc32[0:32, i, :]) for i in range(16)] + \
                         [(512 + j * 128, 128, 0, mixF[:, 4 + j, :]) for j in range(4)]
                linear(chunks, ('oc32', 'mixF'), od_w_out[o_], D, add_resid)
        rmsnorm(4 + layer, hF, 'hF')
        linear(std(hF, 8), 'hF', ffn_w1[layer], FFN, to_tile(projF, 'projF'))
        fw.op('act', lambda e: e.activation(out=projF[:, 0:22, :], in_=projF[:, 0:22, :], func=AF.Silu), reads=['projF'], writes=['projF'])

        def gate(mt, msz, p, pkey):
            fw.op('dve', lambda e: e.tensor_tensor(out=gF[0:msz, mt, :], in0=projF[0:msz, mt, :], in1=p, op=ALU.mult),
                  reads=[pkey, 'projF'], writes=['gF'])
        linear(std(hF, 8), 'hF', ffn_w3[layer], FFN, gate)
        linear(std(gF, 22), 'gF', ffn_w2[layer], D, add_resid)

    rmsnorm(8, sq, 'sq2')
    for k in range(8):
        b = banks[k % 2]
        fw.op('pe', lambda e, k=k, b=b: e.transpose(b[:, 0:128], sq[:, k, :], ident[:]),
              reads=['sq2', 'ident'], writes=[f'bank{k % 2}'])
        fw.op('act', lambda e, k=k, b=b: e.copy(out=xtok[:, k * 128:(k + 1) * 128], in_=b[:, 0:128]),
              reads=[f'bank{k % 2}'], writes=['xtok'])
    fw.dma('sp', ys, xtok[:], reads=['xtok'], writes=['ys'])
    sc_sample.__exit__(None, None, None)

    sc_prompt = fw.scope()
    sc_prompt.__enter__()
    negc = fw.sb("negc", [128, 128]); bkp = fw.sb("bkp", [128, 384]); mskp = fw.sb("mskp", [128, 384])
    ADall = fw.sb("ADall", [128, 4, 384]); adtmp = fw.sb("adtmp", [128, 384])
    ones16 = fw.sb("ones16", [128, 128], BF16); zeros16 = fw.sb("zeros16", [128, 128], BF16)
    stage = [fw.sb(f"stage{i}", [128, 512]) for i in range(3)]
    fw.dma('sp', negc[:], negc_in, writes=['negc'])
    fw.dma('sp', bkp[:], bkp_in, writes=['bkp'])
    fw.dma('sp', mskp[:], mskp_in, writes=['mskp'])
    Vop(lambda e: e.memset(ones16[:], 1.0), [], ['ones16'])
    Vop(lambda e: e.memset(zeros16[:], 0.0), [], ['zeros16'])
    for h in range(4):
        vcopy(ADall[:, h, :], mskp[:], ['mskp'], ['ADall'])
        for bkt in range(32):
            vts(adtmp[:], bkp[:], float(bkt), ALU.is_equal, ['bkp', 'rb_bc'], ['adtmp'], rb_bc[:, bkt * 4 + h:bkt * 4 + h + 1], MUL)
            vtt(ADall[:, h, :], ADall[:, h, :], adtmp[:], ADD, ['adtmp', 'ADall'], ['ADall'])
    if TP > NTILE * 128:
        Vop(lambda e: e.memset(stage[0][:], 0.0), [], ['stage0'])
        for k in range(8):
            fw.dma('sp', mixD[k, :, NTILE * 128:TP], stage[0][:, 0:TP - NTILE * 128], reads=['stage0'], writes=['mixD'])
    stg_i = [0]

    def to_dram(t0):
        def c(mt, msz, p, pkey):
            si = stg_i[0] % 3
            stg_i[0] += 1
            acopy(stage[si][0:msz, :], p, [pkey], [f'stage{si}'])
            fw.dma('sp', projD[mt, 0:msz, t0:t0 + 512], stage[si][0:msz, :], reads=[f'stage{si}'], writes=['projD'])
        return c

    def dense_phase(layer):
        nonlocal xF, sq, rstd, hF, mixF, gF, xtok
        state['nt'] = 512
        with fw.scope():
            _, xF, sq, rstd, hF, _, mixF, gF = alloc_dense(512, f"_p{layer}", False)
            xtok = fw.sb(f"xtokp{layer}", [128, D])
            for gi in range(NGRP):
                t0 = gi * 512
                src = xp0 if layer == 0 else xD
                fw.dma('sp', xF[:], src[:, :, t0:t0 + 512].rearrange("k p n -> p k n"), writes=['xF'])
                if layer > 0:
                    pl = layer - 1
                    fw.dma('sp', sq[:], mixD[:, :, t0:t0 + 512].rearrange("k p n -> p k n"), writes=['sq'])
                    vcopy(mixF[:], sq[:], ['sq'], ['mixF'])
                    if pl % 2 == 1:
                        def cons_glu(mt, msz, p, pkey):
                            act(stage[2][:, :], p, AF.Sigmoid, [pkey], ['stage2'])
                            vtt(gF[:, mt, :], sq[:, mt, :], stage[2][:, :], MUL, ['sq', 'stage2'], ['gF'])
                        linear(std(mixF, 4), 'mixF', c_w_glu[pl // 2], 512, cons_glu)
                        vcopy(mixF[:, 0:4, :], gF[:, 0:4, :], ['gF'], ['mixF'])
                        linear(std(mixF, 8), 'mixF', od_w_out[pl // 2], D, add_resid)
                    else:
                        linear(std(mixF, 8), 'mixF', ev_w_out[pl // 2], D, add_resid)
                    rmsnorm(4 + pl, hF, 'hF')

                    def cons_silu(mt, msz, p, pkey):
                        act(gF[0:msz, mt, :], p, AF.Silu, [pkey], ['gF'])

                    def cons_gate(mt, msz, p, pkey):
                        vtt(gF[0:msz, mt, :], gF[0:msz, mt, :], p, MUL, [pkey, 'gF'], ['gF'])
                    linear(std(hF, 8), 'hF', ffn_w1[pl], FFN, cons_silu)
                    linear(std(hF, 8), 'hF', ffn_w3[pl], FFN, cons_gate)
                    linear(std(gF, 22), 'gF', ffn_w2[pl], D, add_resid)
                if layer < DEPTH:
                    rmsnorm(layer, hF, 'hF')
                    W = ev_w_in[layer // 2] if layer % 2 == 0 else od_w_in[layer // 2]
                    linear(std(hF, 8), 'hF', W, EV_COLS if layer % 2 == 0 else OD_COLS, to_dram(t0))
                    fw.dma('sp', xD[:, :, t0:t0 + 512].rearrange("k p n -> p k n"), xF[:], reads=['xF'], writes=['xD'])
                else:
                    rmsnorm(8, sq, 'sq2')
                    for jj in range(4):
                        j = gi * 4 + jj
                        if j >= NTILE:
                            break
                        for k in range(8):
                            b = banks[4 + k % 2]
                            tpose(b[:, 0:128], sq[:, k, jj * 128:(jj + 1) * 128], ident[:], ['sq2'], [f'bank{4 + k % 2}'])
                            acopy(xtok[:, k * 128:(k + 1) * 128], b[:, 0:128], [f'bank{4 + k % 2}'], ['xtok'])
                        r0 = 16 if j == 0 else 0
                        r1 = 16 if j == NTILE - 1 else 128
                        fw.dma('sp', yp[128 * j - 16 + r0:128 * j - 16 + r1, :], xtok[r0:r1, :], reads=['xtok'], writes=['yp'])
        state['nt'] = NTOK

    def mlstm_prompt(e_):
        A = lambda n, shp, dt=F32: fw.sb(n + f"_mp{e_}", shp, dt)
        g8 = A("g8", [8, 128]); xcP = A("xcP", [128, 8, 131]); cacc = A("cacc", [128, 8, 128]); qkF = A("qkF", [128, 8, 128])
        cwb = A("cwb", [128, 8, 5]); stg = A("stg", [16, 1024]); vfm = A("vfm", [128, 8, 128])
        bvtok = A("bvtok", [128, 4, 129]); botok = A("botok", [128, 512])
        LI = A("LI", [128, 4, 128]); LF = A("LF", [128, 4, 128]); BR = A("BR", [128, 4, 128]); MT = A("MT", [128, 4, 128])
        Cst = A("Cst", [128, 4, 129]); mcar = A("mcar", [128, 4]); igfg_bc = A("igfg", [128, 8]); nig = A("nig", [128, 8])
        bnorm_bc = A("bnorm", [128, 512]); mrow = A("mrow", [1, 4]); hst = A("hst", [128, 128]); tmp3 = A("tmp3", [128, 8, 3])
        WK = [A(f"wk{i}", [128, 129]) for i in range(14)]
        CL = A("cl", [128, 24])

        def dg(src_ap, src_key, col_ap):
            Vop(lambda e: e.scalar_tensor_tensor(out=WK[11][:, 0:128], in0=src_ap, scalar=1.0, in1=ident[:], op0=MUL, op1=MUL,
                                                 accum_out=col_ap), [src_key, 'ident'], ['wk11', 'cl'])
        fw.dma('sp', stg[0:5, :], convwb[e_], writes=['stg'])
        for k in range(8):
            tpose(banks[4][:, k * 8:k * 8 + 5], stg[0:5, k * 128:(k + 1) * 128], ident[0:5, 0:5], ['stg'], ['bank4'])
        for k in range(8):
            acopy(cwb[:, k, :], banks[4][:, k * 8:k * 8 + 5], ['bank4'], ['cwb'])
        fw.dma('sp', igfg_bc[:], igfg[e_].partition_broadcast(128), writes=['igfg_bc'])
        fw.dma('sp', bnorm_bc[:], b_norm[e_].partition_broadcast(128), writes=['bnorm_bc'])
        vts(nig[:], igfg_bc[:], -1.0, MUL, ['igfg_bc'], ['nig'])
        Vop(lambda e: e.memset(Cst[:], 0.0), [], ['Cst'])
        Vop(lambda e: e.memset(mcar[:], 0.0), [], ['mcar'])
        Vop(lambda e: e.memset(xcP[:], 0.0), [], ['xcP'])
        Vop(lambda e: e.memset(bvtok[:], 1.0), [], ['bvtok'])
        for j in range(NTILE):
            t0 = 128 * j
            ts_ = slice(t0, t0 + 128)
            fw.dma('sp', xcP[:, :, 3:131], projD[12:20, :, ts_].rearrange("k p n -> p k n"), writes=['xcP'])
            fw.dma('sp', g8[:], projD[28, 0:8, ts_], writes=['g8'])
            fw.dma('sp', vfm[:], projD[20:28, :, ts_].rearrange("k p n -> p k n"), writes=['vfm'])
            for h in range(4):
                tpose(banks[6][:, h * 128:(h + 1) * 128], vfm[:, h, :], ident[:], ['vfm'], ['bank6'])
            acopy(bvtok[:, :, 0:128], banks[6][:, 0:512].rearrange("p (h v) -> p h v", v=128), ['bank6'], ['bvtok'])
            for h in range(4):
                tpose(banks[6][:, h * 128:(h + 1) * 128], vfm[:, 4 + h, :], ident[:], ['vfm'], ['bank6'])
            acopy(botok[:], banks[6][:, 0:512], ['bank6'], ['botok'])
            for k in range(8):
                vts(cacc[:, k, :], xcP[:, k, 0:128], cwb[:, k, 0:1], MUL, ['xcP', 'cwb'], ['cacc'], cwb[:, k, 4:5], ADD)
                for jj in range(1, 4):
                    vstt(cacc[:, k, :], xcP[:, k, jj:jj + 128], cwb[:, k, jj:jj + 1], cacc[:, k, :], MUL, ADD, ['xcP', 'cwb', 'cacc'], ['cacc'])
            act(qkF[:], cacc[:], AF.Silu, ['cacc'], ['qkF'])
            Aop(lambda e: e.mul(out=qkF[:, 0:4, :], in_=qkF[:, 0:4, :], mul=128 ** -0.5), ['qkF'], ['qkF'])
            if j == NTILE - 1:
                vcopy(tmp3[:], xcP[:, :, 16:19], ['xcP'], ['tmp3'])
            vcopy(xcP[:, :, 0:3], xcP[:, :, 128:131], ['xcP', 'cacc'], ['xcP'])
            for q in range(8):
                bi = 4 + q // 4
                Pop(lambda e, q=q, bi=bi: e.matmul(banks[bi][:, (q % 4) * 128:(q % 4 + 1) * 128], lhsT=selr[0:8, q * 128:(q + 1) * 128],
                                                   rhs=g8[0:8, :], start=True, stop=True), ['selr', 'g8'], [f'bank{bi}'])
            for h in range(4):
                vts(LI[:, h, :], banks[4][:, h * 128:(h + 1) * 128], igfg_bc[:, h:h + 1], ADD, ['bank4', 'igfg_bc'], ['LI'])
                act(LF[:, h, :], banks[5][:, h * 128:(h + 1) * 128], AF.Exp, ['bank5', 'nig'], ['LF'], bias=nig[:, 4 + h:5 + h], scale=-1.0)
            act(LF[:], LF[:], AF.Ln, ['LF', 'onec'], ['LF'], bias=onec[:], scale=1.0)
            vts(LF[:], LF[:], -1.0, MUL, ['LF'], ['LF'])
            if j == NTILE - 1:
                Vop(lambda e: e.memset(LI[:, :, 16:128], -1e30), ['LI'], ['LI'])
                Vop(lambda e: e.memset(LF[:, :, 16:128], 0.0), ['LF'], ['LF'])
            for h in range(4):
                Vop(lambda e, h=h: e.tensor_tensor_scan(out=BR[:, h, :], data0=ones[:], data1=LF[:, h, :], initial=0.0,
                                                        op0=MUL, op1=ADD), ['ones', 'LF'], ['BR'])
                Vop(lambda e, h=h: e.tensor_tensor_scan(out=MT[:, h, :], data0=LF[:, h, :], data1=LI[:, h, :], initial=mcar[:, h:h + 1],
                                                        op0=ADD, op1=ALU.max), ['LF', 'LI', 'mcar'], ['MT'])
            for h in range(4):
                hs = slice(h * 128, (h + 1) * 128)
                Pop(lambda e, h=h: e.matmul(banks[4][:, 0:128], lhsT=qkF[:, 4 + h, :], rhs=qkF[:, h, :], start=True, stop=True), ['qkF'], ['bank4'])
                vtt(WK[5][:, 0:128], BR[:, h, :], MT[:, h, :], SUB, ['BR', 'MT'], ['wk5'])
                vtt(WK[5][:, 0:128], WK[5][:, 0:128], negc[:], ADD, ['wk5', 'negc'], ['wk5'])
                vtt(WK[6][:, 0:128], LI[:, h, :], BR[:, h, :], SUB, ['LI', 'BR'], ['wk6'])
                dg(WK[6][:, 0:128], 'wk6', CL[:, 8:9])
                act(WK[5][:, 0:128], WK[5][:, 0:128], AF.Exp, ['wk5', 'cl'], ['wk5'], bias=CL[:, 8:9], scale=1.0)
                vtt(WK[6][:, 0:128], banks[4][:, 0:128], WK[5][:, 0:128], MUL, ['bank4', 'wk5', 'wk6'], ['wk6'])
                Pop(lambda e, h=h: e.matmul(banks[5][:, 0:129], lhsT=WK[6][:, 0:128], rhs=bvtok[:, h, :], start=True, stop=True), ['wk6', 'bvtok'], ['bank5'])
                Pop(lambda e, h=h: e.matmul(banks[6][:, 0:129], lhsT=qkF[:, h, :], rhs=Cst[:, h, :], start=True, stop=True), ['qkF', 'Cst'], ['bank6'])
                acopy(WK[7][:, 0:129], banks[6][:, 0:129], ['bank6'], ['wk7'])
                vstt(WK[8][:, 0:128], BR[:, h, :], mcar[:, h:h + 1], MT[:, h, :], ADD, SUB, ['BR', 'MT', 'mcar'], ['wk8'])
                act(WK[8][:, 0:128], WK[8][:, 0:128], AF.Exp, ['wk8'], ['wk8'])
                dg(WK[8][:, 0:128], 'wk8', CL[:, 9:10])
                vstt(WK[9][:, 0:129], WK[7][:, 0:129], CL[:, 9:10], banks[5][:, 0:129], MUL, ADD, ['wk7', 'cl', 'bank5'], ['wk9'])
                vts(CL[:, 20:21], WK[9][:, 128:129], -1.0, MUL, ['wk9'], ['cl'])
                vtt(CL[:, 10:11], CL[:, 20:21], WK[9][:, 128:129], ALU.max, ['wk9', 'cl'], ['cl'])
                dg(MT[:, h, :], 'MT', CL[:, 11:12])
                act(CL[:, 12:13], CL[:, 11:12], AF.Exp, ['cl'], ['cl'], scale=-1.0)
                vtt(CL[:, 13:14], CL[:, 10:11], CL[:, 12:13], ALU.max, ['cl'], ['cl'])
                Vop(lambda e: e.reciprocal(out=CL[:, 14:15], in_=CL[:, 13:14]), ['cl'], ['cl'])
                vts(WK[10][:, 0:128], WK[9][:, 0:128], CL[:, 14:15], MUL, ['wk9', 'cl'], ['wk10'])
                Vop(lambda e: e.scalar_tensor_tensor(out=WK[11][:, 0:128], in0=WK[10][:, 0:128], scalar=1.0, in1=WK[10][:, 0:128],
                                                     op0=MUL, op1=MUL, accum_out=CL[:, 15:16]), ['wk10'], ['wk11', 'cl'])
                act(CL[:, 16:17], CL[:, 15:16], AF.Sqrt, ['cl', 'epsc'], ['cl'], scale=1.0 / 128, bias=epsc[:])
                Vop(lambda e: e.reciprocal(out=CL[:, 17:18], in_=CL[:, 16:17]), ['cl'], ['cl'])
                act(WK[11][:, 0:128], botok[:, hs], AF.Sigmoid, ['botok', 'wk11'], ['wk11'])
                vstt(WK[10][:, 0:128], WK[10][:, 0:128], CL[:, 17:18], bnorm_bc[:, hs], MUL, MUL, ['wk10', 'cl', 'bnorm_bc'], ['wk10'])
                vtt(WK[10][:, 0:128], WK[10][:, 0:128], WK[11][:, 0:128], MUL, ['wk10', 'wk11'], ['wk10'])
                tpose(banks[5][:, 256:384], WK[10][:, 0:128], ident[:], ['wk10'], ['bank5'])
                acopy(hst[:], banks[5][:, 256:384], ['bank5'], ['hst'])
                fw.dma('sp', mixD[4 + h, :, ts_], hst[:], reads=['hst'], writes=['mixD'])
                tpose(banks[4][:, 128:256], qkF[:, 4 + h, :], ident[:], ['qkF'], ['bank4'])
                acopy(WK[12][:, 0:128], banks[4][:, 128:256], ['bank4'], ['wk12'])
                vtt(WK[13][:, 0:128], LI[:, h, :], BR[:, h, :], SUB, ['LI', 'BR'], ['wk13'])
                vts(WK[13][:, 0:128], WK[13][:, 0:128], BR[:, h, 127:128], ADD, ['wk13', 'BR', 'MT'], ['wk13'], MT[:, h, 127:128], SUB)
                act(WK[13][:, 0:128], WK[13][:, 0:128], AF.Exp, ['wk13'], ['wk13'])
                dg(WK[13][:, 0:128], 'wk13', CL[:, 18:19])
                vts(WK[12][:, 0:128], WK[12][:, 0:128], CL[:, 18:19], MUL, ['wk12', 'cl'], ['wk12'])
                Pop(lambda e, h=h: e.matmul(banks[5][:, 0:129], lhsT=WK[12][:, 0:128], rhs=bvtok[:, h, :], start=True, stop=True), ['wk12', 'bvtok'], ['bank5'])
                vstt(CL[:, 19:20], BR[:, h, 127:128], mcar[:, h:h + 1], MT[:, h, 127:128], ADD, SUB, ['BR', 'MT', 'mcar'], ['cl'])
                act(CL[:, 19:20], CL[:, 19:20], AF.Exp, ['cl'], ['cl'])
                vstt(Cst[:, h, :], Cst[:, h, :], CL[:, 19:20], banks[5][:, 0:129], MUL, ADD, ['Cst', 'cl', 'bank5'], ['Cst'])
                vcopy(mcar[:, h:h + 1], MT[:, h, 127:128], ['MT', 'mcar'], ['mcar'])
        for h in range(4):
            fw.dma('sp', pbc[e_, h], Cst[:, h, 0:128], reads=['Cst'], writes=[f'pbc{h}'])
            fw.dma('sp', pbn[e_, h].rearrange("(k o) -> k o", o=1), Cst[:, h, 128:129], reads=['Cst'], writes=[f'pbn{h}'])
        acopy(mrow[0:1, :], mcar[0:1, :], ['mcar'], ['mrow'])
        fw.dma('sp', pbm[e_].rearrange("(o k) -> o k", o=1), mrow[0:1, :], reads=['mrow'], writes=['pbm'])
        for k in range(8):
            bi = 4 + k // 4
            tpose(banks[bi][0:3, (k % 4) * 128:(k % 4 + 1) * 128], tmp3[:, k, :], ident[:], ['tmp3'], [f'bank{bi}'])
        acopy(stg[0:3, 0:512], banks[4][0:3, 0:512], ['bank4'], ['stg'])
        acopy(stg[0:3, 512:1024], banks[5][0:3, 0:512], ['bank5'], ['stg'])
        fw.dma('sp', pbconv[e_], stg[0:3, :], reads=['stg'], writes=['pbconv'])

    def attention_prompt(e_, layer):
        lam_init = LAM_INIT[layer]
        A = lambda n, shp, dt=F32: fw.sb(n + f"_ap{e_}", shp, dt)
        kT16 = A("kT16", [128, NTILE * 128], BF16); Vt16 = A("Vt16", [128, NTILE, 128], BF16)
        st32 = A("st32", [128, 512]); q16 = A("q16", [128, 512], BF16); PTp = A("PTp", [128, 2, 512], BF16)
        tk = [A(f"tk{i}", [128, 128]) for i in range(2)]
        O1 = A("O1", [128, 512]); O2 = A("O2", [128, 512]); O3 = A("O3", [128, 512]); ntmp = A("ntmp", [128, 128])
        lq_bc = A("lq_bc", [128, 256]); lamc = A("lamc", [128, 4]); subc = A("subc", [128, 1]); w0 = A("w0", [128, 64])
        fw.dma('sp', lq_bc[:], lqk[e_].rearrange("a d -> (a d)").partition_broadcast(128), writes=['lq_bc'])
        fw.dma('sp', subc[:], a_subln[e_].rearrange("(k o) -> k o", o=1), writes=['subc'])
        Vop(lambda e: e.scalar_tensor_tensor(out=w0[:], in0=lq_bc[:, 0:64], scalar=1.0, in1=lq_bc[:, 64:128], op0=MUL, op1=MUL,
                                             accum_out=lamc[:, 0:1]), ['lq_bc'], ['w0', 'lamc'])
        Vop(lambda e: e.scalar_tensor_tensor(out=w0[:], in0=lq_bc[:, 128:192], scalar=1.0, in1=lq_bc[:, 192:256], op0=MUL, op1=MUL,
                                             accum_out=lamc[:, 1:2]), ['lq_bc', 'w0'], ['w0', 'lamc'])
        act(lamc[:, 0:2], lamc[:, 0:2], AF.Exp, ['lamc'], ['lamc'])
        vtt(lamc[:, 2:3], lamc[:, 0:1], lamc[:, 1:2], SUB, ['lamc'], ['lamc'])
        vts(lamc[:, 3:4], lamc[:, 2:3], -1.0, MUL, ['lamc'], ['lamc'], -lam_init, ADD)
        vts(subc[:], subc[:], 1.0 - lam_init, MUL, ['subc'], ['subc'])
        tki = [0]
        for h in range(ATT_HEADS):
            hs = slice(h * 128, (h + 1) * 128)
            for which, ptile in ((0, 4 + h), (1, 8 + h)):
                if 'kv'[which] not in PREP:
                    continue
                for c in range(NGRP):
                    c0 = c * 512
                    w = min(512, NTILE * 128 - c0)
                    if w <= 0:
                        break
                    fw.dma('sp', st32[:, 0:w], projD[ptile, :, c0:c0 + w], writes=['st32'])
                    if which == 0:
                        vcopy(kT16[:, c0:c0 + w], st32[:, 0:w], ['st32'], ['kT16'])
                    for jj in range(w // 128):
                        j = c * 4 + jj
                        bi = 6 + jj % 2
                        tpose(banks[bi][:, 0:128], st32[:, jj * 128:(jj + 1) * 128], ident[:], ['st32'], [f'bank{bi}'])
                        ti = tki[0] % 2
                        tki[0] += 1
                        acopy(tk[ti][:], banks[bi][:, 0:128], [f'bank{bi}'], [f'tk{ti}'])
                        if which == 1:
                            vcopy(Vt16[:, j, :], tk[ti][:], [f'tk{ti}'], ['Vt16'])
                        nv = 16 if j == NTILE - 1 else 128
                        dst = pak if which == 0 else pav
                        fw.dma('sp', dst[e_, 128 * j:128 * j + nv, hs], tk[ti][0:nv, :], reads=[f'tk{ti}'], writes=[f'pakv{which}'])
            for g in range(NGRP):
                ntl = min(4, NTILE - 4 * g)
                if ntl <= 0 or ATT_MODE < 2:
                    break
                w = ntl * 128
                fw.dma('sp', st32[:, 0:w], projD[h, :, g * 512:g * 512 + w], writes=['st32'])
                vcopy(q16[:, 0:w], st32[:, 0:w], ['st32'], ['q16'])
                for m in range(2):
                    Pop(lambda e, m=m, w=w: e.matmul(banks[m][:, 0:w], lhsT=zeros16[:], rhs=q16[:, 0:w], start=True, stop=False),
                        ['zeros16', 'q16'], [f'bank{m}'])
                    Pop(lambda e, m=m, w=w: e.matmul(banks[2 + m][:, 0:w], lhsT=zeros16[:], rhs=q16[:, 0:w], start=True, stop=False),
                        ['zeros16', 'q16'], [f'bank{2 + m}'])

                def pv(m, kt, nk, cs, ncol):
                    Pop(lambda e: e.matmul(banks[m][:, cs], lhsT=Vt16[0:nk, kt, :], rhs=PTp[0:nk, m, 0:ncol], start=False, stop=False),
                        ['Vt16', 'PTp'], [f'bank{m}'])
                    Pop(lambda e: e.matmul(banks[2 + m][:, cs], lhsT=ones16[0:nk, :], rhs=PTp[0:nk, m, 0:ncol], start=False, stop=False),
                        ['ones16', 'PTp'], [f'bank{2 + m}'])
                for kt in range(0, 4 * g - 1 if ATT_MODE >= 3 else 0):
                    for m in range(2):
                        ms = slice(64 * m, 64 * m + 64)
                        sb_ = 4 + m + 2 * (kt % 2)
                        Pop(lambda e, m=m, ms=ms, kt=kt, sb_=sb_, w=w: e.matmul(banks[sb_][:, 0:w], lhsT=kT16[ms, kt * 128:(kt + 1) * 128],
                                                                                rhs=q16[ms, 0:w], start=True, stop=True), ['kT16', 'q16'], [f'bank{sb_}'])
                        act(PTp[:, m, 0:w], banks[sb_][:, 0:w], AF.Exp, [f'bank{sb_}', 'rb_bc'], ['PTp'], bias=rb_bc[:, 60 + h:61 + h], scale=0.125)
                        pv(m, kt, 128, slice(0, w), w)
                for jj in range(ntl if ATT_MODE >= 4 else 0):
                    i = 4 * g + jj
                    cs = slice(jj * 128, (jj + 1) * 128)
                    for kt in range(max(0, 4 * g - 1), min(i + 2, NTILE)):
                        nk = 16 if kt == i + 1 else 128
                        adi = None if kt <= i - 2 else (1 if kt == i - 1 else (0 if kt == i else 2))
                        for m in range(2):
                            ms = slice(64 * m, 64 * m + 64)
                            sb_ = 4 + m + 2 * (kt % 2)
                            Pop(lambda e, m=m, ms=ms, kt=kt, sb_=sb_, nk=nk, cs=cs: e.matmul(
                                banks[sb_][0:nk, 0:128], lhsT=kT16[ms, kt * 128:kt * 128 + nk], rhs=q16[ms, cs], start=True, stop=True),
                                ['kT16', 'q16'], [f'bank{sb_}'])
                            if adi is None:
                                act(PTp[0:nk, m, 0:128], banks[sb_][0:nk, 0:128], AF.Exp, [f'bank{sb_}', 'rb_bc'], ['PTp'],
                                    bias=rb_bc[0:nk, 60 + h:61 + h], scale=0.125)
                            else:
                                vstt(ntmp[0:nk, :], banks[sb_][0:nk, 0:128], 0.125, ADall[0:nk, h, adi * 128:(adi + 1) * 128], MUL, ADD,
                                     [f'bank{sb_}', 'ADall'], ['ntmp'])
                                act(PTp[0:nk, m, 0:128], ntmp[0:nk, :], AF.Exp, ['ntmp'], ['PTp'])
                            pv(m, kt, nk, cs, 128)
                for m in range(2):
                    Pop(lambda e, m=m, w=w: e.matmul(banks[m][:, 0:w], lhsT=zeros16[:], rhs=q16[:, 0:w], start=False, stop=True),
                        ['zeros16', 'q16'], [f'bank{m}'])
                    Pop(lambda e, m=m, w=w: e.matmul(banks[2 + m][:, 0:w], lhsT=zeros16[:], rhs=q16[:, 0:w], start=False, stop=True),
                        ['zeros16', 'q16'], [f'bank{2 + m}'])
                ws = slice(0, w)
                Vop(lambda e, ws=ws: e.reciprocal(out=O1[:, ws], in_=banks[2][:, ws]), ['bank2'], ['O1'])
                vtt(O2[:, ws], banks[0][:, ws], O1[:, ws], MUL, ['bank0', 'O1'], ['O2'])
                Vop(lambda e, ws=ws: e.reciprocal(out=O1[:, ws], in_=banks[3][:, ws]), ['bank3', 'O1'], ['O1'])
                vts(O1[:, ws], O1[:, ws], lamc[:, 3:4], MUL, ['O1', 'lamc'], ['O1'])
                vtt(O3[:, ws], banks[1][:, ws], O1[:, ws], MUL, ['bank1', 'O1'], ['O3'])
                vtt(O2[:, ws], O2[:, ws], O3[:, ws], ADD, ['O2', 'O3'], ['O2'])
                vtt(O1[:, ws], O2[:, ws], O2[:, ws], MUL, ['O2', 'O1'], ['O1'])
                Pop(lambda e, ws=ws: e.matmul(banks[7][:, ws], lhsT=ones[:], rhs=O1[:, ws], start=True, stop=True), ['ones', 'O1'], ['bank7'])
                act(O3[:, ws], banks[7][:, ws], AF.Sqrt, ['bank7', 'epsc', 'O3'], ['O3'], scale=1.0 / 128, bias=epsc[:])
                Vop(lambda e, ws=ws: e.reciprocal(out=O3[:, ws], in_=O3[:, ws]), ['O3'], ['O3'])
                vstt(O2[:, ws], O2[:, ws], subc[:, 0:1], O3[:, ws], MUL, MUL, ['O2', 'subc', 'O3'], ['O2'])
                fw.dma('sp', mixD[h, :, g * 512:g * 512 + w], O2[:, ws], reads=['O2'], writes=['mixD'])

    for layer in range(PROMPT_LAYERS + 1):
        if layer > DEPTH:
            break
        if layer == PROMPT_LAYERS and PROMPT_LAYERS < DEPTH:
            break
        dense_phase(layer)
        if layer < DEPTH:
            if layer % 2 == 0:
                if 'm' in P_STAGES:
                    with fw.scope():
                        mlstm_prompt(layer // 2)
                if 'a' in P_STAGES:
                    with fw.scope():
                        attention_prompt(layer // 2, layer)
            else:
                if 's' in P_STAGES:
                    with fw.scope():
                        s5_prompt(layer // 2)
                if 'o' in P_STAGES:
                    with fw.scope():
                        rwkv_prompt(layer // 2)
    if DBG:
        dbgmix = dout("dbgmix", [8, 128, TP])
        fw.barrier()
        for k in range(8):
            fw.dma('sp', dbgmix[k], mixD[k], writes=[f'dbgmix{k}'])
    sc_prompt.__exit__(None, None, None)
    fw.barrier()
    fw.close()
    return nc


_REF_SHAPES = None


def _out_shapes():
    N_EVEN, N_ODD, B, S, T = 2, 2, 32, 16384, 32
    p = [(1, S, D), (B, T, D),
         (N_EVEN, 1, 16 + S, 4, 128), (N_EVEN, 1, 16 + S, 4, 128), (N_EVEN, 1, 4, 128, 128), (N_EVEN, 1, 4, 128),
         (N_EVEN, 1, 4), (N_EVEN, 1, 3, 1024), (N_ODD, 1, 32, 64), (N_ODD, 1, 32, 64), (N_ODD, 1, 8, 64, 64),
         (N_ODD, 1, 1, 1792),
         (N_EVEN, B, T, 4, 128), (N_EVEN, B, T, 4, 128), (N_EVEN, B, 4, 128, 128), (N_EVEN, B, 4, 128),
         (N_EVEN, B, 4), (N_EVEN, B, 3, 1024), (N_ODD, B, 32, 64), (N_ODD, B, 32, 64), (N_ODD, B, 8, 64, 64),
         (N_ODD, B, 1, 1792)]
    return p


def _t5_bucket_np(rel):
    rel = np.asarray(rel, np.int64)
    nb, max_exact = 16, 8
    ret = np.where(rel > 0, nb, 0)
    n = np.abs(rel)
    nf = np.maximum(n, 1).astype(np.float32)
    large = max_exact + (np.log(nf / np.float32(max_exact)) / np.float32(math.log(128 / max_exact))
                         * np.float32(nb - max_exact)).astype(np.int32)
    large = np.minimum(large, nb - 1)
    return ret + np.where(n < max_exact, n, large)


def _consts():
    c = {}
    c["ident"] = np.eye(128, dtype=np.float32)
    n_ = np.arange(128)
    kl = np.arange(128)[:, None]
    qi = np.arange(32)[None, :]
    bkm = np.full((128, 96), -1.0, np.float32)
    bkm[:, 0:32] = _t5_bucket_np((1920 + kl) - (2064 + qi))
    bkm[0:16, 32:64] = _t5_bucket_np((2048 + kl[:16]) - (2064 + qi))
    bkm[0:32, 64:96] = _t5_bucket_np(kl[:32] - qi)
    c["bk"] = bkm
    c["negc"] = np.where(n_[None, :] >= n_[:, None], 0.0, -1e30).astype(np.float32)
    kr = n_[:, None]; qr = n_[None, :]
    c["bkp"] = np.concatenate([_t5_bucket_np(kr - qr), _t5_bucket_np(kr - 128 - qr), _t5_bucket_np(kr + 128 - qr)], axis=1).astype(np.float32)
    thr = np.where(qr < 16, 16, np.where(qr < 80, 80, 128))
    m0 = np.where(kr < thr, 0.0, -30000.0)
    m1 = np.where((kr < 16) & (qr >= 80), 0.0, -30000.0)
    c["mskp"] = np.concatenate([m0, np.zeros((128, 128)), m1], axis=1).astype(np.float32)
    c["blk"] = np.kron(np.eye(2, dtype=np.float32), np.ones((64, 64), np.float32))
    c["id2"] = np.concatenate([np.eye(64, dtype=np.float32)] * 2, axis=0)
    selr = np.zeros((8, 8, 128), np.float32)
    for j in range(8):
        selr[j, j, :] = 1.0
    c["selr"] = selr.reshape(8, 1024)
    n = np.arange(128)
    same = (n[:, None] // 32) == (n[None, :] // 32)
    c["negbd"] = np.where(same & (n[None, :] >= n[:, None]), 0.0, -1e30).astype(np.float32)
    c["bmask"] = ((n[:, None] // 32) == np.arange(4)[None, :]).astype(np.float32)
    c["scanmask"] = np.broadcast_to((n % 32 != 0).astype(np.float32)[None, :], (128, 128)).copy()
    return c


def kernel(**inputs):
    f = lambda a: np.ascontiguousarray(np.asarray(a, dtype=np.float32))
    nc = build_program()
    shared = {k: f(inputs[k]) for k in ("norm_mix", "norm_ffn", "ev_w_in", "ev_w_out", "od_w_in", "od_w_out",
                                        "ffn_w1", "ffn_w3", "ffn_w2", "a_subln", "b_norm")}
    shared["norm_final"] = f(inputs["norm_final"]).reshape(1, D)
    shared["relb"] = f(inputs["rel_bias"]).reshape(1, 128)
    shared["lqk"] = f(np.stack([inputs["a_lq1"], inputs["a_lk1"], inputs["a_lq2"], inputs["a_lk2"]], axis=1))
    shared["convwb"] = f(np.concatenate([inputs["b_conv_w"], np.asarray(inputs["b_conv_b"])[:, None, :]], axis=1))
    shared["igfg"] = f(np.concatenate([inputs["b_ig_bias"], inputs["b_fg_bias"]], axis=-1))
    for k in ("c_b_re", "c_b_im", "c_c_re", "c_c_im", "c_w_glu", "d_w_w2", "d_w_a2", "d_w_g2"):
        shared[k] = f(inputs[k])
    ldt = np.repeat(np.asarray(inputs["c_log_dt"], np.float32)[:, :, None], 64, axis=2)
    shared["s5p"] = f(np.stack([inputs["c_lam_re"], inputs["c_lam_im"], ldt], axis=1)).reshape(2, 48, 128)
    shared["cd16"] = f(inputs["c_d"]).reshape(2, 16, 32)
    shared["dvecs"] = f(np.concatenate([np.asarray(inputs[k], np.float32) for k in
                                        ("d_mu", "d_w0", "d_a0", "d_k_k", "d_k_a", "d_r_k", "d_ln_g", "d_ln_b")], axis=1)).reshape(2, 42, 128)
    shared.update(_consts())
    xs = f(inputs["x_sample"])
    xfull = np.zeros((TP, D), np.float32)
    xfull[0:16] = np.asarray(inputs["meta_tokens"], np.float32)
    xfull[16:NVALID] = np.asarray(inputs["x_prompt"], np.float32)[0][:NVALID - 16]
    shared["xp0"] = np.ascontiguousarray(xfull.T).reshape(8, 128, TP)
    in_maps = []
    for c in range(NCORES):
        bs = slice(4 * c, 4 * c + 4)
        m = dict(shared)
        m["xs"] = xs[bs].reshape(NTOK, D)
        m["cak"] = f(inputs["cache_a_k"][:, bs]).reshape(2, 4, 2064, 512)
        m["cav"] = f(inputs["cache_a_v"][:, bs]).reshape(2, 4, 2064, 512)
        m["sbc"] = f(inputs["state_b_c"][:, bs])
        m["sbn"] = f(inputs["state_b_n"][:, bs]).reshape(2, 16, 128)
        m["sbm"] = f(inputs["state_b_m"][:, bs]).reshape(2, 16)
        m["sbconv"] = f(inputs["state_b_conv"][:, bs]).reshape(2, 12, 1024)
        m["sc0"] = f(np.stack([inputs["state_c_re"][:, bs], inputs["state_c_im"][:, bs]], axis=1)).reshape(2, 128, 128)
        m["sds"] = f(inputs["state_d_s"][:, bs])
        m["sdsh"] = f(inputs["state_d_shift"][:, bs]).reshape(2, 4, 1792)
        in_maps.append(m)
    res = run_bass_kernel_spmd(nc, in_maps, core_ids=list(range(NCORES)))
    R = res.results
    outs = [np.zeros(s, np.float32) for s in _out_shapes()]
    cat = lambda name, shp: np.concatenate([r[name].reshape(shp) for r in R], axis=1)
    outs[1] = np.concatenate([r["ys"].reshape(4, 32, D) for r in R], axis=0)
    outs[12] = cat("nak", (2, 4, 32, 4, 128))
    outs[13] = cat("nav", (2, 4, 32, 4, 128))
    outs[14] = cat("nbc", (2, 4, 4, 128, 128))
    outs[15] = cat("nbn", (2, 4, 4, 128))
    outs[16] = cat("nbm", (2, 4, 4))
    outs[17] = cat("nbconv", (2, 4, 3, 1024))
    cs = np.concatenate([r["ncs"].reshape(2, 2, 4, 32, 64) for r in R], axis=2)
    outs[18] = np.ascontiguousarray(cs[:, 0])
    outs[19] = np.ascontiguousarray(cs[:, 1])
    outs[20] = cat("nds", (2, 4, 8, 64, 64))
    outs[21] = cat("ndsh", (2, 4, 1, 1792))
    r0 = R[0]
    outs[0] = r0["yp"].reshape(1, NVALID - 16, D)
    outs[2] = r0["pak"].reshape(2, 1, NVALID, 4, 128)
    outs[3] = r0["pav"].reshape(2, 1, NVALID, 4, 128)
    outs[4] = r0["pbc"].reshape(2, 1, 4, 128, 128)
    outs[5] = r0["pbn"].reshape(2, 1, 4, 128)
    outs[6] = r0["pbm"].reshape(2, 1, 4)
    outs[7] = r0["pbconv"].reshape(2, 1, 3, 1024)
    pc = r0["pcs"].reshape(2, 2, 1, 32, 64)
    outs[8] = np.ascontiguousarray(pc[:, 0])
    outs[9] = np.ascontiguousarray(pc[:, 1])
    outs[10] = r0["pds"].reshape(2, 1, 8, 64, 64)
    outs[11] = r0["pdsh"].reshape(2, 1, 1, 1792)
    global _DBGMIX
    _DBGMIX = r0["dbgmix"] if DBG else None
    global _DBG
    _DBG = [r["dbg"] for r in R]
    return tuple(outs)
```
